# Optimizing a Trainium2 kernel written in Bass

```python
import math
import jax, jax.numpy as jnp
from jax import lax
import numpy as np

D_MODEL = 1024
BATCH = 16
SEQ = 2048
DEPTH = 2

HEAD_DIM = 64
SB_HEADS = 8
DIFF_HEADS = 4
SWA_Q_HEADS = 8
SWA_KV_HEADS = 2
SWA_GROUP = SWA_Q_HEADS // SWA_KV_HEADS
WINDOW = 128
BLOCK = 128
D_FF = 2816
N_BUCKETS = 32
MAX_DISTANCE = 128
N_BIAS_HEADS = DIFF_HEADS + SWA_Q_HEADS
EPS = 1e-6

SB_W = SB_HEADS * HEAD_DIM
DIFF_QK_W = DIFF_HEADS * 2 * HEAD_DIM
DIFF_V_W = DIFF_HEADS * 2 * HEAD_DIM
SWA_Q_W = SWA_Q_HEADS * HEAD_DIM
SWA_KV_W = SWA_KV_HEADS * HEAD_DIM
IN_WIDTHS = (SB_W, SB_W, SB_W, DIFF_QK_W, DIFF_QK_W, DIFF_V_W,
             SWA_Q_W, SWA_KV_W, SWA_KV_W, D_MODEL, D_MODEL, D_MODEL)
IN_W = SB_W * 3 + DIFF_QK_W * 2 + DIFF_V_W + SWA_Q_W + 2 * SWA_KV_W + 3 * D_MODEL

kernel_name = "hybrid_sb_diff_swa_macaron"


def _offsets(widths):
    out, acc = [], 0
    for w in widths[:-1]:
        acc += w
        out.append(acc)
    return out


def rmsnorm(x, g):
    xf = x.astype(jnp.float32)
    y = xf * lax.rsqrt(jnp.mean(xf * xf, axis=-1, keepdims=True) + EPS)
    return (y * g.astype(jnp.float32)).astype(x.dtype)


def swiglu(h, w_in, w_out):
    g, u = jnp.split(h @ w_in, 2, axis=-1)
    return (jax.nn.silu(g) * u) @ w_out


def t5_bucket(dist):
    n = jnp.maximum(dist, 0)
    max_exact = N_BUCKETS // 2
    is_small = n < max_exact
    nf = jnp.maximum(n, 1).astype(jnp.float32)
    large = max_exact + (jnp.log(nf / max_exact) / math.log(MAX_DISTANCE / max_exact)
                         * (N_BUCKETS - max_exact)).astype(jnp.int32)
    large = jnp.minimum(large, N_BUCKETS - 1)
    return jnp.where(is_small, n, large)


def stick_breaking_attention(q, k, v):
    S = q.shape[1]
    scale = HEAD_DIM ** -0.5
    outs = []
    for i in range(S // BLOCK):
        q0 = i * BLOCK
        L = q0 + BLOCK
        z = jnp.einsum('bqhd,bkhd->bhqk', q[:, q0:L], k[:, :L],
                       preferred_element_type=jnp.float32) * scale
        t = q0 + jnp.arange(BLOCK)[:, None]
        s = jnp.arange(L)[None, :]
        mask = s < t
        log_fail = jnp.where(mask, jax.nn.log_sigmoid(-z), 0.0)
        between = lax.cumsum(log_fail, axis=3, reverse=True) - log_fail
        w = jnp.where(mask, jnp.exp(jax.nn.log_sigmoid(z) + between), 0.0)
        outs.append(jnp.einsum('bhqk,bkhd->bqhd', w.astype(v.dtype), v[:, :L]))
    return jnp.concatenate(outs, axis=1)


def differential_attention(q, k, v, bias_table, lam, lam_init, sub_gain):
    B, S = q.shape[0], q.shape[1]
    scale = HEAD_DIM ** -0.5
    outs = []
    for i in range(S // BLOCK):
        q0 = i * BLOCK
        L = q0 + BLOCK
        z = jnp.einsum('bqhcd,bkhcd->bhcqk', q[:, q0:L], k[:, :L],
                       preferred_element_type=jnp.float32) * scale
        dist = (q0 + jnp.arange(BLOCK))[:, None] - jnp.arange(L)[None, :]
        bias = bias_table[t5_bucket(dist)].astype(jnp.float32).transpose(2, 0, 1)
        z = jnp.where(dist >= 0, z + bias[None, :, None], -jnp.inf)
        p = jax.nn.softmax(z, axis=-1)
        w = p[:, :, 0] - lam * p[:, :, 1]
        outs.append(jnp.einsum('bhqk,bkhe->bqhe', w.astype(v.dtype), v[:, :L]))
    o = jnp.concatenate(outs, axis=1)
    o = rmsnorm(o, sub_gain) * (1.0 - lam_init)
    return o.reshape(B, S, DIFF_HEADS * 2 * HEAD_DIM)


def sliding_window_attention(q, k, v, bias_table, sinks):
    B, S = q.shape[0], q.shape[1]
    nb = S // BLOCK
    scale = HEAD_DIM ** -0.5
    qb = q.reshape(B, nb, BLOCK, SWA_KV_HEADS, SWA_GROUP, HEAD_DIM)
    kb = k.reshape(B, nb, BLOCK, SWA_KV_HEADS, HEAD_DIM)
    vb = v.reshape(B, nb, BLOCK, SWA_KV_HEADS, HEAD_DIM)
    pad = ((0, 0), (1, 0), (0, 0), (0, 0), (0, 0))
    kw = jnp.concatenate([jnp.pad(kb, pad)[:, :-1], kb], axis=2)
    vw = jnp.concatenate([jnp.pad(vb, pad)[:, :-1], vb], axis=2)
    z = jnp.einsum('bnqhgd,bnkhd->bnhgqk', qb, kw,
                   preferred_element_type=jnp.float32) * scale
    a = jnp.arange(BLOCK)[:, None]
    c = jnp.arange(2 * BLOCK)[None, :]
    dist = BLOCK + a - c
    band = (dist >= 0) & (dist < WINDOW)
    valid = band[None] & ((jnp.arange(nb)[:, None, None] > 0) | (c >= BLOCK)[None])
    bias = bias_table[t5_bucket(dist)].astype(jnp.float32)
    bias = bias.reshape(BLOCK, 2 * BLOCK, SWA_KV_HEADS, SWA_GROUP).transpose(2, 3, 0, 1)
    z = jnp.where(valid[None, :, None, None], z + bias[None, None], -jnp.inf)
    sink = sinks.astype(jnp.float32).reshape(SWA_KV_HEADS, SWA_GROUP)
    sink_col = jnp.broadcast_to(sink[None, None, :, :, None, None], z.shape[:-1] + (1,))
    p = jax.nn.softmax(jnp.concatenate([z, sink_col], axis=-1), axis=-1)[..., :-1]
    o = jnp.einsum('bnhgqk,bnkhd->bnqhgd', p.astype(v.dtype), vw)
    return o.reshape(B, S, SWA_Q_HEADS * HEAD_DIM)


def hybrid_mixer(h, w_in, q_norm_diff, k_norm_diff, q_norm_swa, k_norm_swa, lam_vecs, lam_init,
                 diff_subln, sinks, rel_bias, w_proj_sb, w_proj_diff, w_proj_swa, w_out):
    B, S, _ = h.shape
    proj = h @ w_in
    (qa, ka, va, qd, kd, vd, qs, ks, vs, ga, gd, gs) = jnp.split(proj, _offsets(IN_WIDTHS), axis=-1)
    hd = (B, S, SB_HEADS, HEAD_DIM)
    o_sb = stick_breaking_attention(qa.reshape(hd), ka.reshape(hd), va.reshape(hd)).reshape(B, S, SB_W)
    qd = rmsnorm(qd.reshape(B, S, DIFF_HEADS, 2, HEAD_DIM), q_norm_diff)
    kd = rmsnorm(kd.reshape(B, S, DIFF_HEADS, 2, HEAD_DIM), k_norm_diff)
    vd = vd.reshape(B, S, DIFF_HEADS, 2 * HEAD_DIM)
    lv = lam_vecs.astype(jnp.float32)
    lam = jnp.exp(jnp.sum(lv[0] * lv[1])) - jnp.exp(jnp.sum(lv[2] * lv[3])) + lam_init
    o_diff = differential_attention(qd, kd, vd, rel_bias[:, :DIFF_HEADS], lam, lam_init, diff_subln)
    qs = rmsnorm(qs.reshape(B, S, SWA_KV_HEADS, SWA_GROUP, HEAD_DIM), q_norm_swa)
    ks = rmsnorm(ks.reshape(B, S, SWA_KV_HEADS, HEAD_DIM), k_norm_swa)
    vs = vs.reshape(B, S, SWA_KV_HEADS, HEAD_DIM)
    o_swa = sliding_window_attention(qs, ks, vs, rel_bias[:, DIFF_HEADS:], sinks)
    merged = (jax.nn.sigmoid(ga) * (o_sb @ w_proj_sb)
              + jax.nn.sigmoid(gd) * (o_diff @ w_proj_diff)
              + jax.nn.sigmoid(gs) * (o_swa @ w_proj_swa))
    return merged @ w_out


def setup_inputs(seed: int = 0) -> dict:
    key = jax.random.key(seed)
    ks = jax.random.split(key, 24)
    f32 = jnp.float32

    def w(k, shape, fan_in):
        return jax.random.normal(k, shape, f32) * (fan_in ** -0.5)

    def gain(k, shape):
        return 1.0 + 0.01 * jax.random.normal(k, shape, f32)

    return {
        "x": jax.random.normal(ks[0], (BATCH, SEQ, D_MODEL), f32),
        "ffn1_norm": gain(ks[1], (DEPTH, D_MODEL)),
        "ffn1_w_in": w(ks[2], (DEPTH, D_MODEL, 2 * D_FF), D_MODEL),
        "ffn1_w_out": w(ks[3], (DEPTH, D_FF, D_MODEL), D_FF),
        "mix_norm": gain(ks[4], (DEPTH, D_MODEL)),
        "w_in": w(ks[5], (DEPTH, D_MODEL, IN_W), D_MODEL),
        "q_norm_diff": gain(ks[6], (DEPTH, HEAD_DIM)),
        "k_norm_diff": gain(ks[7], (DEPTH, HEAD_DIM)),
        "q_norm_swa": gain(ks[8], (DEPTH, HEAD_DIM)),
        "k_norm_swa": gain(ks[9], (DEPTH, HEAD_DIM)),
        "diff_lambda": 0.1 * jax.random.normal(ks[10], (DEPTH, 4, HEAD_DIM), f32),
        "diff_subln": gain(ks[11], (DEPTH, 2 * HEAD_DIM)),
        "swa_sinks": 0.5 * jax.random.normal(ks[12], (DEPTH, SWA_Q_HEADS), f32),
        "rel_bias": 0.5 * jax.random.normal(ks[13], (N_BUCKETS, N_BIAS_HEADS), f32),
        "w_proj_sb": w(ks[14], (DEPTH, SB_W, D_MODEL), SB_W),
        "w_proj_diff": w(ks[15], (DEPTH, DIFF_V_W, D_MODEL), DIFF_V_W),
        "w_proj_swa": w(ks[16], (DEPTH, SWA_Q_W, D_MODEL), SWA_Q_W),
        "w_out": w(ks[17], (DEPTH, D_MODEL, D_MODEL), D_MODEL),
        "ffn2_norm": gain(ks[18], (DEPTH, D_MODEL)),
        "ffn2_w_in": w(ks[19], (DEPTH, D_MODEL, 2 * D_FF), D_MODEL),
        "ffn2_w_out": w(ks[20], (DEPTH, D_FF, D_MODEL), D_FF),
    }


def reference(x, ffn1_norm, ffn1_w_in, ffn1_w_out, mix_norm, w_in, q_norm_diff, k_norm_diff,
              q_norm_swa, k_norm_swa, diff_lambda, diff_subln, swa_sinks, rel_bias,
              w_proj_sb, w_proj_diff, w_proj_swa, w_out, ffn2_norm, ffn2_w_in, ffn2_w_out):
    for l in range(DEPTH):
        lam_init = 0.8 - 0.6 * math.exp(-0.3 * l)
        x = x + 0.5 * swiglu(rmsnorm(x, ffn1_norm[l]), ffn1_w_in[l], ffn1_w_out[l])
        x = x + hybrid_mixer(rmsnorm(x, mix_norm[l]), w_in[l], q_norm_diff[l], k_norm_diff[l],
                             q_norm_swa[l], k_norm_swa[l], diff_lambda[l], lam_init, diff_subln[l],
                             swa_sinks[l], rel_bias, w_proj_sb[l], w_proj_diff[l], w_proj_swa[l],
                             w_out[l])
        x = x + 0.5 * swiglu(rmsnorm(x, ffn2_norm[l]), ffn2_w_in[l], ffn2_w_out[l])
    return x
```

```python
import contextlib
import math
import numpy as np
import concourse.bass as bass
import concourse.mybir as mybir
from concourse.bass_utils import run_bass_kernel_spmd

F32 = mybir.dt.float32
BF16 = mybir.dt.bfloat16
AF = mybir.ActivationFunctionType
ALU = mybir.AluOpType

D_MODEL = 1024
SEQ = 2048
DEPTH = 2
D_FF = 2816
NJ = D_FF // 128
IN_W = 6912
EPS = 1e-6
N_CORES = 8
SEQ_PER_CORE = 2
MASKV = -30000.0
SCALE = 0.125

OFF_QA, OFF_KA, OFF_VA = 0, 512, 1024
OFF_QD, OFF_KD, OFF_VD = 1536, 2048, 2560
OFF_QS, OFF_KS, OFF_VS = 3072, 3584, 3712
OFF_GA, OFF_GD, OFF_GS = 3840, 4864, 5888

PC_NORM = 0
PC_QK = 48
PC_SINK = 56
PC_CFAR = 72
PC_LAM = 76
PC_SUBLN = 588
NP = 844


class Res:
    __slots__ = ("name", "w", "r", "sem", "semval")

    def __init__(self, name):
        self.name = name
        self.w = None
        self.r = {}
        self.sem = None
        self.semval = 0


class Sched:
    ROT = 30000
    ENGS = ("pe", "act", "dve", "pool", "sp")

    def __init__(self, nc, stack):
        self.nc = nc
        self.stack = stack
        self.ops = {e: [] for e in self.ENGS}
        self.cnt = {e: 0 for e in self.ENGS}
        self.epoch = {e: 0 for e in self.ENGS}
        self.waited = {e: {} for e in self.ENGS}
        self.semh = {}
        self.nsem = 0
        self.pending_noinc = {e: False for e in self.ENGS}
        self.n_instr = 0
        self.n_wait = 0

    def _sem(self, key):
        h = self.semh.get(key)
        if h is None:
            h = self.stack.enter_context(self.nc.semaphore("s%d" % self.nsem))
            self.nsem += 1
            self.semh[key] = h
        return h

    def _next_id(self, eng):
        if self.cnt[eng] >= self.ROT and not self.pending_noinc[eng]:
            self.epoch[eng] += 1
            self.cnt[eng] = 0
        return (("e", eng, self.epoch[eng]), self.cnt[eng] + 1)

    def _collect(self, eng, reads, writes):
        deps = {}

        def add(d):
            if d is None:
                return
            k, v = d
            if eng == "pe" and k[0] == "e" and k[1] == "pe":
                return
            if deps.get(k, 0) < v:
                deps[k] = v
        for r in reads:
            add(r.w)
        for w in writes:
            add(w.w)
            for d in w.r.items():
                add(d)
        out = []
        wd = self.waited[eng]
        for k, v in deps.items():
            if wd.get(k, 0) >= v:
                continue
            wd[k] = v
            out.append((k, v))
        return out

    def op(self, eng, emit, reads=(), writes=(), inc=True):
        waits = self._collect(eng, reads, writes)
        myid = self._next_id(eng)
        if inc:
            self.cnt[eng] += 1
            self.pending_noinc[eng] = False
        else:
            self.pending_noinc[eng] = True
        for r in reads:
            if r.r.get(myid[0], 0) < myid[1]:
                r.r[myid[0]] = myid[1]
        for w in writes:
            w.w = myid
            w.r = {}
        self.ops[eng].append((waits, emit, myid[0] if inc else None, 1))
        self.n_instr += 1
        self.n_wait += len(waits)

    def dma(self, queue, out_ap, in_ap, reads=(), writes=(), dst=None):
        waits = self._collect(queue, reads, writes)
        if dst.sem is None:
            dst.sem = ("d", dst.name, id(dst))
        dst.semval += 16
        myid = (dst.sem, dst.semval)
        for r in reads:
            if r.r.get(myid[0], 0) < myid[1]:
                r.r[myid[0]] = myid[1]
        for w in writes:
            w.w = myid
            w.r = {}
        self.ops[queue].append((waits, (lambda e: e.dma_start(out=out_ap, in_=in_ap)), dst.sem, 16))
        self.n_instr += 1
        self.n_wait += len(waits)
        return myid

    def wait_all(self, eng, ids):
        self.ops[eng].append((list(ids), None, None, 0))

    def emit_all(self):
        nc = self.nc
        for e in self.ENGS:
            for waits, emit, inck, incv in self.ops[e]:
                for k, v in waits:
                    self._sem(k)
                if inck is not None:
                    self._sem(inck)
        ops = self.ops
        semh = self.semh

        def run(ename, e):
            for waits, emit, inck, incv in ops[ename]:
                for k, v in waits:
                    e.wait_ge(semh[k], v)
                if emit is not None:
                    ins = emit(e)
                    if inck is not None:
                        ins.then_inc(semh[inck], incv)

        with nc.Block() as block:
            if ops["pe"]:
                @block.tensor
                def _(e):
                    run("pe", e)
            if ops["act"]:
                @block.scalar
                def _(e):
                    run("act", e)
            if ops["dve"]:
                @block.vector
                def _(e):
                    run("dve", e)
            if ops["pool"]:
                @block.gpsimd
                def _(e):
                    run("pool", e)
            if ops["sp"]:
                @block.sync
                def _(e):
                    run("sp", e)


def build_program(n_seq=SEQ_PER_CORE, layers=(0, 1), phases=("ffn1", "mix", "ffn2"),
                  branches=("sb", "diff", "swa"), GC=2, debug=None):
    nc = bass.Bass("TRN2", target_bir_lowering=False, dynamic_dma_scratch_size=512)
    dram = {}

    def din(name, shape):
        dram[name] = nc.dram_tensor(name, list(shape), F32, kind="ExternalInput").ap()
        return dram[name]

    x_d = din("x", [n_seq, 8, 128, SEQ])
    w_ffn_in = [din("ffn1_w_in", [DEPTH, D_MODEL, 2 * D_FF]), din("ffn2_w_in", [DEPTH, D_MODEL, 2 * D_FF])]
    w_ffn_out = [din("ffn1_w_out", [DEPTH, D_FF, D_MODEL]), din("ffn2_w_out", [DEPTH, D_FF, D_MODEL])]
    w_in_d = din("w_in", [DEPTH, D_MODEL, IN_W])
    w_proj_d = {"sb": din("w_proj_sb", [DEPTH, 512, D_MODEL]),
                "diff": din("w_proj_diff", [DEPTH, 512, D_MODEL]),
                "swa": din("w_proj_swa", [DEPTH, 512, D_MODEL])}
    w_out_d = din("w_out", [DEPTH, D_MODEL, D_MODEL])
    params_d = din("params", [128, NP])
    btiles_d = din("btiles", [128, 12, 256])
    y_d = nc.dram_tensor("y", [n_seq, 8, 128, SEQ], F32, kind="ExternalOutput").ap()
    dbg_d = nc.dram_tensor("dbg", [128, 4096], F32, kind="ExternalOutput").ap() if debug else None

    with contextlib.ExitStack() as st:
        S = Sched(nc, st)

        def sb(name, shape, dt):
            return st.enter_context(nc.sbuf_tensor(name, list(shape), dt))

        def ps(name, shape, dt):
            return st.enter_context(nc.psum_tensor(name, list(shape), dt))

        xres = sb("xres", [128, 8, SEQ], F32)
        r_xres = [[Res("xres%d_%d" % (c, t)) for t in range(4)] for c in range(8)]
        xn = sb("xn", [128, 8, SEQ], BF16)
        r_xn = [[Res("xn%d_%d" % (c, t)) for t in range(4)] for c in range(8)]
        mrg = sb("mrg", [128, 8192], BF16)
        r_mrg = [Res("mrg%d" % i) for i in range(16)]
        oT = sb("oT", [128, 4, 1024], BF16)
        r_oT = [[Res("oT%d_%d" % (c, q)) for q in range(8)] for c in range(4)]
        Wall = sb("Wall", [128, 6 * 2048], BF16)
        r_W = [Res("W%d" % i) for i in range(6)]
        qT = sb("qT", [128, 1024], BF16)
        r_qT = [Res("qT%d" % i) for i in range(2)]
        kT = sb("kT", [128, SEQ], BF16)
        r_kT = [Res("kT%d" % i) for i in range(4)]
        vv = sb("vv", [128, 16, 128], BF16)
        r_vv = [Res("vv%d" % i) for i in range(4)]
        NSTG = 4
        stg = [sb("stg%d" % i, [128, 512], F32) for i in range(NSTG)]
        r_stg = [Res("stg%d" % i) for i in range(NSTG)]
        el = [sb("el%d" % i, [128, 1024], F32) for i in range(2)]
        r_el = [Res("el%d" % i) for i in range(2)]
        ca = [sb("ca%d" % i, [128, 1024], F32) for i in range(2)]
        r_ca = [Res("ca%d" % i) for i in range(2)]
        wb = [sb("wb%d" % i, [128, 1024], BF16) for i in range(2)]
        r_wb = [Res("wb%d" % i) for i in range(2)]
        wT = [sb("wT%d" % i, [128, 1024], BF16) for i in range(2)]
        r_wT = [Res("wT%d" % i) for i in range(2)]
        sq = [sb("sq%d" % i, [128, 512], BF16) for i in range(2)]
        r_sq = [Res("sq%d" % i) for i in range(2)]
        sg = [sb("sg%d" % i, [128, 512], F32) for i in range(2)]
        r_sg = [Res("sg%d" % i) for i in range(2)]
        rstd = sb("rstd", [128, 512], F32)
        r_rstd = Res("rstd")
        osb = [sb("osb%d" % i, [128, 128], BF16) for i in range(2)]
        r_osb = [Res("osb%d" % i) for i in range(2)]
        of32 = [sb("of32_%d" % i, [128, 128], F32) for i in range(2)]
        r_of32 = [Res("of32_%d" % i) for i in range(2)]
        btl = sb("btl", [128, 12, 256], F32)
        r_btl = Res("btl")
        prm = sb("prm", [128, NP], F32)
        r_prm = Res("prm")
        ones_bf = sb("ones_bf", [128, 128], BF16)
        bones_bf = sb("bones_bf", [128, 128], BF16)
        ident_bf = sb("ident_bf", [128, 128], BF16)
        onesrow = sb("onesrow", [128, 1024], BF16)
        r_const = Res("const")
        small = sb("small", [128, 64], F32)
        r_small = [Res("small%d" % i) for i in range(64)]
        lamt = sb("lamt", [128, 8], F32)
        r_lamt = Res("lamt")
        esink = sb("esink", [128, 16], F32)
        subg = sb("subg", [128, 2, 128], F32)
        tmp64 = sb("tmp64", [128, 64], F32)
        sqj = sb("sqj", [128, 128], F32)
        r_tmp64 = Res("tmp64")
        carry = [sb("carry%d" % i, [128, 1], F32) for i in range(2)]
        r_carry = [Res("carry%d" % i) for i in range(2)]

        dbgf = sb("dbgf", [128, 1024], F32) if debug else None
        r_dbg = Res("dbg")
        dbg_ids = []
        ZA = ps("ZA", [128, 1024], F32)
        ZB = ps("ZB", [128, 1024], F32)
        r_Z = [[Res("ZA0"), Res("ZA1")], [Res("ZB0"), Res("ZB1")]]
        Zt = [ZA, ZB]
        Dbanks = [(ZA, 0, r_Z[0][0]), (ZA, 512, r_Z[0][1]), (ZB, 0, r_Z[1][0]), (ZB, 512, r_Z[1][1])]
        Tt = [ps("T0", [128, 1024], BF16), ps("T1", [128, 1024], BF16)]
        r_T = [Res("T0"), Res("T1")]
        Ot = [ps("O0", [128, 512], F32), ps("O1", [128, 512], F32)]
        r_O = [Res("O0"), Res("O1")]

        state = {"d": 0, "o": 0, "stg": 0, "sq": 0, "sg": 0}

        def next_d():
            i = state["d"]
            state["d"] = (i + 1) % 4
            t, off, r = Dbanks[i]
            return t[:, off:off + 512], r

        def next_o():
            i = state["o"]
            state["o"] = (i + 1) % 2
            return Ot[i], r_O[i]

        def ACT(out, in_, func, reads, writes, **kw):
            S.op("act", lambda e: e.activation(out=out, in_=in_, func=func, **kw), reads, writes)

        def MM(out, lhsT, rhs, start, stop, reads, writes, inc):
            S.op("pe", lambda e: e.matmul(out, lhsT=lhsT, rhs=rhs, start=start, stop=stop),
                 reads, writes, inc=inc)

        def TR(out, in_, reads, writes, inc=True):
            S.op("pe", lambda e: e.transpose(out, in_, ident_bf[:]), reads, writes, inc=inc)

        def STT(eng, out, in0, scalar, in1, op0, op1, reads, writes):
            S.op(eng, lambda e: e.scalar_tensor_tensor(out=out, in0=in0, scalar=scalar, in1=in1, op0=op0, op1=op1),
                 reads, writes)

        def TT(eng, out, in0, in1, op, reads, writes):
            S.op(eng, lambda e: e.tensor_tensor(out=out, in0=in0, in1=in1, op=op), reads, writes)

        def TS(eng, out, in0, s1, s2, op0, op1, reads, writes):
            S.op(eng, lambda e: e.tensor_scalar(out=out, in0=in0, scalar1=s1, scalar2=s2, op0=op0, op1=op1),
                 reads, writes)

        def COPY(eng, out, in_, reads, writes):
            S.op(eng, lambda e: e.tensor_copy(out=out, in_=in_), reads, writes)

        def load_w(dst_ap, src_ap, n, dst_res, extra_dst=None):
            i = state["stg"]
            state["stg"] = (i + 1) % NSTG
            S.dma("sp", stg[i][:, :n], src_ap, writes=[r_stg[i]], dst=r_stg[i])
            COPY("pool", dst_ap, stg[i][:, :n], [r_stg[i]], dst_res)
            if extra_dst is not None:
                COPY("pool", extra_dst, stg[i][:, :n], [r_stg[i]], dst_res)

        def wview(r0, a, b):
            return Wall[:, r0 * 2048: r0 * 2048 + a * b].rearrange("p (a b) -> p a b", a=a)

        def pcol(c, n=1):
            return prm[:, c:c + n]

        S.dma("sp", prm[:], params_d, writes=[r_prm], dst=r_prm)
        S.dma("sp", btl[:], btiles_d, writes=[r_btl], dst=r_btl)
        S.op("pool", lambda e: e.memset(ones_bf[:], 1.0), [], [r_const])
        S.op("pool", lambda e: e.memset(onesrow[:], 1.0), [], [r_const])
        S.op("pool", lambda e: e.affine_select(out=ident_bf[:], in_=ones_bf[:], pattern=[[1, 128]],
                                               compare_op=ALU.is_equal, fill=0.0, base=0,
                                               channel_multiplier=-1), [r_const], [r_const])
        S.op("pool", lambda e: e.memset(bones_bf[:], 0.0), [], [r_const])
        S.op("pool", lambda e: e.memset(bones_bf[0:64, 0:64], 1.0), [], [r_const])
        S.op("pool", lambda e: e.memset(bones_bf[64:128, 64:128], 1.0), [], [r_const])
        for l in range(DEPTH):
            lam_init = 0.8 - 0.6 * math.exp(-0.3 * l)
            base = PC_LAM + l * 256
            for k in range(2):
                TT("dve", tmp64[:], pcol(base + 128 * k, 64), pcol(base + 128 * k + 64, 64), ALU.mult,
                   [r_prm], [r_tmp64])
                S.op("dve", lambda e, k=k, l=l: e.reduce_sum(out=small[:, 2 * l + k:2 * l + k + 1], in_=tmp64[:],
                                                             axis=mybir.AxisListType.X),
                     [r_tmp64], [r_small[2 * l + k]])
                ACT(small[:, 2 * l + k:2 * l + k + 1], small[:, 2 * l + k:2 * l + k + 1], AF.Exp,
                    [r_small[2 * l + k]], [r_small[2 * l + k]])
            STT("dve", lamt[:, l:l + 1], small[:, 2 * l:2 * l + 1], float(lam_init), small[:, 2 * l + 1:2 * l + 2],
                ALU.add, ALU.subtract, [r_small[2 * l], r_small[2 * l + 1]], [r_lamt])
            TS("dve", subg[:, l, :], pcol(PC_SUBLN + l * 128, 128), float(1.0 - lam_init), None, ALU.mult, ALU.bypass,
               [r_prm], [r_lamt])
        ACT(esink[:], pcol(PC_SINK, 16), AF.Exp, [r_prm], [r_lamt])

        def rmsnorm_to_xn(gcol):
            for tt in range(4):
                tsl = slice(tt * 512, (tt + 1) * 512)
                ob, r_ob = next_o()
                for dc in range(8):
                    i = state["sq"]
                    state["sq"] = (i + 1) % 2
                    ACT(sq[i][:], xres[:, dc, tsl], AF.Square, [r_xres[dc][tt]], [r_sq[i]])
                    MM(ob[:], ones_bf[:], sq[i][:], dc == 0, dc == 7, [r_sq[i], r_const], [r_ob], inc=True)
                ACT(rstd[:], ob[:], AF.Sqrt, [r_ob], [r_rstd], scale=1.0 / D_MODEL, bias=EPS)
                S.op("dve", lambda e: e.reciprocal(out=rstd[:], in_=rstd[:]), [r_rstd], [r_rstd])
                for dc in range(8):
                    STT("dve", xn[:, dc, tsl], xres[:, dc, tsl], pcol(gcol + dc), rstd[:], ALU.mult, ALU.mult,
                        [r_xres[dc][tt], r_rstd, r_prm], [r_xn[dc][tt]])

        def ffn(l, which):
            w_in_ap = w_ffn_in[which][l]
            w_out_ap = w_ffn_out[which][l]
            rmsnorm_to_xn(PC_NORM + (l * 3 + (0 if which == 0 else 2)) * 8)
            groups = [(j0, min(j0 + GC, NJ)) for j0 in range(0, NJ, GC)]
            for gi, (j0, j1) in enumerate(groups):
                n = j1 - j0
                buf = gi % 2
                wg = wview(buf * 3 + 0, 8, 256)
                wu = wview(buf * 3 + 1, 8, 256)
                wo = wview(buf * 3 + 2, 2, 1024)
                rg, ru, ro = r_W[buf * 3 + 0], r_W[buf * 3 + 1], r_W[buf * 3 + 2]
                for dc in range(8):
                    load_w(wg[:, dc, :n * 128], w_in_ap[dc * 128:(dc + 1) * 128, j0 * 128:j1 * 128], n * 128, [rg])
                    load_w(wu[:, dc, :n * 128],
                           w_in_ap[dc * 128:(dc + 1) * 128, D_FF + j0 * 128:D_FF + j1 * 128], n * 128, [ru])
                for jj in range(n):
                    for hh in range(2):
                        load_w(wo[:, jj, hh * 512:(hh + 1) * 512],
                               w_out_ap[(j0 + jj) * 128:(j0 + jj + 1) * 128, hh * 512:(hh + 1) * 512], 512, [ro])
                actb = mrg[:, buf * 4096: buf * 4096 + 4096].rearrange("p (a b) -> p a b", a=2)
                for jj in range(n):
                    for tt in range(4):
                        tsl = slice(tt * 512, (tt + 1) * 512)
                        r_act = r_mrg[buf * 8 + jj * 4 + tt]
                        hg, r_hg = next_d()
                        hu, r_hu = next_d()
                        for dc in range(8):
                            MM(hg, wg[:, dc, jj * 128:(jj + 1) * 128], xn[:, dc, tsl], dc == 0, dc == 7,
                               [rg, r_xn[dc][tt]], [r_hg], inc=(dc == 7))
                        for dc in range(8):
                            MM(hu, wu[:, dc, jj * 128:(jj + 1) * 128], xn[:, dc, tsl], dc == 0, dc == 7,
                               [ru, r_xn[dc][tt]], [r_hu], inc=(dc == 7))
                        i = state["sg"]
                        state["sg"] = (i + 1) % 2
                        ACT(sg[i][:], hg, AF.Silu, [r_hg], [r_sg[i]])
                        TT("dve", actb[:, jj, tsl], sg[i][:], hu, ALU.mult, [r_sg[i], r_hu], [r_act])
                for c in range(8):
                    for tt in range(4):
                        tsl = slice(tt * 512, (tt + 1) * 512)
                        ob, r_ob = next_o()
                        for jj in range(n):
                            MM(ob[:], wo[:, jj, c * 128:(c + 1) * 128], actb[:, jj, tsl], jj == 0, jj == n - 1,
                               [ro, r_mrg[buf * 8 + jj * 4 + tt]], [r_ob], inc=(jj == n - 1))
                        STT("dve", xres[:, c, tsl], ob[:], 0.5, xres[:, c, tsl], ALU.mult, ALU.add,
                            [r_ob, r_xres[c][tt]], [r_xres[c][tt]])

        def proj_fm(wv_, rw, col0, dst, r_dst_fn, tok0, ntile, mode, gain_col=None, alt=0):
            for ti in range(ntile):
                t0 = tok0 + ti * 512
                tt = t0 // 512
                d, r_d = next_d()
                for dc in range(8):
                    MM(d, wv_[:, dc, col0:col0 + 128], xn[:, dc, t0:t0 + 512], dc == 0, dc == 7,
                       [r_xn[dc][tt]] + rw, [r_d], inc=(dc == 7))
                dsl = dst[:, ti * 512:(ti + 1) * 512]
                rd = r_dst_fn(ti)
                if mode == "copy":
                    if (ti + alt) % 2 == 0:
                        ACT(dsl, d, AF.Copy, [r_d], [rd])
                    else:
                        COPY("dve", dsl, d, [r_d], [rd])
                else:
                    i = state["sq"]
                    state["sq"] = (i + 1) % 2
                    ACT(sq[i][:], d, AF.Square, [r_d], [r_sq[i]])
                    d2, r_d2 = next_d()
                    MM(d2, bones_bf[:], sq[i][:], True, True, [r_sq[i], r_const], [r_d2], inc=True)
                    j = state["sg"]
                    state["sg"] = (j + 1) % 2
                    ACT(sg[j][:], d2, AF.Sqrt, [r_d2], [r_sg[j]], scale=1.0 / 64.0, bias=EPS)
                    S.op("dve", lambda e, j=j: e.reciprocal(out=sg[j][:], in_=sg[j][:]), [r_sg[j]], [r_sg[j]])
                    STT("dve", dsl, d, pcol(gain_col), sg[j][:], ALU.mult, ALU.mult, [r_d, r_sg[j], r_prm], [rd])

        def proj_v(wv_, rw, col0, ncol, kb0, kb1):
            for g0 in range(kb0, kb1, 4):
                d, r_d = next_d()
                nb = min(4, kb1 - g0)
                for bi in range(nb):
                    kb = g0 + bi
                    for dc in range(8):
                        MM(d[:, bi * 128: bi * 128 + ncol], xn[:, dc, kb * 128:(kb + 1) * 128],
                           wv_[:, dc, col0:col0 + ncol], dc == 0, dc == 7,
                           [r_xn[dc][kb // 4]] + rw, [r_d], inc=(dc == 7 and bi == nb - 1))
                src = d.rearrange("p (a b) -> p a b", a=4)[:, :nb, :ncol]
                COPY("dve", vv[:, g0:g0 + nb, :ncol], src, [r_d], [r_vv[g0 // 4]])

        def attention(kind, l, hp, q0):
            items = []
            for qi in range(8):
                qb = q0 // 128 + qi
                kstart = qb * 128
                kend = SEQ if kind != "swa" else min(SEQ, kstart + 256)
                segs = []
                k0 = kstart
                while k0 < kend:
                    n = min(1024, kend - k0)
                    segs.append((k0, n))
                    k0 += n
                if kind == "diff":
                    subs = [0, 1]
                else:
                    subs = [0, 1]
                for sidx, sub in enumerate(subs):
                    for si, (k0, n) in enumerate(segs):
                        items.append(dict(qi=qi, qb=qb, sub=sub, si=si, k0=k0, n=n, nseg=len(segs),
                                          first_of_qb=(sidx == 0 and si == 0),
                                          last_of_qb=(sidx == len(subs) - 1 and si == len(segs) - 1)))
            ostate = {}

            def stage_A(it, ib):
                base = it["sub"] * 64
                qsl = qT[base:base + 64, it["qi"] * 128:(it["qi"] + 1) * 128]
                n, k0 = it["n"], it["k0"]
                for c0 in range(0, n, 512):
                    cn = min(512, n - c0)
                    MM(Zt[ib][:, c0:c0 + cn], qsl, kT[base:base + 64, k0 + c0:k0 + c0 + cn], True, True,
                       [r_qT[it["qi"] // 4]] + r_kT[(k0 + c0) // 512:(k0 + c0 + cn - 1) // 512 + 1], [r_Z[ib][c0 // 512]], inc=True)

            def stage_B(it, ib):
                n, k0, si = it["n"], it["k0"], it["si"]
                rz = r_Z[ib][:(n + 511) // 512]
                Z = Zt[ib]
                if kind == "sb":
                    ACT(el[ib][:, :n], Z[:, :n], AF.Exp, rz, [r_el[ib]], scale=SCALE)
                    ACT(el[ib][:, :n], el[ib][:, :n], AF.Ln, [r_el[ib]], [r_el[ib]], bias=1.0)
                    if si == 0:
                        S.op("pool", lambda e: e.affine_select(out=el[ib][:, 0:128], in_=el[ib][:, 0:128],
                                                               pattern=[[1, 128]], compare_op=ALU.is_gt, fill=0.0,
                                                               base=0, channel_multiplier=-1),
                             [r_el[ib]], [r_el[ib]])
                        init = 0.0
                        rinit = []
                    else:
                        init = carry[it["sub"]][:, 0:1]
                        rinit = [r_carry[it["sub"]]]
                    S.op("dve", lambda e: e.tensor_tensor_scan(out=ca[ib][:, :n], data0=onesrow[:, :n],
                                                               data1=el[ib][:, :n], initial=init,
                                                               op0=ALU.mult, op1=ALU.add),
                         [r_el[ib], r_const] + rinit, [r_ca[ib]])
                    if si < it["nseg"] - 1:
                        COPY("dve", carry[it["sub"]][:, 0:1], ca[ib][:, n - 1:n], [r_ca[ib]], [r_carry[it["sub"]]])
                    STT("dve", el[ib][:, :n], Z[:, :n], SCALE, ca[ib][:, :n], ALU.mult, ALU.subtract,
                        rz + [r_ca[ib]], [r_el[ib]])
                    ACT(wb[ib][:, :n], el[ib][:, :n], AF.Exp, [r_el[ib]], [r_wb[ib]])
                    if si == 0:
                        S.op("pool", lambda e: e.affine_select(out=wb[ib][:, 0:128], in_=wb[ib][:, 0:128],
                                                               pattern=[[1, 128]], compare_op=ALU.is_gt, fill=0.0,
                                                               base=0, channel_multiplier=-1),
                             [r_wb[ib]], [r_wb[ib]])
                else:
                    if kind == "diff":
                        head = hp
                    else:
                        head = 4 + hp * 2 + it["sub"]
                    bt = btl[:, head, :]
                    scol0 = 8 + it["sub"] * 4 + (it["qi"] % 2) * 16
                    if si == 0:
                        nn = min(256, n)
                        STT("dve", el[ib][:, :nn], Z[:, :nn], SCALE, bt[:, :nn], ALU.mult, ALU.add,
                            rz[:1] + [r_btl], [r_el[ib]])
                        S.op("act", lambda e: e.activation(out=wb[ib][:, :nn], in_=el[ib][:, :nn], func=AF.Exp,
                                                           accum_out=small[:, scol0:scol0 + 1]),
                             [r_el[ib]], [r_wb[ib], r_small[scol0]])
                        if n > nn:
                            TS("dve", el[ib][:, nn:n], Z[:, nn:n], SCALE, pcol(PC_CFAR + head), ALU.mult, ALU.add,
                               rz + [r_prm], [r_el[ib]])
                            S.op("act", lambda e: e.activation(out=wb[ib][:, nn:n], in_=el[ib][:, nn:n], func=AF.Exp,
                                                               accum_out=small[:, scol0 + 1:scol0 + 2]),
                                 [r_el[ib]], [r_wb[ib], r_small[scol0 + 1]])
                    else:
                        TS("dve", el[ib][:, :n], Z[:, :n], SCALE, pcol(PC_CFAR + head), ALU.mult, ALU.add,
                           rz + [r_prm], [r_el[ib]])
                        S.op("act", lambda e: e.activation(out=wb[ib][:, :n], in_=el[ib][:, :n], func=AF.Exp,
                                                           accum_out=small[:, scol0 + 1 + si:scol0 + 2 + si]),
                             [r_el[ib]], [r_wb[ib], r_small[scol0 + 1 + si]])

            def stage_C(it, ib):
                n, k0 = it["n"], it["k0"]
                nb = n // 128
                qi = it["qi"]
                for jb in range(nb):
                    TR(Tt[ib][:, jb * 128:(jb + 1) * 128], wb[ib][:, jb * 128:(jb + 1) * 128],
                       [r_wb[ib], r_const], [r_T[ib]], inc=(jb == nb - 1))
                ACT(wT[ib][:, :n], Tt[ib][:, :n], AF.Copy, [r_T[ib]], [r_wT[ib]])
                if it["first_of_qb"]:
                    if kind == "diff":
                        ostate["o"] = (0, 1)
                    else:
                        oi = state["o"]
                        state["o"] = (oi + 1) % 2
                        ostate["o"] = (oi, oi)
                if kind == "diff":
                    oi = ostate["o"][it["sub"]]
                    ocols = slice(0, 128)
                    vcols = slice(0, 128)
                elif kind == "sb":
                    oi = ostate["o"][0]
                    ocols = slice(it["sub"] * 64, it["sub"] * 64 + 64)
                    vcols = ocols
                else:
                    oi = ostate["o"][0]
                    ocols = slice(it["sub"] * 64, it["sub"] * 64 + 64)
                    g = hp // 2
                    vcols = slice(g * 64, g * 64 + 64)
                for jb in range(nb):
                    kb = k0 // 128 + jb
                    MM(Ot[oi][:, ocols], wT[ib][:, jb * 128:(jb + 1) * 128], vv[:, kb, vcols],
                       it["si"] == 0 and jb == 0, it["si"] == it["nseg"] - 1 and jb == nb - 1,
                       [r_wT[ib], r_vv[kb // 4]], [r_O[oi]], inc=(jb == nb - 1))
                if kind == "swa":
                    sc = 8 + it["sub"] * 4 + (qi % 2) * 16
                    head = hp * 2 + it["sub"]
                    fb = qi % 2
                    TT("dve", small[:, sc + 2:sc + 3], small[:, sc:sc + 1], esink[:, l * 8 + head:l * 8 + head + 1],
                       ALU.add, [r_small[sc], r_lamt], [r_small[sc + 2]])
                    S.op("dve", lambda e: e.reciprocal(out=small[:, sc + 2:sc + 3], in_=small[:, sc + 2:sc + 3]),
                         [r_small[sc + 2]], [r_small[sc + 2]])
                    ACT(osb[fb][:, ocols], Ot[oi][:, ocols], AF.Copy, [r_O[oi], r_small[sc + 2]], [r_osb[fb]],
                        scale=small[:, sc + 2:sc + 3])
                if it["last_of_qb"]:
                    stage_F(it, qi)

            def stage_F(it, qi):
                fb = qi % 2
                if kind == "sb":
                    oi = ostate["o"][0]
                    ACT(osb[fb][:], Ot[oi][:, 0:128], AF.Copy, [r_O[oi]], [r_osb[fb]])
                elif kind == "diff":
                    nseg = it["nseg"]
                    for c in range(2):
                        sc = 8 + c * 4 + (qi % 2) * 16
                        tot = 16 + c
                        n0 = it_n0[qi]
                        cols = [sc]
                        if n0 > 256:
                            cols.append(sc + 1)
                        if nseg > 1:
                            cols.append(sc + 2)
                        if len(cols) == 1:
                            COPY("dve", small[:, tot:tot + 1], small[:, cols[0]:cols[0] + 1], [r_small[cols[0]]],
                                 [r_small[tot]])
                        else:
                            TT("dve", small[:, tot:tot + 1], small[:, cols[0]:cols[0] + 1],
                               small[:, cols[1]:cols[1] + 1], ALU.add, [r_small[cols[0]], r_small[cols[1]]],
                               [r_small[tot]])
                            if len(cols) == 3:
                                TT("dve", small[:, tot:tot + 1], small[:, tot:tot + 1],
                                   small[:, cols[2]:cols[2] + 1], ALU.add, [r_small[tot], r_small[cols[2]]],
                                   [r_small[tot]])
                        S.op("dve", lambda e, tot=tot: e.reciprocal(out=small[:, tot:tot + 1], in_=small[:, tot:tot + 1]),
                             [r_small[tot]], [r_small[tot]])
                    STT("dve", small[:, 18:19], small[:, 17:18], -1.0, lamt[:, l:l + 1], ALU.mult, ALU.mult,
                        [r_small[17], r_lamt], [r_small[18]])
                    ACT(of32[fb][:], Ot[0][:, 0:128], AF.Copy, [r_O[0], r_small[16]], [r_of32[fb]],
                        scale=small[:, 16:17])
                    STT("dve", of32[fb][:], Ot[1][:, 0:128], small[:, 18:19], of32[fb][:], ALU.mult, ALU.add,
                        [r_O[1], r_small[18], r_of32[fb]], [r_of32[fb]])
                    S.op("act", lambda e: e.activation(out=sqj[:], in_=of32[fb][:],
                                                       func=AF.Square, accum_out=small[:, 19:20]),
                         [r_of32[fb]], [r_tmp64, r_small[19]])
                    ACT(small[:, 19:20], small[:, 19:20], AF.Sqrt, [r_small[19]], [r_small[19]],
                        scale=1.0 / 128.0, bias=EPS)
                    S.op("dve", lambda e: e.reciprocal(out=small[:, 19:20], in_=small[:, 19:20]),
                         [r_small[19]], [r_small[19]])
                    STT("dve", osb[fb][:], of32[fb][:], small[:, 19:20], subg[:, l, :], ALU.mult, ALU.mult,
                        [r_of32[fb], r_small[19], r_lamt], [r_osb[fb]])
                tb = qi % 2
                TR(Tt[tb][:, 0:128], osb[fb][:], [r_osb[fb], r_const], [r_T[tb]], inc=True)
                COPY("dve", oT[:, hp, qi * 128:(qi + 1) * 128], Tt[tb][:, 0:128], [r_T[tb]], [r_oT[hp][qi]])

            it_n0 = {}
            for it in items:
                if it["si"] == 0:
                    it_n0[it["qi"]] = it["n"]
            for i in range(len(items) + 1):
                if i < len(items):
                    stage_A(items[i], i % 2)
                    stage_B(items[i], i % 2)
                if i >= 1:
                    stage_C(items[i - 1], (i - 1) % 2)

        def epilogue(bname, l, first, goff):
            wp_ap = w_proj_d[bname][l]
            for ch in range(2):
                buf = ch
                wp = wview(buf * 3 + 0, 4, 512)
                wgt = wview(buf * 3 + 1, 8, 512)
                rp = [r_W[buf * 3 + 0]]
                rgt = [r_W[buf * 3 + 1], r_W[buf * 3 + 2]]
                for k in range(4):
                    load_w(wp[:, k, :], wp_ap[k * 128:(k + 1) * 128, ch * 512:(ch + 1) * 512], 512, rp)
                for dc in range(8):
                    load_w(wgt[:, dc, :], w_in_d[l][dc * 128:(dc + 1) * 128, goff + ch * 512: goff + (ch + 1) * 512],
                           512, rgt)
                for cc in range(4):
                    c = ch * 4 + cc
                    for ti in range(2):
                        t0 = state["tok0"] + ti * 512
                        tt = t0 // 512
                        P, r_P = next_d()
                        G, r_G = next_d()
                        for k in range(4):
                            MM(P, wp[:, k, cc * 128:(cc + 1) * 128], oT[:, k, ti * 512:(ti + 1) * 512], k == 0, k == 3,
                               rp + r_oT[k][ti * 4:(ti + 1) * 4], [r_P], inc=(k == 3))
                        for dc in range(8):
                            MM(G, wgt[:, dc, cc * 128:(cc + 1) * 128], xn[:, dc, t0:t0 + 512], dc == 0, dc == 7,
                               rgt + [r_xn[dc][tt]], [r_G], inc=(dc == 7))
                        i = state["sg"]
                        state["sg"] = (i + 1) % 2
                        ACT(sg[i][:], G, AF.Sigmoid, [r_G], [r_sg[i]])
                        msl = mrg[:, (c * 2 + ti) * 512:(c * 2 + ti + 1) * 512]
                        rm = r_mrg[c * 2 + ti]
                        if first:
                            TT("dve", msl, sg[i][:], P, ALU.mult, [r_sg[i], r_P], [rm])
                        else:
                            TT("dve", sg[i][:], sg[i][:], P, ALU.mult, [r_sg[i], r_P], [r_sg[i]])
                            TT("pool", msl, msl, sg[i][:], ALU.add, [r_sg[i], rm], [rm])

        def mixer_out(l):
            for ch in range(2):
                wo = wview(ch * 3, 8, 512)
                ro = [r_W[ch * 3], r_W[ch * 3 + 1]]
                for c in range(8):
                    load_w(wo[:, c, :], w_out_d[l][c * 128:(c + 1) * 128, ch * 512:(ch + 1) * 512], 512, ro)
                for cc in range(4):
                    c2 = ch * 4 + cc
                    for ti in range(2):
                        t0 = state["tok0"] + ti * 512
                        tt = t0 // 512
                        ob, r_ob = next_o()
                        for c in range(8):
                            MM(ob[:], wo[:, c, cc * 128:(cc + 1) * 128], mrg[:, (c * 2 + ti) * 512:(c * 2 + ti + 1) * 512],
                               c == 0, c == 7, ro + [r_mrg[c * 2 + ti]], [r_ob], inc=(c == 7))
                        TT("dve", xres[:, c2, t0:t0 + 512], ob[:], xres[:, c2, t0:t0 + 512], ALU.add,
                           [r_ob, r_xres[c2][tt]], [r_xres[c2][tt]])

        def mixer(l):
            rmsnorm_to_xn(PC_NORM + (l * 3 + 1) * 8)
            for half in range(2):
                tok0 = half * 1024
                state["tok0"] = tok0
                nkt = (SEQ - tok0) // 512
                first = True
                for bname in branches:
                    if bname == "sb":
                        qoff, koff, voff, goff = OFF_QA, OFF_KA, OFF_VA, OFF_GA
                    elif bname == "diff":
                        qoff, koff, voff, goff = OFF_QD, OFF_KD, OFF_VD, OFF_GD
                    else:
                        qoff, koff, voff, goff = OFF_QS, OFF_KS, OFF_VS, OFF_GS
                    wq = wview(0, 8, 512)
                    wk = wview(2, 8, 512)
                    wv_ = wview(4, 8, 512)
                    rq, rk, rv = [r_W[0], r_W[1]], [r_W[2], r_W[3]], [r_W[4], r_W[5]]
                    for dc in range(8):
                        rows = slice(dc * 128, (dc + 1) * 128)
                        load_w(wq[:, dc, :], w_in_d[l][rows, qoff:qoff + 512], 512, rq)
                        if bname != "swa":
                            load_w(wk[:, dc, :], w_in_d[l][rows, koff:koff + 512], 512, rk)
                            load_w(wv_[:, dc, :], w_in_d[l][rows, voff:voff + 512], 512, rv)
                        else:
                            i = state["stg"]
                            state["stg"] = (i + 1) % NSTG
                            S.dma("sp", stg[i][:, :256], w_in_d[l][rows, koff:koff + 256], writes=[r_stg[i]],
                                  dst=r_stg[i])
                            for g in range(2):
                                for dup in range(2):
                                    COPY("pool", wk[:, dc, g * 128 + dup * 64: g * 128 + dup * 64 + 64],
                                         stg[i][:, g * 64:(g + 1) * 64], [r_stg[i]], rk)
                            COPY("pool", wv_[:, dc, 0:128], stg[i][:, 128:256], [r_stg[i]], rv)
                    for hp in range(4):
                        if bname == "sb":
                            proj_fm(wq, rq, hp * 128, qT, lambda ti: r_qT[ti], tok0, 2, "copy", alt=0)
                            proj_fm(wk, rk, hp * 128, kT[:, tok0:], lambda ti: r_kT[(tok0 // 512) + ti], tok0, nkt,
                                    "copy", alt=1)
                            proj_v(wv_, rv, hp * 128, 128, tok0 // 128, 16)
                        elif bname == "diff":
                            proj_fm(wq, rq, hp * 128, qT, lambda ti: r_qT[ti], tok0, 2, "qknorm",
                                    gain_col=PC_QK + l * 4 + 0)
                            proj_fm(wk, rk, hp * 128, kT[:, tok0:], lambda ti: r_kT[(tok0 // 512) + ti], tok0, nkt,
                                    "qknorm", gain_col=PC_QK + l * 4 + 1)
                            proj_v(wv_, rv, hp * 128, 128, tok0 // 128, 16)
                        else:
                            proj_fm(wq, rq, hp * 128, qT, lambda ti: r_qT[ti], tok0, 2, "qknorm",
                                    gain_col=PC_QK + l * 4 + 2)
                            if hp % 2 == 0:
                                g = hp // 2
                                proj_fm(wk, rk, g * 128, kT[:, tok0:], lambda ti: r_kT[(tok0 // 512) + ti], tok0, nkt,
                                        "qknorm", gain_col=PC_QK + l * 4 + 3)
                            if hp == 0:
                                proj_v(wv_, rv, 0, 128, tok0 // 128, 16)
                        attention(bname, l, hp, tok0)
                    if debug == "oT" and half == 1:
                        for c4 in range(4):
                            COPY("dve", dbgf[:], oT[:, c4, :], r_oT[c4], [r_dbg])
                            dbg_ids.append(S.dma("sp", dbg_d[:, c4 * 1024:(c4 + 1) * 1024], dbgf[:], reads=[r_dbg],
                                                 dst=r_dbg))
                    epilogue(bname, l, first, goff)
                    first = False
                mixer_out(l)

        out_ids = []
        r_out = [Res("yout%d" % i) for i in range(8)]
        for s in range(n_seq):
            for dc in range(8):
                S.dma("sp", xres[:, dc, :], x_d[s, dc], writes=r_xres[dc], dst=r_xres[dc][0])
                for t in range(1, 4):
                    r_xres[dc][t].w = r_xres[dc][0].w
            for l in layers:
                if "ffn1" in phases:
                    ffn(l, 0)
                if "mix" in phases:
                    mixer(l)
                if "ffn2" in phases:
                    ffn(l, 1)
            for dc in range(8):
                out_ids.append(S.dma("sp", y_d[s, dc], xres[:, dc, :], reads=r_xres[dc], dst=r_out[dc]))
        S.wait_all("sp", out_ids[-8:] + dbg_ids)
        S.emit_all()
        stats = dict(n_instr=S.n_instr, n_wait=S.n_wait, nsem=S.nsem)
    return nc, stats


def _t5_bucket_np(n):
    n = np.maximum(n, 0)
    max_exact = 16
    nf = np.maximum(n, 1).astype(np.float32)
    large = max_exact + (np.log(nf / np.float32(max_exact)) / np.float32(math.log(128 / max_exact))
                         * np.float32(32 - max_exact)).astype(np.int32)
    large = np.minimum(large, 31)
    return np.where(n < max_exact, n, large)


def _host_layout(inputs):
    f32 = np.float32
    prm = np.zeros((128, NP), f32)
    p = np.arange(128)
    norms = [inputs["ffn1_norm"], inputs["mix_norm"], inputs["ffn2_norm"]]
    for l in range(DEPTH):
        for which in range(3):
            g = np.asarray(norms[which][l], f32).reshape(8, 128)
            prm[:, PC_NORM + (l * 3 + which) * 8: PC_NORM + (l * 3 + which) * 8 + 8] = g.T
        for k, name in enumerate(["q_norm_diff", "k_norm_diff", "q_norm_swa", "k_norm_swa"]):
            prm[:, PC_QK + l * 4 + k] = np.asarray(inputs[name][l], f32)[p % 64]
        prm[:, PC_SINK + l * 8: PC_SINK + l * 8 + 8] = np.asarray(inputs["swa_sinks"][l], f32)[None, :]
        prm[:, PC_LAM + l * 256: PC_LAM + (l + 1) * 256] = np.asarray(inputs["diff_lambda"][l], f32).reshape(1, 256)
        prm[:, PC_SUBLN + l * 128: PC_SUBLN + (l + 1) * 128] = np.asarray(inputs["diff_subln"][l], f32)[None, :]
    rb = np.asarray(inputs["rel_bias"], f32)
    prm[:, PC_CFAR: PC_CFAR + 4] = rb[31, 0:4][None, :]
    i = np.arange(128)[:, None]
    j = np.arange(128)[None, :]
    d0 = j - i
    d1 = 128 + j - i
    b0 = _t5_bucket_np(d0)
    b1 = _t5_bucket_np(d1)
    bt = np.zeros((128, 12, 256), f32)
    for h in range(12):
        t0 = rb[b0, h]
        t0 = np.where(d0 >= 0, t0, f32(MASKV))
        t1 = rb[b1, h]
        if h >= 4:
            t1 = np.where(d1 < 128, t1, f32(MASKV))
        bt[:, h, 0:128] = t0
        bt[:, h, 128:256] = t1
    return prm, bt


_CACHE = {}


def kernel(**inputs):
    x = np.asarray(inputs["x"], np.float32)
    B = x.shape[0]
    prm, bt = _host_layout(inputs)
    if "nc" not in _CACHE:
        _CACHE["nc"] = build_program()[0]
    nc = _CACHE["nc"]
    shared = {k: np.ascontiguousarray(np.asarray(inputs[k], np.float32)) for k in
              ["ffn1_w_in", "ffn1_w_out", "ffn2_w_in", "ffn2_w_out", "w_in", "w_proj_sb", "w_proj_diff",
               "w_proj_swa", "w_out"]}
    shared["params"] = prm
    shared["btiles"] = bt
    in_maps = []
    for c in range(N_CORES):
        xs = x[c * SEQ_PER_CORE:(c + 1) * SEQ_PER_CORE]
        xs = xs[:, ::-1, :]
        xt = np.ascontiguousarray(xs.transpose(0, 2, 1)).reshape(SEQ_PER_CORE, 8, 128, SEQ)
        m = dict(shared)
        m["x"] = xt
        in_maps.append(m)
    res = run_bass_kernel_spmd(nc, in_maps, core_ids=list(range(N_CORES)))
    out = np.empty((B, SEQ, D_MODEL), np.float32)
    for c in range(N_CORES):
        y = np.asarray(res.results[c]["y"]).reshape(SEQ_PER_CORE, D_MODEL, SEQ)
        out[c * SEQ_PER_CORE:(c + 1) * SEQ_PER_CORE] = y.transpose(0, 2, 1)[:, ::-1, :]
    return out
```

```python
import contextlib
import math
import numpy as np
import concourse.bass as bass
import concourse.mybir as mybir
from concourse.bass_utils import run_bass_kernel_spmd

F32 = mybir.dt.float32
BF16 = mybir.dt.bfloat16
AF = mybir.ActivationFunctionType
ALU = mybir.AluOpType

D_MODEL = 1024
SEQ = 2048
DEPTH = 2
D_FF = 2816
NJ = D_FF // 128
IN_W = 6912
EPS = 1e-6
N_CORES = 8
SEQ_PER_CORE = 2
MASKV = -30000.0
SCALE = 0.125

OFF_QA, OFF_KA, OFF_VA = 0, 512, 1024
OFF_QD, OFF_KD, OFF_VD = 1536, 2048, 2560
OFF_QS, OFF_KS, OFF_VS = 3072, 3584, 3712
OFF_GA, OFF_GD, OFF_GS = 3840, 4864, 5888

PC_NORM = 0
PC_QK = 48
PC_SINK = 56
PC_CFAR = 72
PC_LAM = 76
PC_SUBLN = 588
NP = 844


class Res:
    __slots__ = ("name", "w", "r", "sem", "semval")

    def __init__(self, name):
        self.name = name
        self.w = None
        self.r = {}
        self.sem = None
        self.semval = 0


class Sched:
    ROT = 30000
    ENGS = ("pe", "act", "dve", "pool", "sp")

    def __init__(self, nc, stack):
        self.nc = nc
        self.stack = stack
        self.ops = {e: [] for e in self.ENGS}
        self.cnt = {e: 0 for e in self.ENGS}
        self.epoch = {e: 0 for e in self.ENGS}
        self.waited = {e: {} for e in self.ENGS}
        self.semh = {}
        self.nsem = 0
        self.pending_noinc = {e: False for e in self.ENGS}
        self.n_instr = 0
        self.n_wait = 0

    def _sem(self, key):
        h = self.semh.get(key)
        if h is None:
            h = self.stack.enter_context(self.nc.semaphore("s%d" % self.nsem))
            self.nsem += 1
            self.semh[key] = h
        return h

    def _next_id(self, eng):
        if self.cnt[eng] >= self.ROT and not self.pending_noinc[eng]:
            self.epoch[eng] += 1
            self.cnt[eng] = 0
        return (("e", eng, self.epoch[eng]), self.cnt[eng] + 1)

    def _collect(self, eng, reads, writes):
        deps = {}

        def add(d):
            if d is None:
                return
            k, v = d
            if eng == "pe" and k[0] == "e" and k[1] == "pe":
                return
            if deps.get(k, 0) < v:
                deps[k] = v
        for r in reads:
            add(r.w)
        for w in writes:
            add(w.w)
            for d in w.r.items():
                add(d)
        out = []
        wd = self.waited[eng]
        for k, v in deps.items():
            if wd.get(k, 0) >= v:
                continue
            wd[k] = v
            out.append((k, v))
        return out

    def op(self, eng, emit, reads=(), writes=(), inc=True):
        waits = self._collect(eng, reads, writes)
        myid = self._next_id(eng)
        if inc:
            self.cnt[eng] += 1
            self.pending_noinc[eng] = False
        else:
            self.pending_noinc[eng] = True
        for r in reads:
            if r.r.get(myid[0], 0) < myid[1]:
                r.r[myid[0]] = myid[1]
        for w in writes:
            w.w = myid
            w.r = {}
        self.ops[eng].append((waits, emit, myid[0] if inc else None, 1))
        self.n_instr += 1
        self.n_wait += len(waits)

    def dma(self, queue, out_ap, in_ap, reads=(), writes=(), dst=None):
        waits = self._collect(queue, reads, writes)
        if dst.sem is None:
            dst.sem = ("d", dst.name, id(dst))
        dst.semval += 16
        myid = (dst.sem, dst.semval)
        for r in reads:
            if r.r.get(myid[0], 0) < myid[1]:
                r.r[myid[0]] = myid[1]
        for w in writes:
            w.w = myid
            w.r = {}
        self.ops[queue].append((waits, (lambda e: e.dma_start(out=out_ap, in_=in_ap)), dst.sem, 16))
        self.n_instr += 1
        self.n_wait += len(waits)
        return myid

    def wait_all(self, eng, ids):
        self.ops[eng].append((list(ids), None, None, 0))

    def emit_all(self):
        nc = self.nc
        for e in self.ENGS:
            for waits, emit, inck, incv in self.ops[e]:
                for k, v in waits:
                    self._sem(k)
                if inck is not None:
                    self._sem(inck)
        ops = self.ops
        semh = self.semh

        def run(ename, e):
            for waits, emit, inck, incv in ops[ename]:
                for k, v in waits:
                    e.wait_ge(semh[k], v)
                if emit is not None:
                    ins = emit(e)
                    if inck is not None:
                        ins.then_inc(semh[inck], incv)

        with nc.Block() as block:
            if ops["pe"]:
                @block.tensor
                def _(e):
                    run("pe", e)
            if ops["act"]:
                @block.scalar
                def _(e):
                    run("act", e)
            if ops["dve"]:
                @block.vector
                def _(e):
                    run("dve", e)
            if ops["pool"]:
                @block.gpsimd
                def _(e):
                    run("pool", e)
            if ops["sp"]:
                @block.sync
                def _(e):
                    run("sp", e)


def build_program(n_seq=SEQ_PER_CORE, layers=(0, 1), phases=("ffn1", "mix", "ffn2"),
                  branches=("sb", "diff", "swa"), GC=2, debug=None):
    nc = bass.Bass("TRN2", target_bir_lowering=False, dynamic_dma_scratch_size=512)
    dram = {}

    def din(name, shape):
        dram[name] = nc.dram_tensor(name, list(shape), F32, kind="ExternalInput").ap()
        return dram[name]

    x_d = din("x", [n_seq, 8, 128, SEQ])
    w_ffn_in = [din("ffn1_w_in", [DEPTH, D_MODEL, 2 * D_FF]), din("ffn2_w_in", [DEPTH, D_MODEL, 2 * D_FF])]
    w_ffn_out = [din("ffn1_w_out", [DEPTH, D_FF, D_MODEL]), din("ffn2_w_out", [DEPTH, D_FF, D_MODEL])]
    w_in_d = din("w_in", [DEPTH, D_MODEL, IN_W])
    w_proj_d = {"sb": din("w_proj_sb", [DEPTH, 512, D_MODEL]),
                "diff": din("w_proj_diff", [DEPTH, 512, D_MODEL]),
                "swa": din("w_proj_swa", [DEPTH, 512, D_MODEL])}
    w_out_d = din("w_out", [DEPTH, D_MODEL, D_MODEL])
    params_d = din("params", [128, NP])
    btiles_d = din("btiles", [128, 12, 256])
    y_d = nc.dram_tensor("y", [n_seq, 8, 128, SEQ], F32, kind="ExternalOutput").ap()
    dbg_d = nc.dram_tensor("dbg", [128, 4096], F32, kind="ExternalOutput").ap() if debug else None

    with contextlib.ExitStack() as st:
        S = Sched(nc, st)

        def sb(name, shape, dt):
            return st.enter_context(nc.sbuf_tensor(name, list(shape), dt))

        def ps(name, shape, dt):
            return st.enter_context(nc.psum_tensor(name, list(shape), dt))

        xres = sb("xres", [128, 8, SEQ], F32)
        r_xres = [[Res("xres%d_%d" % (c, t)) for t in range(4)] for c in range(8)]
        xn = sb("xn", [128, 8, SEQ], BF16)
        r_xn = [[Res("xn%d_%d" % (c, t)) for t in range(4)] for c in range(8)]
        mrg = sb("mrg", [128, 8192], BF16)
        r_mrg = [Res("mrg%d" % i) for i in range(16)]
        oT = sb("oT", [128, 4, 1024], BF16)
        r_oT = [[Res("oT%d_%d" % (c, q)) for q in range(8)] for c in range(4)]
        Wall = sb("Wall", [128, 6 * 2048], BF16)
        r_W = [Res("W%d" % i) for i in range(6)]
        qT = sb("qT", [128, 1024], BF16)
        r_qT = [Res("qT%d" % i) for i in range(2)]
        kT = sb("kT", [128, SEQ], BF16)
        r_kT = [Res("kT%d" % i) for i in range(4)]
        vv = sb("vv", [128, 16, 128], BF16)
        r_vv = [Res("vv%d" % i) for i in range(4)]
        NSTG = 4
        stg = [sb("stg%d" % i, [128, 512], F32) for i in range(NSTG)]
        r_stg = [Res("stg%d" % i) for i in range(NSTG)]
        el = [sb("el%d" % i, [128, 1024], F32) for i in range(2)]
        r_el = [Res("el%d" % i) for i in range(2)]
        ca = [sb("ca%d" % i, [128, 1024], F32) for i in range(2)]
        r_ca = [Res("ca%d" % i) for i in range(2)]
        wb = [sb("wb%d" % i, [128, 1024], BF16) for i in range(2)]
        r_wb = [Res("wb%d" % i) for i in range(2)]
        wT = [sb("wT%d" % i, [128, 1024], BF16) for i in range(2)]
        r_wT = [Res("wT%d" % i) for i in range(2)]
        sq = [sb("sq%d" % i, [128, 512], BF16) for i in range(2)]
        r_sq = [Res("sq%d" % i) for i in range(2)]
        sg = [sb("sg%d" % i, [128, 512], F32) for i in range(2)]
        r_sg = [Res("sg%d" % i) for i in range(2)]
        rstd = sb("rstd", [128, 512], F32)
        r_rstd = Res("rstd")
        osb = [sb("osb%d" % i, [128, 128], BF16) for i in range(2)]
        r_osb = [Res("osb%d" % i) for i in range(2)]
        of32 = [sb("of32_%d" % i, [128, 128], F32) for i in range(2)]
        r_of32 = [Res("of32_%d" % i) for i in range(2)]
        btl = sb("btl", [128, 12, 256], F32)
        r_btl = Res("btl")
        prm = sb("prm", [128, NP], F32)
        r_prm = Res("prm")
        ones_bf = sb("ones_bf", [128, 128], BF16)
        bones_bf = sb("bones_bf", [128, 128], BF16)
        ident_bf = sb("ident_bf", [128, 128], BF16)
        onesrow = sb("onesrow", [128, 1024], BF16)
        r_const = Res("const")
        small = sb("small", [128, 64], F32)
        r_small = [Res("small%d" % i) for i in range(64)]
        lamt = sb("lamt", [128, 8], F32)
        r_lamt = Res("lamt")
        esink = sb("esink", [128, 16], F32)
        subg = sb("subg", [128, 2, 128], F32)
        tmp64 = sb("tmp64", [128, 64], F32)
        sqj = sb("sqj", [128, 128], F32)
        r_tmp64 = Res("tmp64")
        carry = [sb("carry%d" % i, [128, 1], F32) for i in range(2)]
        r_carry = [Res("carry%d" % i) for i in range(2)]

        dbgf = sb("dbgf", [128, 1024], F32) if debug else None
        r_dbg = Res("dbg")
        dbg_ids = []
        ZA = ps("ZA", [128, 1024], F32)
        ZB = ps("ZB", [128, 1024], F32)
        r_Z = [[Res("ZA0"), Res("ZA1")], [Res("ZB0"), Res("ZB1")]]
        Zt = [ZA, ZB]
        Dbanks = [(ZA, 0, r_Z[0][0]), (ZA, 512, r_Z[0][1]), (ZB, 0, r_Z[1][0]), (ZB, 512, r_Z[1][1])]
        Tt = [ps("T0", [128, 1024], BF16), ps("T1", [128, 1024], BF16)]
        r_T = [Res("T0"), Res("T1")]
        Ot = [ps("O0", [128, 512], F32), ps("O1", [128, 512], F32)]
        r_O = [Res("O0"), Res("O1")]

        state = {"d": 0, "o": 0, "stg": 0, "sq": 0, "sg": 0}

        def next_d():
            i = state["d"]
            state["d"] = (i + 1) % 4
            t, off, r = Dbanks[i]
            return t[:, off:off + 512], r

        def next_o():
            i = state["o"]
            state["o"] = (i + 1) % 2
            return Ot[i], r_O[i]

        def ACT(out, in_, func, reads, writes, **kw):
            S.op("act", lambda e: e.activation(out=out, in_=in_, func=func, **kw), reads, writes)

        def MM(out, lhsT, rhs, start, stop, reads, writes, inc):
            S.op("pe", lambda e: e.matmul(out, lhsT=lhsT, rhs=rhs, start=start, stop=stop),
                 reads, writes, inc=inc)

        def TR(out, in_, reads, writes, inc=True):
            S.op("pe", lambda e: e.transpose(out, in_, ident_bf[:]), reads, writes, inc=inc)

        def STT(eng, out, in0, scalar, in1, op0, op1, reads, writes):
            S.op(eng, lambda e: e.scalar_tensor_tensor(out=out, in0=in0, scalar=scalar, in1=in1, op0=op0, op1=op1),
                 reads, writes)

        def TT(eng, out, in0, in1, op, reads, writes):
            S.op(eng, lambda e: e.tensor_tensor(out=out, in0=in0, in1=in1, op=op), reads, writes)

        def TS(eng, out, in0, s1, s2, op0, op1, reads, writes):
            S.op(eng, lambda e: e.tensor_scalar(out=out, in0=in0, scalar1=s1, scalar2=s2, op0=op0, op1=op1),
                 reads, writes)

        def COPY(eng, out, in_, reads, writes):
            S.op(eng, lambda e: e.tensor_copy(out=out, in_=in_), reads, writes)

        def load_w(dst_ap, src_ap, n, dst_res, extra_dst=None):
            i = state["stg"]
            state["stg"] = (i + 1) % NSTG
            S.dma("sp", stg[i][:, :n], src_ap, writes=[r_stg[i]], dst=r_stg[i])
            COPY("pool", dst_ap, stg[i][:, :n], [r_stg[i]], dst_res)
            if extra_dst is not None:
                COPY("pool", extra_dst, stg[i][:, :n], [r_stg[i]], dst_res)

        def wview(r0, a, b):
            return Wall[:, r0 * 2048: r0 * 2048 + a * b].rearrange("p (a b) -> p a b", a=a)

        def pcol(c, n=1):
            return prm[:, c:c + n]

        S.dma("sp", prm[:], params_d, writes=[r_prm], dst=r_prm)
        S.dma("sp", btl[:], btiles_d, writes=[r_btl], dst=r_btl)
        S.op("pool", lambda e: e.memset(ones_bf[:], 1.0), [], [r_const])
        S.op("pool", lambda e: e.memset(onesrow[:], 1.0), [], [r_const])
        S.op("pool", lambda e: e.affine_select(out=ident_bf[:], in_=ones_bf[:], pattern=[[1, 128]],
                                               compare_op=ALU.is_equal, fill=0.0, base=0,
                                               channel_multiplier=-1), [r_const], [r_const])
        S.op("pool", lambda e: e.memset(bones_bf[:], 0.0), [], [r_const])
        S.op("pool", lambda e: e.memset(bones_bf[0:64, 0:64], 1.0), [], [r_const])
        S.op("pool", lambda e: e.memset(bones_bf[64:128, 64:128], 1.0), [], [r_const])
        for l in range(DEPTH):
            lam_init = 0.8 - 0.6 * math.exp(-0.3 * l)
            base = PC_LAM + l * 256
            for k in range(2):
                TT("dve", tmp64[:], pcol(base + 128 * k, 64), pcol(base + 128 * k + 64, 64), ALU.mult,
                   [r_prm], [r_tmp64])
                S.op("dve", lambda e, k=k, l=l: e.reduce_sum(out=small[:, 2 * l + k:2 * l + k + 1], in_=tmp64[:],
                                                             axis=mybir.AxisListType.X),
                     [r_tmp64], [r_small[2 * l + k]])
                ACT(small[:, 2 * l + k:2 * l + k + 1], small[:, 2 * l + k:2 * l + k + 1], AF.Exp,
                    [r_small[2 * l + k]], [r_small[2 * l + k]])
            STT("dve", lamt[:, l:l + 1], small[:, 2 * l:2 * l + 1], float(lam_init), small[:, 2 * l + 1:2 * l + 2],
                ALU.add, ALU.subtract, [r_small[2 * l], r_small[2 * l + 1]], [r_lamt])
            TS("dve", subg[:, l, :], pcol(PC_SUBLN + l * 128, 128), float(1.0 - lam_init), None, ALU.mult, ALU.bypass,
               [r_prm], [r_lamt])
        ACT(esink[:], pcol(PC_SINK, 16), AF.Exp, [r_prm], [r_lamt])

        def rmsnorm_to_xn(gcol):
            for tt in range(4):
                tsl = slice(tt * 512, (tt + 1) * 512)
                ob, r_ob = next_o()
                for dc in range(8):
                    i = state["sq"]
                    state["sq"] = (i + 1) % 2
                    ACT(sq[i][:], xres[:, dc, tsl], AF.Square, [r_xres[dc][tt]], [r_sq[i]])
                    MM(ob[:], ones_bf[:], sq[i][:], dc == 0, dc == 7, [r_sq[i], r_const], [r_ob], inc=True)
                ACT(rstd[:], ob[:], AF.Ln, [r_ob], [r_rstd], scale=1.0 / D_MODEL, bias=EPS)
                ACT(rstd[:], rstd[:], AF.Exp, [r_rstd], [r_rstd], scale=-0.5)
                for dc in range(8):
                    STT("dve", xn[:, dc, tsl], xres[:, dc, tsl], pcol(gcol + dc), rstd[:], ALU.mult, ALU.mult,
                        [r_xres[dc][tt], r_rstd, r_prm], [r_xn[dc][tt]])

        def ffn(l, which):
            w_in_ap = w_ffn_in[which][l]
            w_out_ap = w_ffn_out[which][l]
            rmsnorm_to_xn(PC_NORM + (l * 3 + (0 if which == 0 else 2)) * 8)
            groups = [(j0, min(j0 + GC, NJ)) for j0 in range(0, NJ, GC)]
            for gi, (j0, j1) in enumerate(groups):
                n = j1 - j0
                buf = gi % 2
                wg = wview(buf * 3 + 0, 8, 256)
                wu = wview(buf * 3 + 1, 8, 256)
                wo = wview(buf * 3 + 2, 2, 1024)
                rg, ru, ro = r_W[buf * 3 + 0], r_W[buf * 3 + 1], r_W[buf * 3 + 2]
                for dc in range(8):
                    load_w(wg[:, dc, :n * 128], w_in_ap[dc * 128:(dc + 1) * 128, j0 * 128:j1 * 128], n * 128, [rg])
                    load_w(wu[:, dc, :n * 128],
                           w_in_ap[dc * 128:(dc + 1) * 128, D_FF + j0 * 128:D_FF + j1 * 128], n * 128, [ru])
                for jj in range(n):
                    for hh in range(2):
                        load_w(wo[:, jj, hh * 512:(hh + 1) * 512],
                               w_out_ap[(j0 + jj) * 128:(j0 + jj + 1) * 128, hh * 512:(hh + 1) * 512], 512, [ro])
                actb = mrg[:, buf * 4096: buf * 4096 + 4096].rearrange("p (a b) -> p a b", a=2)
                for jj in range(n):
                    for tt in range(4):
                        tsl = slice(tt * 512, (tt + 1) * 512)
                        r_act = r_mrg[buf * 8 + jj * 4 + tt]
                        hg, r_hg = next_d()
                        hu, r_hu = next_d()
                        for dc in range(8):
                            MM(hg, wg[:, dc, jj * 128:(jj + 1) * 128], xn[:, dc, tsl], dc == 0, dc == 7,
                               [rg, r_xn[dc][tt]], [r_hg], inc=(dc == 7))
                        for dc in range(8):
                            MM(hu, wu[:, dc, jj * 128:(jj + 1) * 128], xn[:, dc, tsl], dc == 0, dc == 7,
                               [ru, r_xn[dc][tt]], [r_hu], inc=(dc == 7))
                        i = state["sg"]
                        state["sg"] = (i + 1) % 2
                        ACT(sg[i][:], hg, AF.Silu, [r_hg], [r_sg[i]])
                        TT("dve", actb[:, jj, tsl], sg[i][:], hu, ALU.mult, [r_sg[i], r_hu], [r_act])
                for c in range(8):
                    for tt in range(4):
                        tsl = slice(tt * 512, (tt + 1) * 512)
                        ob, r_ob = next_o()
                        for jj in range(n):
                            MM(ob[:], wo[:, jj, c * 128:(c + 1) * 128], actb[:, jj, tsl], jj == 0, jj == n - 1,
                               [ro, r_mrg[buf * 8 + jj * 4 + tt]], [r_ob], inc=(jj == n - 1))
                        STT("dve", xres[:, c, tsl], ob[:], 0.5, xres[:, c, tsl], ALU.mult, ALU.add,
                            [r_ob, r_xres[c][tt]], [r_xres[c][tt]])

        def proj_fm(wv_, rw, col0, dst, r_dst_fn, tok0, ntile, mode, gain_col=None, alt=0):
            for ti in range(ntile):
                t0 = tok0 + ti * 512
                tt = t0 // 512
                d, r_d = next_d()
                for dc in range(8):
                    MM(d, wv_[:, dc, col0:col0 + 128], xn[:, dc, t0:t0 + 512], dc == 0, dc == 7,
                       [r_xn[dc][tt]] + rw, [r_d], inc=(dc == 7))
                dsl = dst[:, ti * 512:(ti + 1) * 512]
                rd = r_dst_fn(ti)
                if mode == "copy":
                    if (ti + alt) % 2 == 0:
                        ACT(dsl, d, AF.Copy, [r_d], [rd])
                    else:
                        COPY("dve", dsl, d, [r_d], [rd])
                else:
                    i = state["sq"]
                    state["sq"] = (i + 1) % 2
                    ACT(sq[i][:], d, AF.Square, [r_d], [r_sq[i]])
                    d2, r_d2 = next_d()
                    MM(d2, bones_bf[:], sq[i][:], True, True, [r_sq[i], r_const], [r_d2], inc=True)
                    j = state["sg"]
                    state["sg"] = (j + 1) % 2
                    ACT(sg[j][:], d2, AF.Ln, [r_d2], [r_sg[j]], scale=1.0 / 64.0, bias=EPS)
                    ACT(sg[j][:], sg[j][:], AF.Exp, [r_sg[j]], [r_sg[j]], scale=-0.5)
                    STT("dve", dsl, d, pcol(gain_col), sg[j][:], ALU.mult, ALU.mult, [r_d, r_sg[j], r_prm], [rd])

        def proj_v(wv_, rw, col0, ncol, kb0, kb1):
            for g0 in range(kb0, kb1, 4):
                d, r_d = next_d()
                nb = min(4, kb1 - g0)
                for bi in range(nb):
                    kb = g0 + bi
                    for dc in range(8):
                        MM(d[:, bi * 128: bi * 128 + ncol], xn[:, dc, kb * 128:(kb + 1) * 128],
                           wv_[:, dc, col0:col0 + ncol], dc == 0, dc == 7,
                           [r_xn[dc][kb // 4]] + rw, [r_d], inc=(dc == 7 and bi == nb - 1))
                src = d.rearrange("p (a b) -> p a b", a=4)[:, :nb, :ncol]
                COPY("dve", vv[:, g0:g0 + nb, :ncol], src, [r_d], [r_vv[g0 // 4]])

        def attention(kind, l, hp, q0):
            items = []
            for qi in range(8):
                qb = q0 // 128 + qi
                kstart = qb * 128
                kend = SEQ if kind != "swa" else min(SEQ, kstart + 256)
                segs = []
                k0 = kstart
                while k0 < kend:
                    n = min(1024, kend - k0)
                    segs.append((k0, n))
                    k0 += n
                if kind == "diff":
                    subs = [0, 1]
                else:
                    subs = [0, 1]
                for sidx, sub in enumerate(subs):
                    for si, (k0, n) in enumerate(segs):
                        items.append(dict(qi=qi, qb=qb, sub=sub, si=si, k0=k0, n=n, nseg=len(segs),
                                          first_of_qb=(sidx == 0 and si == 0),
                                          last_of_qb=(sidx == len(subs) - 1 and si == len(segs) - 1)))
            ostate = {}

            def stage_A(it, ib):
                base = it["sub"] * 64
                qsl = qT[base:base + 64, it["qi"] * 128:(it["qi"] + 1) * 128]
                n, k0 = it["n"], it["k0"]
                for c0 in range(0, n, 512):
                    cn = min(512, n - c0)
                    MM(Zt[ib][:, c0:c0 + cn], qsl, kT[base:base + 64, k0 + c0:k0 + c0 + cn], True, True,
                       [r_qT[it["qi"] // 4]] + r_kT[(k0 + c0) // 512:(k0 + c0 + cn - 1) // 512 + 1], [r_Z[ib][c0 // 512]], inc=True)

            def scols(it):
                return 8 + it["sub"] * 4 + (it["qi"] % 4) * 8

            def stage_B1(it, ib):
                n, k0, si = it["n"], it["k0"], it["si"]
                rz = r_Z[ib][:(n + 511) // 512]
                Z = Zt[ib]
                if kind == "sb":
                    ACT(el[ib][:, :n], Z[:, :n], AF.Exp, rz, [r_el[ib]], scale=SCALE)
                    ACT(el[ib][:, :n], el[ib][:, :n], AF.Ln, [r_el[ib]], [r_el[ib]], bias=1.0)
                    if si == 0:
                        S.op("pool", lambda e: e.affine_select(out=el[ib][:, 0:128], in_=el[ib][:, 0:128],
                                                               pattern=[[1, 128]], compare_op=ALU.is_gt, fill=0.0,
                                                               base=0, channel_multiplier=-1),
                             [r_el[ib]], [r_el[ib]])
                        init = 0.0
                        rinit = []
                    else:
                        init = carry[it["sub"]][:, 0:1]
                        rinit = [r_carry[it["sub"]]]
                    S.op("dve", lambda e: e.tensor_tensor_scan(out=ca[ib][:, :n], data0=onesrow[:, :n],
                                                               data1=el[ib][:, :n], initial=init,
                                                               op0=ALU.mult, op1=ALU.add),
                         [r_el[ib], r_const] + rinit, [r_ca[ib]])
                    if si < it["nseg"] - 1:
                        COPY("dve", carry[it["sub"]][:, 0:1], ca[ib][:, n - 1:n], [r_ca[ib]], [r_carry[it["sub"]]])
                    STT("dve", el[ib][:, :n], Z[:, :n], SCALE, ca[ib][:, :n], ALU.mult, ALU.subtract,
                        rz + [r_ca[ib]], [r_el[ib]])
                else:
                    head = hp if kind == "diff" else 4 + hp * 2 + it["sub"]
                    bt = btl[:, head, :]
                    if si == 0:
                        nn = min(256, n)
                        STT("dve", el[ib][:, :nn], Z[:, :nn], SCALE, bt[:, :nn], ALU.mult, ALU.add,
                            rz[:1] + [r_btl], [r_el[ib]])
                        if n > nn:
                            TS("dve", el[ib][:, nn:n], Z[:, nn:n], SCALE, pcol(PC_CFAR + head), ALU.mult, ALU.add,
                               rz + [r_prm], [r_el[ib]])
                    else:
                        TS("dve", el[ib][:, :n], Z[:, :n], SCALE, pcol(PC_CFAR + head), ALU.mult, ALU.add,
                           rz + [r_prm], [r_el[ib]])

            def stage_B2(it, ib):
                n, si = it["n"], it["si"]
                if kind == "sb":
                    ACT(wb[ib][:, :n], el[ib][:, :n], AF.Exp, [r_el[ib]], [r_wb[ib]])
                    if si == 0:
                        S.op("pool", lambda e: e.affine_select(out=wb[ib][:, 0:128], in_=wb[ib][:, 0:128],
                                                               pattern=[[1, 128]], compare_op=ALU.is_gt, fill=0.0,
                                                               base=0, channel_multiplier=-1),
                             [r_wb[ib]], [r_wb[ib]])
                else:
                    scol0 = scols(it)
                    if si == 0:
                        nn = min(256, n)
                        S.op("act", lambda e: e.activation(out=wb[ib][:, :nn], in_=el[ib][:, :nn], func=AF.Exp,
                                                           accum_out=small[:, scol0:scol0 + 1]),
                             [r_el[ib]], [r_wb[ib], r_small[scol0]])
                        if n > nn:
                            S.op("act", lambda e: e.activation(out=wb[ib][:, nn:n], in_=el[ib][:, nn:n], func=AF.Exp,
                                                               accum_out=small[:, scol0 + 1:scol0 + 2]),
                                 [r_el[ib]], [r_wb[ib], r_small[scol0 + 1]])
                    else:
                        S.op("act", lambda e: e.activation(out=wb[ib][:, :n], in_=el[ib][:, :n], func=AF.Exp,
                                                           accum_out=small[:, scol0 + 1 + si:scol0 + 2 + si]),
                             [r_el[ib]], [r_wb[ib], r_small[scol0 + 1 + si]])

            def stage_T(it, ib):
                nb = it["n"] // 128
                for jb in range(nb):
                    TR(Tt[ib][:, jb * 128:(jb + 1) * 128], wb[ib][:, jb * 128:(jb + 1) * 128],
                       [r_wb[ib], r_const], [r_T[ib]], inc=(jb == nb - 1))

            def stage_CP(it, ib):
                n = it["n"]
                ACT(wT[ib][:, :n], Tt[ib][:, :n], AF.Copy, [r_T[ib]], [r_wT[ib]])

            def stage_PV(it, ib):
                n, k0 = it["n"], it["k0"]
                nb = n // 128
                qi = it["qi"]
                if it["first_of_qb"]:
                    if kind == "diff":
                        ostate["o"] = (0, 1)
                    else:
                        oi = state["o"]
                        state["o"] = (oi + 1) % 2
                        ostate["o"] = (oi, oi)
                if kind == "diff":
                    oi = ostate["o"][it["sub"]]
                    ocols = slice(0, 128)
                    vcols = slice(0, 128)
                elif kind == "sb":
                    oi = ostate["o"][0]
                    ocols = slice(it["sub"] * 64, it["sub"] * 64 + 64)
                    vcols = ocols
                else:
                    oi = ostate["o"][0]
                    ocols = slice(it["sub"] * 64, it["sub"] * 64 + 64)
                    g = hp // 2
                    vcols = slice(g * 64, g * 64 + 64)
                for jb in range(nb):
                    kb = k0 // 128 + jb
                    MM(Ot[oi][:, ocols], wT[ib][:, jb * 128:(jb + 1) * 128], vv[:, kb, vcols],
                       it["si"] == 0 and jb == 0, it["si"] == it["nseg"] - 1 and jb == nb - 1,
                       [r_wT[ib], r_vv[kb // 4]], [r_O[oi]], inc=(jb == nb - 1))
                if kind == "swa":
                    sc = scols(it)
                    head = hp * 2 + it["sub"]
                    fb = qi % 2
                    TT("dve", small[:, sc + 2:sc + 3], small[:, sc:sc + 1], esink[:, l * 8 + head:l * 8 + head + 1],
                       ALU.add, [r_small[sc], r_lamt], [r_small[sc + 2]])
                    S.op("dve", lambda e: e.reciprocal(out=small[:, sc + 2:sc + 3], in_=small[:, sc + 2:sc + 3]),
                         [r_small[sc + 2]], [r_small[sc + 2]])
                    ACT(osb[fb][:, ocols], Ot[oi][:, ocols], AF.Copy, [r_O[oi], r_small[sc + 2]], [r_osb[fb]],
                        scale=small[:, sc + 2:sc + 3])
                if it["last_of_qb"]:
                    stage_F(it, qi)

            def stage_F(it, qi):
                fb = qi % 2
                oi = ostate["o"][0]
                if kind == "sb":
                    ACT(osb[fb][:], Ot[oi][:, 0:128], AF.Copy, [r_O[oi]], [r_osb[fb]])
                elif kind == "diff":
                    nseg = it["nseg"]
                    for c in range(2):
                        sc = 8 + c * 4 + (qi % 4) * 8
                        tot = 40 + c
                        n0 = it_n0[qi]
                        cols = [sc]
                        if n0 > 256:
                            cols.append(sc + 1)
                        if nseg > 1:
                            cols.append(sc + 2)
                        if len(cols) == 1:
                            COPY("dve", small[:, tot:tot + 1], small[:, cols[0]:cols[0] + 1], [r_small[cols[0]]],
                                 [r_small[tot]])
                        else:
                            TT("dve", small[:, tot:tot + 1], small[:, cols[0]:cols[0] + 1],
                               small[:, cols[1]:cols[1] + 1], ALU.add, [r_small[cols[0]], r_small[cols[1]]],
                               [r_small[tot]])
                            if len(cols) == 3:
                                TT("dve", small[:, tot:tot + 1], small[:, tot:tot + 1],
                                   small[:, cols[2]:cols[2] + 1], ALU.add, [r_small[tot], r_small[cols[2]]],
                                   [r_small[tot]])
                        S.op("dve", lambda e, tot=tot: e.reciprocal(out=small[:, tot:tot + 1], in_=small[:, tot:tot + 1]),
                             [r_small[tot]], [r_small[tot]])
                    STT("dve", small[:, 42:43], small[:, 41:42], -1.0, lamt[:, l:l + 1], ALU.mult, ALU.mult,
                        [r_small[41], r_lamt], [r_small[42]])
                    ACT(of32[fb][:], Ot[0][:, 0:128], AF.Copy, [r_O[0], r_small[40]], [r_of32[fb]],
                        scale=small[:, 40:41])
                    STT("dve", of32[fb][:], Ot[1][:, 0:128], small[:, 42:43], of32[fb][:], ALU.mult, ALU.add,
                        [r_O[1], r_small[42], r_of32[fb]], [r_of32[fb]])
                    S.op("act", lambda e: e.activation(out=sqj[:], in_=of32[fb][:],
                                                       func=AF.Square, accum_out=small[:, 43:44]),
                         [r_of32[fb]], [r_tmp64, r_small[43]])
                    ACT(small[:, 43:44], small[:, 43:44], AF.Ln, [r_small[43]], [r_small[43]],
                        scale=1.0 / 128.0, bias=EPS)
                    ACT(small[:, 43:44], small[:, 43:44], AF.Exp, [r_small[43]], [r_small[43]], scale=-0.5)
                    STT("dve", osb[fb][:], of32[fb][:], small[:, 43:44], subg[:, l, :], ALU.mult, ALU.mult,
                        [r_of32[fb], r_small[43], r_lamt], [r_osb[fb]])
                tdst = Ot[oi][:, 256:320].bitcast(BF16)
                TR(tdst, osb[fb][:], [r_osb[fb], r_const], [r_O[oi]], inc=True)
                COPY("dve", oT[:, hp, qi * 128:(qi + 1) * 128], tdst, [r_O[oi]], [r_oT[hp][qi]])

            it_n0 = {}
            for it in items:
                if it["si"] == 0:
                    it_n0[it["qi"]] = it["n"]
            NI = len(items)
            for k in range(-2, NI + 3):
                if 0 <= k + 2 < NI:
                    stage_A(items[k + 2], (k + 2) % 2)
                if 0 <= k + 1 < NI:
                    stage_B1(items[k + 1], (k + 1) % 2)
                if 0 <= k < NI:
                    stage_B2(items[k], k % 2)
                if 0 <= k - 1 < NI:
                    stage_T(items[k - 1], (k - 1) % 2)
                if 0 <= k - 2 < NI:
                    stage_CP(items[k - 2], (k - 2) % 2)
                if 0 <= k - 3 < NI:
                    stage_PV(items[k - 3], (k - 3) % 2)

        def epilogue(bname, l, first, goff):
            wp_ap = w_proj_d[bname][l]
            for ch in range(2):
                buf = ch
                wp = wview(buf * 3 + 0, 4, 512)
                wgt = wview(buf * 3 + 1, 8, 512)
                rp = [r_W[buf * 3 + 0]]
                rgt = [r_W[buf * 3 + 1], r_W[buf * 3 + 2]]
                for k in range(4):
                    load_w(wp[:, k, :], wp_ap[k * 128:(k + 1) * 128, ch * 512:(ch + 1) * 512], 512, rp)
                for dc in range(8):
                    load_w(wgt[:, dc, :], w_in_d[l][dc * 128:(dc + 1) * 128, goff + ch * 512: goff + (ch + 1) * 512],
                           512, rgt)
                for cc in range(4):
                    c = ch * 4 + cc
                    for ti in range(2):
                        t0 = state["tok0"] + ti * 512
                        tt = t0 // 512
                        P, r_P = next_d()
                        G, r_G = next_d()
                        for k in range(4):
                            MM(P, wp[:, k, cc * 128:(cc + 1) * 128], oT[:, k, ti * 512:(ti + 1) * 512], k == 0, k == 3,
                               rp + r_oT[k][ti * 4:(ti + 1) * 4], [r_P], inc=(k == 3))
                        for dc in range(8):
                            MM(G, wgt[:, dc, cc * 128:(cc + 1) * 128], xn[:, dc, t0:t0 + 512], dc == 0, dc == 7,
                               rgt + [r_xn[dc][tt]], [r_G], inc=(dc == 7))
                        i = state["sg"]
                        state["sg"] = (i + 1) % 2
                        ACT(sg[i][:], G, AF.Sigmoid, [r_G], [r_sg[i]])
                        msl = mrg[:, (c * 2 + ti) * 512:(c * 2 + ti + 1) * 512]
                        rm = r_mrg[c * 2 + ti]
                        if first:
                            TT("dve", msl, sg[i][:], P, ALU.mult, [r_sg[i], r_P], [rm])
                        else:
                            TT("dve", sg[i][:], sg[i][:], P, ALU.mult, [r_sg[i], r_P], [r_sg[i]])
                            TT("pool", msl, msl, sg[i][:], ALU.add, [r_sg[i], rm], [rm])

        def mixer_out(l):
            for ch in range(2):
                wo = wview(ch * 3, 8, 512)
                ro = [r_W[ch * 3], r_W[ch * 3 + 1]]
                for c in range(8):
                    load_w(wo[:, c, :], w_out_d[l][c * 128:(c + 1) * 128, ch * 512:(ch + 1) * 512], 512, ro)
                for cc in range(4):
                    c2 = ch * 4 + cc
                    for ti in range(2):
                        t0 = state["tok0"] + ti * 512
                        tt = t0 // 512
                        ob, r_ob = next_o()
                        for c in range(8):
                            MM(ob[:], wo[:, c, cc * 128:(cc + 1) * 128], mrg[:, (c * 2 + ti) * 512:(c * 2 + ti + 1) * 512],
                               c == 0, c == 7, ro + [r_mrg[c * 2 + ti]], [r_ob], inc=(c == 7))
                        TT("dve", xres[:, c2, t0:t0 + 512], ob[:], xres[:, c2, t0:t0 + 512], ALU.add,
                           [r_ob, r_xres[c2][tt]], [r_xres[c2][tt]])

        def mixer(l):
            rmsnorm_to_xn(PC_NORM + (l * 3 + 1) * 8)
            for half in range(2):
                tok0 = half * 1024
                state["tok0"] = tok0
                nkt = (SEQ - tok0) // 512
                first = True
                for bname in branches:
                    if bname == "sb":
                        qoff, koff, voff, goff = OFF_QA, OFF_KA, OFF_VA, OFF_GA
                    elif bname == "diff":
                        qoff, koff, voff, goff = OFF_QD, OFF_KD, OFF_VD, OFF_GD
                    else:
                        qoff, koff, voff, goff = OFF_QS, OFF_KS, OFF_VS, OFF_GS
                    wq = wview(0, 8, 512)
                    wk = wview(2, 8, 512)
                    wv_ = wview(4, 8, 512)
                    rq, rk, rv = [r_W[0], r_W[1]], [r_W[2], r_W[3]], [r_W[4], r_W[5]]
                    for dc in range(8):
                        rows = slice(dc * 128, (dc + 1) * 128)
                        load_w(wq[:, dc, :], w_in_d[l][rows, qoff:qoff + 512], 512, rq)
                        if bname != "swa":
                            load_w(wk[:, dc, :], w_in_d[l][rows, koff:koff + 512], 512, rk)
                            load_w(wv_[:, dc, :], w_in_d[l][rows, voff:voff + 512], 512, rv)
                        else:
                            i = state["stg"]
                            state["stg"] = (i + 1) % NSTG
                            S.dma("sp", stg[i][:, :256], w_in_d[l][rows, koff:koff + 256], writes=[r_stg[i]],
                                  dst=r_stg[i])
                            for g in range(2):
                                for dup in range(2):
                                    COPY("pool", wk[:, dc, g * 128 + dup * 64: g * 128 + dup * 64 + 64],
                                         stg[i][:, g * 64:(g + 1) * 64], [r_stg[i]], rk)
                            COPY("pool", wv_[:, dc, 0:128], stg[i][:, 128:256], [r_stg[i]], rv)
                    for hp in range(4):
                        if bname == "sb":
                            proj_fm(wq, rq, hp * 128, qT, lambda ti: r_qT[ti], tok0, 2, "copy", alt=0)
                            proj_fm(wk, rk, hp * 128, kT[:, tok0:], lambda ti: r_kT[(tok0 // 512) + ti], tok0, nkt,
                                    "copy", alt=1)
                            proj_v(wv_, rv, hp * 128, 128, tok0 // 128, 16)
                        elif bname == "diff":
                            proj_fm(wq, rq, hp * 128, qT, lambda ti: r_qT[ti], tok0, 2, "qknorm",
                                    gain_col=PC_QK + l * 4 + 0)
                            proj_fm(wk, rk, hp * 128, kT[:, tok0:], lambda ti: r_kT[(tok0 // 512) + ti], tok0, nkt,
                                    "qknorm", gain_col=PC_QK + l * 4 + 1)
                            proj_v(wv_, rv, hp * 128, 128, tok0 // 128, 16)
                        else:
                            proj_fm(wq, rq, hp * 128, qT, lambda ti: r_qT[ti], tok0, 2, "qknorm",
                                    gain_col=PC_QK + l * 4 + 2)
                            if hp % 2 == 0:
                                g = hp // 2
                                proj_fm(wk, rk, g * 128, kT[:, tok0:], lambda ti: r_kT[(tok0 // 512) + ti], tok0, nkt,
                                        "qknorm", gain_col=PC_QK + l * 4 + 3)
                            if hp == 0:
                                proj_v(wv_, rv, 0, 128, tok0 // 128, 16)
                        attention(bname, l, hp, tok0)
                    if debug == "oT" and half == 1:
                        for c4 in range(4):
                            COPY("dve", dbgf[:], oT[:, c4, :], r_oT[c4], [r_dbg])
                            dbg_ids.append(S.dma("sp", dbg_d[:, c4 * 1024:(c4 + 1) * 1024], dbgf[:], reads=[r_dbg],
                                                 dst=r_dbg))
                    epilogue(bname, l, first, goff)
                    first = False
                mixer_out(l)

        out_ids = []
        r_out = [Res("yout%d" % i) for i in range(8)]
        for s in range(n_seq):
            for dc in range(8):
                S.dma("sp", xres[:, dc, :], x_d[s, dc], writes=r_xres[dc], dst=r_xres[dc][0])
                for t in range(1, 4):
                    r_xres[dc][t].w = r_xres[dc][0].w
            for l in layers:
                if "ffn1" in phases:
                    ffn(l, 0)
                if "mix" in phases:
                    mixer(l)
                if "ffn2" in phases:
                    ffn(l, 1)
            for dc in range(8):
                out_ids.append(S.dma("sp", y_d[s, dc], xres[:, dc, :], reads=r_xres[dc], dst=r_out[dc]))
        S.wait_all("sp", out_ids[-8:] + dbg_ids)
        S.emit_all()
        stats = dict(n_instr=S.n_instr, n_wait=S.n_wait, nsem=S.nsem)
    return nc, stats


def _t5_bucket_np(n):
    n = np.maximum(n, 0)
    max_exact = 16
    nf = np.maximum(n, 1).astype(np.float32)
    large = max_exact + (np.log(nf / np.float32(max_exact)) / np.float32(math.log(128 / max_exact))
                         * np.float32(32 - max_exact)).astype(np.int32)
    large = np.minimum(large, 31)
    return np.where(n < max_exact, n, large)


def _host_layout(inputs):
    f32 = np.float32
    prm = np.zeros((128, NP), f32)
    p = np.arange(128)
    norms = [inputs["ffn1_norm"], inputs["mix_norm"], inputs["ffn2_norm"]]
    for l in range(DEPTH):
        for which in range(3):
            g = np.asarray(norms[which][l], f32).reshape(8, 128)
            prm[:, PC_NORM + (l * 3 + which) * 8: PC_NORM + (l * 3 + which) * 8 + 8] = g.T
        for k, name in enumerate(["q_norm_diff", "k_norm_diff", "q_norm_swa", "k_norm_swa"]):
            prm[:, PC_QK + l * 4 + k] = np.asarray(inputs[name][l], f32)[p % 64]
        prm[:, PC_SINK + l * 8: PC_SINK + l * 8 + 8] = np.asarray(inputs["swa_sinks"][l], f32)[None, :]
        prm[:, PC_LAM + l * 256: PC_LAM + (l + 1) * 256] = np.asarray(inputs["diff_lambda"][l], f32).reshape(1, 256)
        prm[:, PC_SUBLN + l * 128: PC_SUBLN + (l + 1) * 128] = np.asarray(inputs["diff_subln"][l], f32)[None, :]
    rb = np.asarray(inputs["rel_bias"], f32)
    prm[:, PC_CFAR: PC_CFAR + 4] = rb[31, 0:4][None, :]
    i = np.arange(128)[:, None]
    j = np.arange(128)[None, :]
    d0 = j - i
    d1 = 128 + j - i
    b0 = _t5_bucket_np(d0)
    b1 = _t5_bucket_np(d1)
    bt = np.zeros((128, 12, 256), f32)
    for h in range(12):
        t0 = rb[b0, h]
        t0 = np.where(d0 >= 0, t0, f32(MASKV))
        t1 = rb[b1, h]
        if h >= 4:
            t1 = np.where(d1 < 128, t1, f32(MASKV))
        bt[:, h, 0:128] = t0
        bt[:, h, 128:256] = t1
    return prm, bt


_CACHE = {}


def kernel(**inputs):
    x = np.asarray(inputs["x"], np.float32)
    B = x.shape[0]
    prm, bt = _host_layout(inputs)
    if "nc" not in _CACHE:
        _CACHE["nc"] = build_program()[0]
    nc = _CACHE["nc"]
    shared = {k: np.ascontiguousarray(np.asarray(inputs[k], np.float32)) for k in
              ["ffn1_w_in", "ffn1_w_out", "ffn2_w_in", "ffn2_w_out", "w_in", "w_proj_sb", "w_proj_diff",
               "w_proj_swa", "w_out"]}
    shared["params"] = prm
    shared["btiles"] = bt
    in_maps = []
    for c in range(N_CORES):
        xs = x[c * SEQ_PER_CORE:(c + 1) * SEQ_PER_CORE]
        xs = xs[:, ::-1, :]
        xt = np.ascontiguousarray(xs.transpose(0, 2, 1)).reshape(SEQ_PER_CORE, 8, 128, SEQ)
        m = dict(shared)
        m["x"] = xt
        in_maps.append(m)
    res = run_bass_kernel_spmd(nc, in_maps, core_ids=list(range(N_CORES)))
    out = np.empty((B, SEQ, D_MODEL), np.float32)
    for c in range(N_CORES):
        y = np.asarray(res.results[c]["y"]).reshape(SEQ_PER_CORE, D_MODEL, SEQ)
        out[c * SEQ_PER_CORE:(c + 1) * SEQ_PER_CORE] = y.transpose(0, 2, 1)[:, ::-1, :]
    return out
```

```python
import contextlib
import math
import numpy as np
import concourse.bass as bass
import concourse.mybir as mybir
from concourse.bass_utils import run_bass_kernel_spmd

F32 = mybir.dt.float32
BF16 = mybir.dt.bfloat16
AF = mybir.ActivationFunctionType
ALU = mybir.AluOpType

D_MODEL = 1024
SEQ = 2048
DEPTH = 2
D_FF = 2816
NJ = D_FF // 128
IN_W = 6912
EPS = 1e-6
N_CORES = 8
SEQ_PER_CORE = 2
MASKV = -30000.0
SCALE = 0.125

OFF_QA, OFF_KA, OFF_VA = 0, 512, 1024
OFF_QD, OFF_KD, OFF_VD = 1536, 2048, 2560
OFF_QS, OFF_KS, OFF_VS = 3072, 3584, 3712
OFF_GA, OFF_GD, OFF_GS = 3840, 4864, 5888

PC_NORM = 0
PC_QK = 48
PC_SINK = 56
PC_CFAR = 72
PC_LAM = 76
PC_SUBLN = 588
NP = 844


class Res:
    __slots__ = ("name", "w", "r", "sem", "semval")

    def __init__(self, name):
        self.name = name
        self.w = None
        self.r = {}
        self.sem = None
        self.semval = 0


class Sched:
    ROT = 30000
    ENGS = ("pe", "act", "dve", "pool", "sp")

    def __init__(self, nc, stack):
        self.nc = nc
        self.stack = stack
        self.ops = {e: [] for e in self.ENGS}
        self.cnt = {e: 0 for e in self.ENGS}
        self.epoch = {e: 0 for e in self.ENGS}
        self.waited = {e: {} for e in self.ENGS}
        self.semh = {}
        self.nsem = 0
        self.pending_noinc = {e: False for e in self.ENGS}
        self.n_instr = 0
        self.n_wait = 0

    def _sem(self, key):
        h = self.semh.get(key)
        if h is None:
            h = self.stack.enter_context(self.nc.semaphore("s%d" % self.nsem))
            self.nsem += 1
            self.semh[key] = h
        return h

    def _next_id(self, eng):
        if self.cnt[eng] >= self.ROT and not self.pending_noinc[eng]:
            self.epoch[eng] += 1
            self.cnt[eng] = 0
        return (("e", eng, self.epoch[eng]), self.cnt[eng] + 1)

    def _collect(self, eng, reads, writes):
        deps = {}

        def add(d):
            if d is None:
                return
            k, v = d
            if eng == "pe" and k[0] == "e" and k[1] == "pe":
                return
            if deps.get(k, 0) < v:
                deps[k] = v
        for r in reads:
            add(r.w)
        for w in writes:
            add(w.w)
            for d in w.r.items():
                add(d)
        out = []
        wd = self.waited[eng]
        for k, v in deps.items():
            if wd.get(k, 0) >= v:
                continue
            wd[k] = v
            out.append((k, v))
        return out

    def op(self, eng, emit, reads=(), writes=(), inc=True):
        waits = self._collect(eng, reads, writes)
        myid = self._next_id(eng)
        if inc:
            self.cnt[eng] += 1
            self.pending_noinc[eng] = False
        else:
            self.pending_noinc[eng] = True
        for r in reads:
            if r.r.get(myid[0], 0) < myid[1]:
                r.r[myid[0]] = myid[1]
        for w in writes:
            w.w = myid
            w.r = {}
        self.ops[eng].append((waits, emit, myid[0] if inc else None, 1))
        self.n_instr += 1
        self.n_wait += len(waits)

    def dma(self, queue, out_ap, in_ap, reads=(), writes=(), dst=None):
        waits = self._collect(queue, reads, writes)
        if dst.sem is None:
            dst.sem = ("d", dst.name, id(dst))
        dst.semval += 16
        myid = (dst.sem, dst.semval)
        for r in reads:
            if r.r.get(myid[0], 0) < myid[1]:
                r.r[myid[0]] = myid[1]
        for w in writes:
            w.w = myid
            w.r = {}
        self.ops[queue].append((waits, (lambda e: e.dma_start(out=out_ap, in_=in_ap)), dst.sem, 16))
        self.n_instr += 1
        self.n_wait += len(waits)
        return myid

    def wait_all(self, eng, ids):
        self.ops[eng].append((list(ids), None, None, 0))

    def emit_all(self):
        nc = self.nc
        for e in self.ENGS:
            for waits, emit, inck, incv in self.ops[e]:
                for k, v in waits:
                    self._sem(k)
                if inck is not None:
                    self._sem(inck)
        ops = self.ops
        semh = self.semh

        def run(ename, e):
            for waits, emit, inck, incv in ops[ename]:
                for k, v in waits:
                    e.wait_ge(semh[k], v)
                if emit is not None:
                    ins = emit(e)
                    if inck is not None:
                        ins.then_inc(semh[inck], incv)

        with nc.Block() as block:
            if ops["pe"]:
                @block.tensor
                def _(e):
                    run("pe", e)
            if ops["act"]:
                @block.scalar
                def _(e):
                    run("act", e)
            if ops["dve"]:
                @block.vector
                def _(e):
                    run("dve", e)
            if ops["pool"]:
                @block.gpsimd
                def _(e):
                    run("pool", e)
            if ops["sp"]:
                @block.sync
                def _(e):
                    run("sp", e)


def build_program(n_seq=SEQ_PER_CORE, layers=(0, 1), phases=("ffn1", "mix", "ffn2"),
                  branches=("sb", "diff", "swa"), GC=2, debug=None):
    nc = bass.Bass("TRN2", target_bir_lowering=False, dynamic_dma_scratch_size=512)
    dram = {}

    def din(name, shape):
        dram[name] = nc.dram_tensor(name, list(shape), F32, kind="ExternalInput").ap()
        return dram[name]

    x_d = din("x", [n_seq, 8, 128, SEQ])
    w_ffn_in = [din("ffn1_w_in", [DEPTH, D_MODEL, 2 * D_FF]), din("ffn2_w_in", [DEPTH, D_MODEL, 2 * D_FF])]
    w_ffn_out = [din("ffn1_w_out", [DEPTH, D_FF, D_MODEL]), din("ffn2_w_out", [DEPTH, D_FF, D_MODEL])]
    w_in_d = din("w_in", [DEPTH, D_MODEL, IN_W])
    w_proj_d = {"sb": din("w_proj_sb", [DEPTH, 512, D_MODEL]),
                "diff": din("w_proj_diff", [DEPTH, 512, D_MODEL]),
                "swa": din("w_proj_swa", [DEPTH, 512, D_MODEL])}
    w_out_d = din("w_out", [DEPTH, D_MODEL, D_MODEL])
    params_d = din("params", [128, NP])
    btiles_d = din("btiles", [128, 12, 256])
    y_d = nc.dram_tensor("y", [n_seq, 8, 128, SEQ], F32, kind="ExternalOutput").ap()
    dbg_d = nc.dram_tensor("dbg", [128, 4096], F32, kind="ExternalOutput").ap() if debug else None

    with contextlib.ExitStack() as st:
        S = Sched(nc, st)

        def sb(name, shape, dt):
            return st.enter_context(nc.sbuf_tensor(name, list(shape), dt))

        def ps(name, shape, dt):
            return st.enter_context(nc.psum_tensor(name, list(shape), dt))

        xres = sb("xres", [128, 8, SEQ], F32)
        r_xres = [[Res("xres%d_%d" % (c, t)) for t in range(4)] for c in range(8)]
        xn = sb("xn", [128, 8, SEQ], BF16)
        r_xn = [[Res("xn%d_%d" % (c, t)) for t in range(4)] for c in range(8)]
        mrg = sb("mrg", [128, 8192], BF16)
        r_mrg = [Res("mrg%d" % i) for i in range(16)]
        oT = sb("oT", [128, 4, 1024], BF16)
        r_oT = [[Res("oT%d_%d" % (c, q)) for q in range(8)] for c in range(4)]
        Wall = sb("Wall", [128, 6 * 2048], BF16)
        r_W = [Res("W%d" % i) for i in range(6)]
        qT = sb("qT", [128, 1024], BF16)
        r_qT = [Res("qT%d" % i) for i in range(2)]
        kT = sb("kT", [128, SEQ], BF16)
        r_kT = [Res("kT%d" % i) for i in range(4)]
        vv = sb("vv", [128, 16, 128], BF16)
        r_vv = [Res("vv%d" % i) for i in range(4)]
        NSTG = 4
        stg = [sb("stg%d" % i, [128, 512], F32) for i in range(NSTG)]
        r_stg = [Res("stg%d" % i) for i in range(NSTG)]
        el = [sb("el%d" % i, [128, 1024], F32) for i in range(2)]
        r_el = [Res("el%d" % i) for i in range(2)]
        ca = [sb("ca%d" % i, [128, 1024], F32) for i in range(2)]
        r_ca = [Res("ca%d" % i) for i in range(2)]
        wb = [sb("wb%d" % i, [128, 1024], BF16) for i in range(2)]
        r_wb = [Res("wb%d" % i) for i in range(2)]
        wT = [sb("wT%d" % i, [128, 1024], BF16) for i in range(2)]
        r_wT = [Res("wT%d" % i) for i in range(2)]
        sq = [sb("sq%d" % i, [128, 512], BF16) for i in range(2)]
        r_sq = [Res("sq%d" % i) for i in range(2)]
        sg = [sb("sg%d" % i, [128, 512], F32) for i in range(2)]
        r_sg = [Res("sg%d" % i) for i in range(2)]
        rstd = sb("rstd", [128, 512], F32)
        r_rstd = Res("rstd")
        osb = [sb("osb%d" % i, [128, 128], BF16) for i in range(2)]
        r_osb = [Res("osb%d" % i) for i in range(2)]
        of32 = [sb("of32_%d" % i, [128, 128], F32) for i in range(2)]
        r_of32 = [Res("of32_%d" % i) for i in range(2)]
        btl = sb("btl", [128, 12, 256], F32)
        r_btl = Res("btl")
        prm = sb("prm", [128, NP], F32)
        r_prm = Res("prm")
        ones_bf = sb("ones_bf", [128, 128], BF16)
        bones_bf = sb("bones_bf", [128, 128], BF16)
        ident_bf = sb("ident_bf", [128, 128], BF16)
        onesrow = sb("onesrow", [128, 1024], BF16)
        r_const = Res("const")
        small = sb("small", [128, 64], F32)
        r_small = [Res("small%d" % i) for i in range(64)]
        lamt = sb("lamt", [128, 8], F32)
        r_lamt = Res("lamt")
        esink = sb("esink", [128, 16], F32)
        subg = sb("subg", [128, 2, 128], F32)
        tmp64 = sb("tmp64", [128, 64], F32)
        sqj = sb("sqj", [128, 128], F32)
        r_tmp64 = Res("tmp64")
        carry = [sb("carry%d" % i, [128, 1], F32) for i in range(2)]
        r_carry = [Res("carry%d" % i) for i in range(2)]

        dbgf = sb("dbgf", [128, 1024], F32) if debug else None
        r_dbg = Res("dbg")
        dbg_ids = []
        ZA = ps("ZA", [128, 1024], F32)
        ZB = ps("ZB", [128, 1024], F32)
        r_Z = [[Res("ZA0"), Res("ZA1")], [Res("ZB0"), Res("ZB1")]]
        Zt = [ZA, ZB]
        Dbanks = [(ZA, 0, r_Z[0][0]), (ZA, 512, r_Z[0][1]), (ZB, 0, r_Z[1][0]), (ZB, 512, r_Z[1][1])]
        Tt = [ps("T0", [128, 1024], BF16), ps("T1", [128, 1024], BF16)]
        r_T = [Res("T0"), Res("T1")]
        Ot = [ps("O0", [128, 512], F32), ps("O1", [128, 512], F32)]
        r_O = [Res("O0"), Res("O1")]

        state = {"d": 0, "o": 0, "stg": 0, "sq": 0, "sg": 0}

        def next_d():
            i = state["d"]
            state["d"] = (i + 1) % 4
            t, off, r = Dbanks[i]
            return t[:, off:off + 512], r

        def next_o():
            i = state["o"]
            state["o"] = (i + 1) % 2
            return Ot[i], r_O[i]

        def ACT(out, in_, func, reads, writes, **kw):
            S.op("act", lambda e: e.activation(out=out, in_=in_, func=func, **kw), reads, writes)

        def MM(out, lhsT, rhs, start, stop, reads, writes, inc):
            S.op("pe", lambda e: e.matmul(out, lhsT=lhsT, rhs=rhs, start=start, stop=stop),
                 reads, writes, inc=inc)

        def TR(out, in_, reads, writes, inc=True):
            S.op("pe", lambda e: e.transpose(out, in_, ident_bf[:]), reads, writes, inc=inc)

        def STT(eng, out, in0, scalar, in1, op0, op1, reads, writes):
            S.op(eng, lambda e: e.scalar_tensor_tensor(out=out, in0=in0, scalar=scalar, in1=in1, op0=op0, op1=op1),
                 reads, writes)

        def TT(eng, out, in0, in1, op, reads, writes):
            S.op(eng, lambda e: e.tensor_tensor(out=out, in0=in0, in1=in1, op=op), reads, writes)

        def TS(eng, out, in0, s1, s2, op0, op1, reads, writes):
            S.op(eng, lambda e: e.tensor_scalar(out=out, in0=in0, scalar1=s1, scalar2=s2, op0=op0, op1=op1),
                 reads, writes)

        def COPY(eng, out, in_, reads, writes):
            S.op(eng, lambda e: e.tensor_copy(out=out, in_=in_), reads, writes)

        def load_w(dst_ap, src_ap, n, dst_res, extra_dst=None):
            i = state["stg"]
            state["stg"] = (i + 1) % NSTG
            S.dma("sp", stg[i][:, :n], src_ap, writes=[r_stg[i]], dst=r_stg[i])
            COPY("pool", dst_ap, stg[i][:, :n], [r_stg[i]], dst_res)
            if extra_dst is not None:
                COPY("pool", extra_dst, stg[i][:, :n], [r_stg[i]], dst_res)

        def wview(r0, a, b):
            return Wall[:, r0 * 2048: r0 * 2048 + a * b].rearrange("p (a b) -> p a b", a=a)

        def pcol(c, n=1):
            return prm[:, c:c + n]

        S.dma("sp", prm[:], params_d, writes=[r_prm], dst=r_prm)
        S.dma("sp", btl[:], btiles_d, writes=[r_btl], dst=r_btl)
        S.op("pool", lambda e: e.memset(ones_bf[:], 1.0), [], [r_const])
        S.op("pool", lambda e: e.memset(onesrow[:], 1.0), [], [r_const])
        S.op("pool", lambda e: e.affine_select(out=ident_bf[:], in_=ones_bf[:], pattern=[[1, 128]],
                                               compare_op=ALU.is_equal, fill=0.0, base=0,
                                               channel_multiplier=-1), [r_const], [r_const])
        S.op("pool", lambda e: e.memset(bones_bf[:], 0.0), [], [r_const])
        S.op("pool", lambda e: e.memset(bones_bf[0:64, 0:64], 1.0), [], [r_const])
        S.op("pool", lambda e: e.memset(bones_bf[64:128, 64:128], 1.0), [], [r_const])
        for l in range(DEPTH):
            lam_init = 0.8 - 0.6 * math.exp(-0.3 * l)
            base = PC_LAM + l * 256
            for k in range(2):
                TT("dve", tmp64[:], pcol(base + 128 * k, 64), pcol(base + 128 * k + 64, 64), ALU.mult,
                   [r_prm], [r_tmp64])
                S.op("dve", lambda e, k=k, l=l: e.reduce_sum(out=small[:, 2 * l + k:2 * l + k + 1], in_=tmp64[:],
                                                             axis=mybir.AxisListType.X),
                     [r_tmp64], [r_small[2 * l + k]])
                ACT(small[:, 2 * l + k:2 * l + k + 1], small[:, 2 * l + k:2 * l + k + 1], AF.Exp,
                    [r_small[2 * l + k]], [r_small[2 * l + k]])
            STT("dve", lamt[:, l:l + 1], small[:, 2 * l:2 * l + 1], float(lam_init), small[:, 2 * l + 1:2 * l + 2],
                ALU.add, ALU.subtract, [r_small[2 * l], r_small[2 * l + 1]], [r_lamt])
            TS("dve", subg[:, l, :], pcol(PC_SUBLN + l * 128, 128), float(1.0 - lam_init), None, ALU.mult, ALU.bypass,
               [r_prm], [r_lamt])
        ACT(esink[:], pcol(PC_SINK, 16), AF.Exp, [r_prm], [r_lamt])

        def rmsnorm_to_xn(gcol):
            for tt in range(4):
                tsl = slice(tt * 512, (tt + 1) * 512)
                ob, r_ob = next_o()
                for dc in range(8):
                    i = state["sq"]
                    state["sq"] = (i + 1) % 2
                    ACT(sq[i][:], xres[:, dc, tsl], AF.Square, [r_xres[dc][tt]], [r_sq[i]])
                    MM(ob[:], ones_bf[:], sq[i][:], dc == 0, dc == 7, [r_sq[i], r_const], [r_ob], inc=True)
                ACT(rstd[:], ob[:], AF.Ln, [r_ob], [r_rstd], scale=1.0 / D_MODEL, bias=EPS)
                ACT(rstd[:], rstd[:], AF.Exp, [r_rstd], [r_rstd], scale=-0.5)
                for dc in range(8):
                    STT("dve", xn[:, dc, tsl], xres[:, dc, tsl], pcol(gcol + dc), rstd[:], ALU.mult, ALU.mult,
                        [r_xres[dc][tt], r_rstd, r_prm], [r_xn[dc][tt]])

        def ffn(l, which):
            w_in_ap = w_ffn_in[which][l]
            w_out_ap = w_ffn_out[which][l]
            rmsnorm_to_xn(PC_NORM + (l * 3 + (0 if which == 0 else 2)) * 8)
            groups = [(j0, min(j0 + GC, NJ)) for j0 in range(0, NJ, GC)]
            ob4 = [(Ot[0][:], r_O[0]), (Ot[1][:], r_O[1]),
                   (Tt[0][:].bitcast(F32), r_T[0]), (Tt[1][:].bitcast(F32), r_T[1])]
            ost = {"i": 0}

            def views(gi):
                buf = gi % 2
                return (wview(buf * 3 + 0, 8, 256), wview(buf * 3 + 1, 8, 256), wview(buf * 3 + 2, 2, 1024),
                        r_W[buf * 3 + 0], r_W[buf * 3 + 1], r_W[buf * 3 + 2],
                        mrg[:, buf * 4096: buf * 4096 + 4096].rearrange("p (a b) -> p a b", a=2), buf)

            def load_group(gi):
                j0, j1 = groups[gi]
                n = j1 - j0
                wg, wu, wo, rg, ru, ro, actb, buf = views(gi)
                for dc in range(8):
                    load_w(wg[:, dc, :n * 128], w_in_ap[dc * 128:(dc + 1) * 128, j0 * 128:j1 * 128], n * 128, [rg])
                    load_w(wu[:, dc, :n * 128],
                           w_in_ap[dc * 128:(dc + 1) * 128, D_FF + j0 * 128:D_FF + j1 * 128], n * 128, [ru])
                for jj in range(n):
                    for hh in range(2):
                        load_w(wo[:, jj, hh * 512:(hh + 1) * 512],
                               w_out_ap[(j0 + jj) * 128:(j0 + jj + 1) * 128, hh * 512:(hh + 1) * 512], 512, [ro])

            def win_unit(gi, jj, tt):
                wg, wu, wo, rg, ru, ro, actb, buf = views(gi)
                tsl = slice(tt * 512, (tt + 1) * 512)
                r_act = r_mrg[buf * 8 + jj * 4 + tt]
                hg, r_hg = next_d()
                hu, r_hu = next_d()
                for dc in range(8):
                    MM(hg, wg[:, dc, jj * 128:(jj + 1) * 128], xn[:, dc, tsl], dc == 0, dc == 7,
                       [rg, r_xn[dc][tt]], [r_hg], inc=(dc == 7))
                for dc in range(8):
                    MM(hu, wu[:, dc, jj * 128:(jj + 1) * 128], xn[:, dc, tsl], dc == 0, dc == 7,
                       [ru, r_xn[dc][tt]], [r_hu], inc=(dc == 7))
                i = state["sg"]
                state["sg"] = (i + 1) % 2
                ACT(sg[i][:], hg, AF.Silu, [r_hg], [r_sg[i]])
                TT("dve", actb[:, jj, tsl], sg[i][:], hu, ALU.mult, [r_sg[i], r_hu], [r_act])

            def wout_unit(gi, c, tt):
                j0, j1 = groups[gi]
                n = j1 - j0
                wg, wu, wo, rg, ru, ro, actb, buf = views(gi)
                tsl = slice(tt * 512, (tt + 1) * 512)
                ob, r_ob = ob4[ost["i"]]
                ost["i"] = (ost["i"] + 1) % 4
                for jj in range(n):
                    MM(ob, wo[:, jj, c * 128:(c + 1) * 128], actb[:, jj, tsl], jj == 0, jj == n - 1,
                       [ro, r_mrg[buf * 8 + jj * 4 + tt]], [r_ob], inc=(jj == n - 1))
                STT("dve", xres[:, c, tsl], ob, 0.5, xres[:, c, tsl], ALU.mult, ALU.add,
                    [r_ob, r_xres[c][tt]], [r_xres[c][tt]])

            pending = []
            for gi in range(len(groups)):
                j0, j1 = groups[gi]
                n = j1 - j0
                load_group(gi)
                wins = [(jj, tt) for jj in range(n) for tt in range(4)]
                per = (len(pending) + len(wins) - 1) // len(wins) if pending else 0
                for (jj, tt) in wins:
                    win_unit(gi, jj, tt)
                    for _ in range(per):
                        if pending:
                            g2, c, t2 = pending.pop(0)
                            wout_unit(g2, c, t2)
                while pending:
                    g2, c, t2 = pending.pop(0)
                    wout_unit(g2, c, t2)
                pending = [(gi, c, tt) for c in range(8) for tt in range(4)]
            while pending:
                g2, c, t2 = pending.pop(0)
                wout_unit(g2, c, t2)

        def proj_fm(wv_, rw, col0, dst, r_dst_fn, tok0, ntile, mode, gain_col=None, alt=0):
            for ti in range(ntile):
                t0 = tok0 + ti * 512
                tt = t0 // 512
                d, r_d = next_d()
                for dc in range(8):
                    MM(d, wv_[:, dc, col0:col0 + 128], xn[:, dc, t0:t0 + 512], dc == 0, dc == 7,
                       [r_xn[dc][tt]] + rw, [r_d], inc=(dc == 7))
                dsl = dst[:, ti * 512:(ti + 1) * 512]
                rd = r_dst_fn(ti)
                if mode == "copy":
                    if (ti + alt) % 2 == 0:
                        ACT(dsl, d, AF.Copy, [r_d], [rd])
                    else:
                        COPY("dve", dsl, d, [r_d], [rd])
                else:
                    i = state["sq"]
                    state["sq"] = (i + 1) % 2
                    ACT(sq[i][:], d, AF.Square, [r_d], [r_sq[i]])
                    d2, r_d2 = next_d()
                    MM(d2, bones_bf[:], sq[i][:], True, True, [r_sq[i], r_const], [r_d2], inc=True)
                    j = state["sg"]
                    state["sg"] = (j + 1) % 2
                    ACT(sg[j][:], d2, AF.Ln, [r_d2], [r_sg[j]], scale=1.0 / 64.0, bias=EPS)
                    ACT(sg[j][:], sg[j][:], AF.Exp, [r_sg[j]], [r_sg[j]], scale=-0.5)
                    STT("dve", dsl, d, pcol(gain_col), sg[j][:], ALU.mult, ALU.mult, [r_d, r_sg[j], r_prm], [rd])

        def proj_v(wv_, rw, col0, ncol, kb0, kb1):
            for g0 in range(kb0, kb1, 4):
                d, r_d = next_d()
                nb = min(4, kb1 - g0)
                for bi in range(nb):
                    kb = g0 + bi
                    for dc in range(8):
                        MM(d[:, bi * 128: bi * 128 + ncol], xn[:, dc, kb * 128:(kb + 1) * 128],
                           wv_[:, dc, col0:col0 + ncol], dc == 0, dc == 7,
                           [r_xn[dc][kb // 4]] + rw, [r_d], inc=(dc == 7 and bi == nb - 1))
                src = d.rearrange("p (a b) -> p a b", a=4)[:, :nb, :ncol]
                COPY("dve", vv[:, g0:g0 + nb, :ncol], src, [r_d], [r_vv[g0 // 4]])

        def attention(kind, l, hp, q0):
            items = []
            for qi in range(8):
                qb = q0 // 128 + qi
                kstart = qb * 128
                kend = SEQ if kind != "swa" else min(SEQ, kstart + 256)
                segs = []
                k0 = kstart
                while k0 < kend:
                    n = min(1024, kend - k0)
                    segs.append((k0, n))
                    k0 += n
                if kind == "diff":
                    subs = [0, 1]
                else:
                    subs = [0, 1]
                for sidx, sub in enumerate(subs):
                    for si, (k0, n) in enumerate(segs):
                        items.append(dict(qi=qi, qb=qb, sub=sub, si=si, k0=k0, n=n, nseg=len(segs),
                                          first_of_qb=(sidx == 0 and si == 0),
                                          last_of_qb=(sidx == len(subs) - 1 and si == len(segs) - 1)))
            ostate = {}

            def stage_A(it, ib):
                base = it["sub"] * 64
                qsl = qT[base:base + 64, it["qi"] * 128:(it["qi"] + 1) * 128]
                n, k0 = it["n"], it["k0"]
                for c0 in range(0, n, 512):
                    cn = min(512, n - c0)
                    MM(Zt[ib][:, c0:c0 + cn], qsl, kT[base:base + 64, k0 + c0:k0 + c0 + cn], True, True,
                       [r_qT[it["qi"] // 4]] + r_kT[(k0 + c0) // 512:(k0 + c0 + cn - 1) // 512 + 1], [r_Z[ib][c0 // 512]], inc=True)

            def scols(it):
                return 8 + it["sub"] * 4 + (it["qi"] % 4) * 8

            def stage_B1(it, ib):
                n, k0, si = it["n"], it["k0"], it["si"]
                rz = r_Z[ib][:(n + 511) // 512]
                Z = Zt[ib]
                if kind == "sb":
                    ACT(el[ib][:, :n], Z[:, :n], AF.Exp, rz, [r_el[ib]], scale=SCALE)
                    ACT(el[ib][:, :n], el[ib][:, :n], AF.Ln, [r_el[ib]], [r_el[ib]], bias=1.0)
                    if si == 0:
                        S.op("pool", lambda e: e.affine_select(out=el[ib][:, 0:128], in_=el[ib][:, 0:128],
                                                               pattern=[[1, 128]], compare_op=ALU.is_gt, fill=0.0,
                                                               base=0, channel_multiplier=-1),
                             [r_el[ib]], [r_el[ib]])
                        init = 0.0
                        rinit = []
                    else:
                        init = carry[it["sub"]][:, 0:1]
                        rinit = [r_carry[it["sub"]]]
                    S.op("dve", lambda e: e.tensor_tensor_scan(out=ca[ib][:, :n], data0=onesrow[:, :n],
                                                               data1=el[ib][:, :n], initial=init,
                                                               op0=ALU.mult, op1=ALU.add),
                         [r_el[ib], r_const] + rinit, [r_ca[ib]])
                    if si < it["nseg"] - 1:
                        COPY("dve", carry[it["sub"]][:, 0:1], ca[ib][:, n - 1:n], [r_ca[ib]], [r_carry[it["sub"]]])
                    STT("dve", el[ib][:, :n], Z[:, :n], SCALE, ca[ib][:, :n], ALU.mult, ALU.subtract,
                        rz + [r_ca[ib]], [r_el[ib]])
                else:
                    head = hp if kind == "diff" else 4 + hp * 2 + it["sub"]
                    bt = btl[:, head, :]
                    if si == 0:
                        nn = min(256, n)
                        STT("dve", el[ib][:, :nn], Z[:, :nn], SCALE, bt[:, :nn], ALU.mult, ALU.add,
                            rz[:1] + [r_btl], [r_el[ib]])
                        if n > nn:
                            TS("dve", el[ib][:, nn:n], Z[:, nn:n], SCALE, pcol(PC_CFAR + head), ALU.mult, ALU.add,
                               rz + [r_prm], [r_el[ib]])
                    else:
                        TS("dve", el[ib][:, :n], Z[:, :n], SCALE, pcol(PC_CFAR + head), ALU.mult, ALU.add,
                           rz + [r_prm], [r_el[ib]])

            def stage_B2(it, ib):
                n, si = it["n"], it["si"]
                if kind == "sb":
                    ACT(wb[ib][:, :n], el[ib][:, :n], AF.Exp, [r_el[ib]], [r_wb[ib]])
                    if si == 0:
                        S.op("pool", lambda e: e.affine_select(out=wb[ib][:, 0:128], in_=wb[ib][:, 0:128],
                                                               pattern=[[1, 128]], compare_op=ALU.is_gt, fill=0.0,
                                                               base=0, channel_multiplier=-1),
                             [r_wb[ib]], [r_wb[ib]])
                else:
                    scol0 = scols(it)
                    if si == 0:
                        nn = min(256, n)
                        S.op("act", lambda e: e.activation(out=wb[ib][:, :nn], in_=el[ib][:, :nn], func=AF.Exp,
                                                           accum_out=small[:, scol0:scol0 + 1]),
                             [r_el[ib]], [r_wb[ib], r_small[scol0]])
                        if n > nn:
                            S.op("act", lambda e: e.activation(out=wb[ib][:, nn:n], in_=el[ib][:, nn:n], func=AF.Exp,
                                                               accum_out=small[:, scol0 + 1:scol0 + 2]),
                                 [r_el[ib]], [r_wb[ib], r_small[scol0 + 1]])
                    else:
                        S.op("act", lambda e: e.activation(out=wb[ib][:, :n], in_=el[ib][:, :n], func=AF.Exp,
                                                           accum_out=small[:, scol0 + 1 + si:scol0 + 2 + si]),
                             [r_el[ib]], [r_wb[ib], r_small[scol0 + 1 + si]])

            def stage_T(it, ib):
                nb = it["n"] // 128
                for jb in range(nb):
                    TR(Tt[ib][:, jb * 128:(jb + 1) * 128], wb[ib][:, jb * 128:(jb + 1) * 128],
                       [r_wb[ib], r_const], [r_T[ib]], inc=(jb == nb - 1))

            def stage_CP(it, ib):
                n = it["n"]
                ACT(wT[ib][:, :n], Tt[ib][:, :n], AF.Copy, [r_T[ib]], [r_wT[ib]])

            def stage_PV(it, ib):
                n, k0 = it["n"], it["k0"]
                nb = n // 128
                qi = it["qi"]
                if it["first_of_qb"]:
                    if kind == "diff":
                        ostate["o"] = (0, 1)
                    else:
                        oi = state["o"]
                        state["o"] = (oi + 1) % 2
                        ostate["o"] = (oi, oi)
                if kind == "diff":
                    oi = ostate["o"][it["sub"]]
                    ocols = slice(0, 128)
                    vcols = slice(0, 128)
                elif kind == "sb":
                    oi = ostate["o"][0]
                    ocols = slice(it["sub"] * 64, it["sub"] * 64 + 64)
                    vcols = ocols
                else:
                    oi = ostate["o"][0]
                    ocols = slice(it["sub"] * 64, it["sub"] * 64 + 64)
                    g = hp // 2
                    vcols = slice(g * 64, g * 64 + 64)
                for jb in range(nb):
                    kb = k0 // 128 + jb
                    MM(Ot[oi][:, ocols], wT[ib][:, jb * 128:(jb + 1) * 128], vv[:, kb, vcols],
                       it["si"] == 0 and jb == 0, it["si"] == it["nseg"] - 1 and jb == nb - 1,
                       [r_wT[ib], r_vv[kb // 4]], [r_O[oi]], inc=(jb == nb - 1))
                if kind == "swa":
                    sc = scols(it)
                    head = hp * 2 + it["sub"]
                    fb = qi % 2
                    TT("dve", small[:, sc + 2:sc + 3], small[:, sc:sc + 1], esink[:, l * 8 + head:l * 8 + head + 1],
                       ALU.add, [r_small[sc], r_lamt], [r_small[sc + 2]])
                    S.op("dve", lambda e: e.reciprocal(out=small[:, sc + 2:sc + 3], in_=small[:, sc + 2:sc + 3]),
                         [r_small[sc + 2]], [r_small[sc + 2]])
                    ACT(osb[fb][:, ocols], Ot[oi][:, ocols], AF.Copy, [r_O[oi], r_small[sc + 2]], [r_osb[fb]],
                        scale=small[:, sc + 2:sc + 3])
                if it["last_of_qb"]:
                    stage_F(it, qi)

            def stage_F(it, qi):
                fb = qi % 2
                oi = ostate["o"][0]
                if kind == "sb":
                    ACT(osb[fb][:], Ot[oi][:, 0:128], AF.Copy, [r_O[oi]], [r_osb[fb]])
                elif kind == "diff":
                    nseg = it["nseg"]
                    for c in range(2):
                        sc = 8 + c * 4 + (qi % 4) * 8
                        tot = 40 + c
                        n0 = it_n0[qi]
                        cols = [sc]
                        if n0 > 256:
                            cols.append(sc + 1)
                        if nseg > 1:
                            cols.append(sc + 2)
                        if len(cols) == 1:
                            COPY("dve", small[:, tot:tot + 1], small[:, cols[0]:cols[0] + 1], [r_small[cols[0]]],
                                 [r_small[tot]])
                        else:
                            TT("dve", small[:, tot:tot + 1], small[:, cols[0]:cols[0] + 1],
                               small[:, cols[1]:cols[1] + 1], ALU.add, [r_small[cols[0]], r_small[cols[1]]],
                               [r_small[tot]])
                            if len(cols) == 3:
                                TT("dve", small[:, tot:tot + 1], small[:, tot:tot + 1],
                                   small[:, cols[2]:cols[2] + 1], ALU.add, [r_small[tot], r_small[cols[2]]],
                                   [r_small[tot]])
                        S.op("dve", lambda e, tot=tot: e.reciprocal(out=small[:, tot:tot + 1], in_=small[:, tot:tot + 1]),
                             [r_small[tot]], [r_small[tot]])
                    STT("dve", small[:, 42:43], small[:, 41:42], -1.0, lamt[:, l:l + 1], ALU.mult, ALU.mult,
                        [r_small[41], r_lamt], [r_small[42]])
                    ACT(of32[fb][:], Ot[0][:, 0:128], AF.Copy, [r_O[0], r_small[40]], [r_of32[fb]],
                        scale=small[:, 40:41])
                    STT("dve", of32[fb][:], Ot[1][:, 0:128], small[:, 42:43], of32[fb][:], ALU.mult, ALU.add,
                        [r_O[1], r_small[42], r_of32[fb]], [r_of32[fb]])
                    S.op("act", lambda e: e.activation(out=sqj[:], in_=of32[fb][:],
                                                       func=AF.Square, accum_out=small[:, 43:44]),
                         [r_of32[fb]], [r_tmp64, r_small[43]])
                    ACT(small[:, 43:44], small[:, 43:44], AF.Ln, [r_small[43]], [r_small[43]],
                        scale=1.0 / 128.0, bias=EPS)
                    ACT(small[:, 43:44], small[:, 43:44], AF.Exp, [r_small[43]], [r_small[43]], scale=-0.5)
                    STT("dve", osb[fb][:], of32[fb][:], small[:, 43:44], subg[:, l, :], ALU.mult, ALU.mult,
                        [r_of32[fb], r_small[43], r_lamt], [r_osb[fb]])
                tdst = Ot[oi][:, 256:320].bitcast(BF16)
                TR(tdst, osb[fb][:], [r_osb[fb], r_const], [r_O[oi]], inc=True)
                COPY("dve", oT[:, hp, qi * 128:(qi + 1) * 128], tdst, [r_O[oi]], [r_oT[hp][qi]])

            it_n0 = {}
            for it in items:
                if it["si"] == 0:
                    it_n0[it["qi"]] = it["n"]
            NI = len(items)
            for k in range(-2, NI + 3):
                if 0 <= k + 2 < NI:
                    stage_A(items[k + 2], (k + 2) % 2)
                if 0 <= k + 1 < NI:
                    stage_B1(items[k + 1], (k + 1) % 2)
                if 0 <= k < NI:
                    stage_B2(items[k], k % 2)
                if 0 <= k - 1 < NI:
                    stage_T(items[k - 1], (k - 1) % 2)
                if 0 <= k - 2 < NI:
                    stage_CP(items[k - 2], (k - 2) % 2)
                if 0 <= k - 3 < NI:
                    stage_PV(items[k - 3], (k - 3) % 2)

        def epilogue(bname, l, first, goff):
            wp_ap = w_proj_d[bname][l]
            for ch in range(2):
                buf = ch
                wp = wview(buf * 3 + 0, 4, 512)
                wgt = wview(buf * 3 + 1, 8, 512)
                rp = [r_W[buf * 3 + 0]]
                rgt = [r_W[buf * 3 + 1], r_W[buf * 3 + 2]]
                for k in range(4):
                    load_w(wp[:, k, :], wp_ap[k * 128:(k + 1) * 128, ch * 512:(ch + 1) * 512], 512, rp)
                for dc in range(8):
                    load_w(wgt[:, dc, :], w_in_d[l][dc * 128:(dc + 1) * 128, goff + ch * 512: goff + (ch + 1) * 512],
                           512, rgt)
                for cc in range(4):
                    c = ch * 4 + cc
                    for ti in range(2):
                        t0 = state["tok0"] + ti * 512
                        tt = t0 // 512
                        P, r_P = next_d()
                        G, r_G = next_d()
                        for k in range(4):
                            MM(P, wp[:, k, cc * 128:(cc + 1) * 128], oT[:, k, ti * 512:(ti + 1) * 512], k == 0, k == 3,
                               rp + r_oT[k][ti * 4:(ti + 1) * 4], [r_P], inc=(k == 3))
                        for dc in range(8):
                            MM(G, wgt[:, dc, cc * 128:(cc + 1) * 128], xn[:, dc, t0:t0 + 512], dc == 0, dc == 7,
                               rgt + [r_xn[dc][tt]], [r_G], inc=(dc == 7))
                        i = state["sg"]
                        state["sg"] = (i + 1) % 2
                        ACT(sg[i][:], G, AF.Sigmoid, [r_G], [r_sg[i]])
                        msl = mrg[:, (c * 2 + ti) * 512:(c * 2 + ti + 1) * 512]
                        rm = r_mrg[c * 2 + ti]
                        if first:
                            TT("dve", msl, sg[i][:], P, ALU.mult, [r_sg[i], r_P], [rm])
                        else:
                            TT("dve", sg[i][:], sg[i][:], P, ALU.mult, [r_sg[i], r_P], [r_sg[i]])
                            TT("pool", msl, msl, sg[i][:], ALU.add, [r_sg[i], rm], [rm])

        def mixer_out(l):
            for ch in range(2):
                wo = wview(ch * 3, 8, 512)
                ro = [r_W[ch * 3], r_W[ch * 3 + 1]]
                for c in range(8):
                    load_w(wo[:, c, :], w_out_d[l][c * 128:(c + 1) * 128, ch * 512:(ch + 1) * 512], 512, ro)
                for cc in range(4):
                    c2 = ch * 4 + cc
                    for ti in range(2):
                        t0 = state["tok0"] + ti * 512
                        tt = t0 // 512
                        ob, r_ob = next_o()
                        for c in range(8):
                            MM(ob[:], wo[:, c, cc * 128:(cc + 1) * 128], mrg[:, (c * 2 + ti) * 512:(c * 2 + ti + 1) * 512],
                               c == 0, c == 7, ro + [r_mrg[c * 2 + ti]], [r_ob], inc=(c == 7))
                        TT("dve", xres[:, c2, t0:t0 + 512], ob[:], xres[:, c2, t0:t0 + 512], ALU.add,
                           [r_ob, r_xres[c2][tt]], [r_xres[c2][tt]])

        def mixer(l):
            rmsnorm_to_xn(PC_NORM + (l * 3 + 1) * 8)
            for half in range(2):
                tok0 = half * 1024
                state["tok0"] = tok0
                nkt = (SEQ - tok0) // 512
                first = True
                for bname in branches:
                    if bname == "sb":
                        qoff, koff, voff, goff = OFF_QA, OFF_KA, OFF_VA, OFF_GA
                    elif bname == "diff":
                        qoff, koff, voff, goff = OFF_QD, OFF_KD, OFF_VD, OFF_GD
                    else:
                        qoff, koff, voff, goff = OFF_QS, OFF_KS, OFF_VS, OFF_GS
                    wq = wview(0, 8, 512)
                    wk = wview(2, 8, 512)
                    wv_ = wview(4, 8, 512)
                    rq, rk, rv = [r_W[0], r_W[1]], [r_W[2], r_W[3]], [r_W[4], r_W[5]]
                    for dc in range(8):
                        rows = slice(dc * 128, (dc + 1) * 128)
                        load_w(wq[:, dc, :], w_in_d[l][rows, qoff:qoff + 512], 512, rq)
                        if bname != "swa":
                            load_w(wk[:, dc, :], w_in_d[l][rows, koff:koff + 512], 512, rk)
                            load_w(wv_[:, dc, :], w_in_d[l][rows, voff:voff + 512], 512, rv)
                        else:
                            i = state["stg"]
                            state["stg"] = (i + 1) % NSTG
                            S.dma("sp", stg[i][:, :256], w_in_d[l][rows, koff:koff + 256], writes=[r_stg[i]],
                                  dst=r_stg[i])
                            for g in range(2):
                                for dup in range(2):
                                    COPY("pool", wk[:, dc, g * 128 + dup * 64: g * 128 + dup * 64 + 64],
                                         stg[i][:, g * 64:(g + 1) * 64], [r_stg[i]], rk)
                            COPY("pool", wv_[:, dc, 0:128], stg[i][:, 128:256], [r_stg[i]], rv)
                    for hp in range(4):
                        if bname == "sb":
                            proj_fm(wq, rq, hp * 128, qT, lambda ti: r_qT[ti], tok0, 2, "copy", alt=0)
                            proj_fm(wk, rk, hp * 128, kT[:, tok0:], lambda ti: r_kT[(tok0 // 512) + ti], tok0, nkt,
                                    "copy", alt=1)
                            proj_v(wv_, rv, hp * 128, 128, tok0 // 128, 16)
                        elif bname == "diff":
                            proj_fm(wq, rq, hp * 128, qT, lambda ti: r_qT[ti], tok0, 2, "qknorm",
                                    gain_col=PC_QK + l * 4 + 0)
                            proj_fm(wk, rk, hp * 128, kT[:, tok0:], lambda ti: r_kT[(tok0 // 512) + ti], tok0, nkt,
                                    "qknorm", gain_col=PC_QK + l * 4 + 1)
                            proj_v(wv_, rv, hp * 128, 128, tok0 // 128, 16)
                        else:
                            proj_fm(wq, rq, hp * 128, qT, lambda ti: r_qT[ti], tok0, 2, "qknorm",
                                    gain_col=PC_QK + l * 4 + 2)
                            if hp % 2 == 0:
                                g = hp // 2
                                proj_fm(wk, rk, g * 128, kT[:, tok0:], lambda ti: r_kT[(tok0 // 512) + ti], tok0, nkt,
                                        "qknorm", gain_col=PC_QK + l * 4 + 3)
                            if hp == 0:
                                proj_v(wv_, rv, 0, 128, tok0 // 128, 16)
                        attention(bname, l, hp, tok0)
                    if debug == "oT" and half == 1:
                        for c4 in range(4):
                            COPY("dve", dbgf[:], oT[:, c4, :], r_oT[c4], [r_dbg])
                            dbg_ids.append(S.dma("sp", dbg_d[:, c4 * 1024:(c4 + 1) * 1024], dbgf[:], reads=[r_dbg],
                                                 dst=r_dbg))
                    epilogue(bname, l, first, goff)
                    first = False
                mixer_out(l)

        out_ids = []
        r_out = [Res("yout%d" % i) for i in range(8)]
        for s in range(n_seq):
            for dc in range(8):
                S.dma("sp", xres[:, dc, :], x_d[s, dc], writes=r_xres[dc], dst=r_xres[dc][0])
                for t in range(1, 4):
                    r_xres[dc][t].w = r_xres[dc][0].w
            for l in layers:
                if "ffn1" in phases:
                    ffn(l, 0)
                if "mix" in phases:
                    mixer(l)
                if "ffn2" in phases:
                    ffn(l, 1)
            for dc in range(8):
                out_ids.append(S.dma("sp", y_d[s, dc], xres[:, dc, :], reads=r_xres[dc], dst=r_out[dc]))
        S.wait_all("sp", out_ids[-8:] + dbg_ids)
        S.emit_all()
        stats = dict(n_instr=S.n_instr, n_wait=S.n_wait, nsem=S.nsem)
    return nc, stats


def _t5_bucket_np(n):
    n = np.maximum(n, 0)
    max_exact = 16
    nf = np.maximum(n, 1).astype(np.float32)
    large = max_exact + (np.log(nf / np.float32(max_exact)) / np.float32(math.log(128 / max_exact))
                         * np.float32(32 - max_exact)).astype(np.int32)
    large = np.minimum(large, 31)
    return np.where(n < max_exact, n, large)


def _host_layout(inputs):
    f32 = np.float32
    prm = np.zeros((128, NP), f32)
    p = np.arange(128)
    norms = [inputs["ffn1_norm"], inputs["mix_norm"], inputs["ffn2_norm"]]
    for l in range(DEPTH):
        for which in range(3):
            g = np.asarray(norms[which][l], f32).reshape(8, 128)
            prm[:, PC_NORM + (l * 3 + which) * 8: PC_NORM + (l * 3 + which) * 8 + 8] = g.T
        for k, name in enumerate(["q_norm_diff", "k_norm_diff", "q_norm_swa", "k_norm_swa"]):
            prm[:, PC_QK + l * 4 + k] = np.asarray(inputs[name][l], f32)[p % 64]
        prm[:, PC_SINK + l * 8: PC_SINK + l * 8 + 8] = np.asarray(inputs["swa_sinks"][l], f32)[None, :]
        prm[:, PC_LAM + l * 256: PC_LAM + (l + 1) * 256] = np.asarray(inputs["diff_lambda"][l], f32).reshape(1, 256)
        prm[:, PC_SUBLN + l * 128: PC_SUBLN + (l + 1) * 128] = np.asarray(inputs["diff_subln"][l], f32)[None, :]
    rb = np.asarray(inputs["rel_bias"], f32)
    prm[:, PC_CFAR: PC_CFAR + 4] = rb[31, 0:4][None, :]
    i = np.arange(128)[:, None]
    j = np.arange(128)[None, :]
    d0 = j - i
    d1 = 128 + j - i
    b0 = _t5_bucket_np(d0)
    b1 = _t5_bucket_np(d1)
    bt = np.zeros((128, 12, 256), f32)
    for h in range(12):
        t0 = rb[b0, h]
        t0 = np.where(d0 >= 0, t0, f32(MASKV))
        t1 = rb[b1, h]
        if h >= 4:
            t1 = np.where(d1 < 128, t1, f32(MASKV))
        bt[:, h, 0:128] = t0
        bt[:, h, 128:256] = t1
    return prm, bt


_CACHE = {}


def kernel(**inputs):
    x = np.asarray(inputs["x"], np.float32)
    B = x.shape[0]
    prm, bt = _host_layout(inputs)
    if "nc" not in _CACHE:
        _CACHE["nc"] = build_program()[0]
    nc = _CACHE["nc"]
    shared = {k: np.ascontiguousarray(np.asarray(inputs[k], np.float32)) for k in
              ["ffn1_w_in", "ffn1_w_out", "ffn2_w_in", "ffn2_w_out", "w_in", "w_proj_sb", "w_proj_diff",
               "w_proj_swa", "w_out"]}
    shared["params"] = prm
    shared["btiles"] = bt
    in_maps = []
    for c in range(N_CORES):
        xs = x[c * SEQ_PER_CORE:(c + 1) * SEQ_PER_CORE]
        xs = xs[:, ::-1, :]
        xt = np.ascontiguousarray(xs.transpose(0, 2, 1)).reshape(SEQ_PER_CORE, 8, 128, SEQ)
        m = dict(shared)
        m["x"] = xt
        in_maps.append(m)
    res = run_bass_kernel_spmd(nc, in_maps, core_ids=list(range(N_CORES)))
    out = np.empty((B, SEQ, D_MODEL), np.float32)
    for c in range(N_CORES):
        y = np.asarray(res.results[c]["y"]).reshape(SEQ_PER_CORE, D_MODEL, SEQ)
        out[c * SEQ_PER_CORE:(c + 1) * SEQ_PER_CORE] = y.transpose(0, 2, 1)[:, ::-1, :]
    return out
```

```python
import contextlib
import math
import numpy as np
import concourse.bass as bass
import concourse.mybir as mybir
from concourse.bass_utils import run_bass_kernel_spmd

F32 = mybir.dt.float32
BF16 = mybir.dt.bfloat16
AF = mybir.ActivationFunctionType
ALU = mybir.AluOpType

D_MODEL = 1024
SEQ = 2048
DEPTH = 2
D_FF = 2816
NJ = D_FF // 128
IN_W = 6912
EPS = 1e-6
N_CORES = 8
SEQ_PER_CORE = 2
MASKV = -30000.0
SCALE = 0.125
SWA_BATCHED = False

OFF_QA, OFF_KA, OFF_VA = 0, 512, 1024
OFF_QD, OFF_KD, OFF_VD = 1536, 2048, 2560
OFF_QS, OFF_KS, OFF_VS = 3072, 3584, 3712
OFF_GA, OFF_GD, OFF_GS = 3840, 4864, 5888

PC_NORM = 0
PC_QK = 48
PC_SINK = 56
PC_CFAR = 72
PC_SUBLN = 76
PC_LAM = 332
NP = 844
NPR = 332


class Res:
    __slots__ = ("name", "w", "r", "sem", "semval")

    def __init__(self, name):
        self.name = name
        self.w = None
        self.r = {}
        self.sem = None
        self.semval = 0


class Sched:
    ROT = 30000
    ENGS = ("pe", "act", "dve", "pool", "sp")

    def __init__(self, nc, stack):
        self.nc = nc
        self.stack = stack
        self.ops = {e: [] for e in self.ENGS}
        self.cnt = {e: 0 for e in self.ENGS}
        self.epoch = {e: 0 for e in self.ENGS}
        self.waited = {e: {} for e in self.ENGS}
        self.semh = {}
        self.nsem = 0
        self.pending_noinc = {e: False for e in self.ENGS}
        self.n_instr = 0
        self.n_wait = 0

    def _sem(self, key):
        h = self.semh.get(key)
        if h is None:
            h = self.stack.enter_context(self.nc.semaphore("s%d" % self.nsem))
            self.nsem += 1
            self.semh[key] = h
        return h

    def _next_id(self, eng):
        if self.cnt[eng] >= self.ROT and not self.pending_noinc[eng]:
            self.epoch[eng] += 1
            self.cnt[eng] = 0
        return (("e", eng, self.epoch[eng]), self.cnt[eng] + 1)

    def _collect(self, eng, reads, writes):
        deps = {}

        def add(d):
            if d is None:
                return
            k, v = d
            if eng == "pe" and k[0] == "e" and k[1] == "pe":
                return
            if deps.get(k, 0) < v:
                deps[k] = v
        for r in reads:
            add(r.w)
        for w in writes:
            add(w.w)
            for d in w.r.items():
                add(d)
        out = []
        wd = self.waited[eng]
        for k, v in deps.items():
            if wd.get(k, 0) >= v:
                continue
            wd[k] = v
            out.append((k, v))
        return out

    def op(self, eng, emit, reads=(), writes=(), inc=True):
        waits = self._collect(eng, reads, writes)
        myid = self._next_id(eng)
        if inc:
            self.cnt[eng] += 1
            self.pending_noinc[eng] = False
        else:
            self.pending_noinc[eng] = True
        for r in reads:
            if r.r.get(myid[0], 0) < myid[1]:
                r.r[myid[0]] = myid[1]
        for w in writes:
            w.w = myid
            w.r = {}
        self.ops[eng].append((waits, emit, myid[0] if inc else None, 1))
        self.n_instr += 1
        self.n_wait += len(waits)

    def dma(self, queue, out_ap, in_ap, reads=(), writes=(), dst=None):
        waits = self._collect(queue, reads, writes)
        if dst.sem is None:
            dst.sem = ("d", dst.name, id(dst))
        dst.semval += 16
        myid = (dst.sem, dst.semval)
        for r in reads:
            if r.r.get(myid[0], 0) < myid[1]:
                r.r[myid[0]] = myid[1]
        for w in writes:
            w.w = myid
            w.r = {}
        self.ops[queue].append((waits, (lambda e: e.dma_start(out=out_ap, in_=in_ap)), dst.sem, 16))
        self.n_instr += 1
        self.n_wait += len(waits)
        return myid

    def wait_all(self, eng, ids):
        self.ops[eng].append((list(ids), None, None, 0))

    def emit_all(self):
        nc = self.nc
        for e in self.ENGS:
            for waits, emit, inck, incv in self.ops[e]:
                for k, v in waits:
                    self._sem(k)
                if inck is not None:
                    self._sem(inck)
        ops = self.ops
        semh = self.semh

        def run(ename, e):
            for waits, emit, inck, incv in ops[ename]:
                for k, v in waits:
                    e.wait_ge(semh[k], v)
                if emit is not None:
                    ins = emit(e)
                    if inck is not None:
                        ins.then_inc(semh[inck], incv)

        with nc.Block() as block:
            if ops["pe"]:
                @block.tensor
                def _(e):
                    run("pe", e)
            if ops["act"]:
                @block.scalar
                def _(e):
                    run("act", e)
            if ops["dve"]:
                @block.vector
                def _(e):
                    run("dve", e)
            if ops["pool"]:
                @block.gpsimd
                def _(e):
                    run("pool", e)
            if ops["sp"]:
                @block.sync
                def _(e):
                    run("sp", e)


def build_program(n_seq=SEQ_PER_CORE, layers=(0, 1), phases=("ffn1", "mix", "ffn2"),
                  branches=("sb", "diff", "swa"), GC=2, debug=None):
    nc = bass.Bass("TRN2", target_bir_lowering=False, dynamic_dma_scratch_size=512)
    dram = {}

    def din(name, shape):
        dram[name] = nc.dram_tensor(name, list(shape), F32, kind="ExternalInput").ap()
        return dram[name]

    x_d = din("x", [n_seq, 8, 128, SEQ])
    w_ffn_in = [din("ffn1_w_in", [DEPTH, D_MODEL, 2 * D_FF]), din("ffn2_w_in", [DEPTH, D_MODEL, 2 * D_FF])]
    w_ffn_out = [din("ffn1_w_out", [DEPTH, D_FF, D_MODEL]), din("ffn2_w_out", [DEPTH, D_FF, D_MODEL])]
    w_in_d = din("w_in", [DEPTH, D_MODEL, IN_W])
    w_proj_d = {"sb": din("w_proj_sb", [DEPTH, 512, D_MODEL]),
                "diff": din("w_proj_diff", [DEPTH, 512, D_MODEL]),
                "swa": din("w_proj_swa", [DEPTH, 512, D_MODEL])}
    w_out_d = din("w_out", [DEPTH, D_MODEL, D_MODEL])
    params_d = din("params", [128, NP])
    btiles_d = din("btiles", [128, 12, 256])
    y_d = nc.dram_tensor("y", [n_seq, 8, 128, SEQ], F32, kind="ExternalOutput").ap()
    dbg_d = nc.dram_tensor("dbg", [128, 4096], F32, kind="ExternalOutput").ap() if debug else None

    with contextlib.ExitStack() as st:
        S = Sched(nc, st)

        def sb(name, shape, dt):
            return st.enter_context(nc.sbuf_tensor(name, list(shape), dt))

        def ps(name, shape, dt):
            return st.enter_context(nc.psum_tensor(name, list(shape), dt))

        xres = sb("xres", [128, 8, SEQ], F32)
        r_xres = [[Res("xres%d_%d" % (c, t)) for t in range(4)] for c in range(8)]
        xn = sb("xn", [128, 8, SEQ], BF16)
        r_xn = [[Res("xn%d_%d" % (c, t)) for t in range(4)] for c in range(8)]
        mrg = sb("mrg", [128, 8192], BF16)
        r_mrg = [Res("mrg%d" % i) for i in range(16)]
        oT = sb("oT", [128, 4, 1024], BF16)
        r_oT = [[Res("oT%d_%d" % (c, q)) for q in range(8)] for c in range(4)]
        Wall = sb("Wall", [128, 6 * 2048], BF16)
        r_W = [Res("W%d" % i) for i in range(6)]
        qTb = [sb("qT%d" % b, [128, 1024], BF16) for b in range(2)]
        r_qTb = [[Res("qT%d_%d" % (b, i)) for i in range(2)] for b in range(2)]
        kTb = [sb("kT%d" % b, [128, SEQ], BF16) for b in range(2)]
        r_kTb = [[Res("kT%d_%d" % (b, i)) for i in range(4)] for b in range(2)]
        vvb = [sb("vv%d" % b, [128, 16, 128], BF16) for b in range(2)]
        r_vvb = [[Res("vv%d_%d" % (b, i)) for i in range(4)] for b in range(2)]
        NSTG = 3
        stg = [sb("stg%d" % i, [128, 512], F32) for i in range(NSTG)]
        r_stg = [Res("stg%d" % i) for i in range(NSTG)]
        el = [sb("el%d" % i, [128, 1024], F32) for i in range(2)]
        r_el = [Res("el%d" % i) for i in range(2)]
        ca = [sb("ca%d" % i, [128, 1024], F32) for i in range(2)]
        r_ca = [Res("ca%d" % i) for i in range(2)]
        wb = [sb("wb%d" % i, [128, 1024], BF16) for i in range(2)]
        r_wb = [Res("wb%d" % i) for i in range(2)]
        wT = [sb("wT%d" % i, [128, 1024], BF16) for i in range(2)]
        r_wT = [Res("wT%d" % i) for i in range(2)]
        sq = [sb("sq%d" % i, [128, 512], BF16) for i in range(2)]
        r_sq = [Res("sq%d" % i) for i in range(2)]
        sg = [sb("sg%d" % i, [128, 512], F32) for i in range(2)]
        r_sg = [Res("sg%d" % i) for i in range(2)]
        rstd = sb("rstd", [128, 512], F32)
        r_rstd = Res("rstd")
        osb = [sb("osb%d" % i, [128, 128], BF16) for i in range(2)]
        r_osb = [Res("osb%d" % i) for i in range(2)]
        osbw = [sb("osbw%d" % i, [128, 256], BF16) for i in range(2)]
        r_osbw = [Res("osbw%d" % i) for i in range(2)]
        of32 = [sb("of32_%d" % i, [128, 128], F32) for i in range(2)]
        r_of32 = [Res("of32_%d" % i) for i in range(2)]
        btl = sb("btl", [128, 12, 256], F32)
        r_btl = Res("btl")
        prm = sb("prm", [128, NPR], F32)
        r_prm = Res("prm")
        ones_bf = sb("ones_bf", [128, 128], BF16)
        bones_bf = sb("bones_bf", [128, 128], BF16)
        ident_bf = sb("ident_bf", [128, 128], BF16)
        onesrow = sb("onesrow", [128, 1024], BF16)
        r_const = Res("const")
        small = sb("small", [128, 64], F32)
        r_small = [Res("small%d" % i) for i in range(64)]
        lamt = sb("lamt", [128, 8], F32)
        r_lamt = Res("lamt")
        esink = sb("esink", [128, 16], F32)
        subg = sb("subg", [128, 2, 128], F32)
        tmp64 = sb("tmp64", [128, 64], F32)
        sqj = sb("sqj", [128, 128], F32)
        r_tmp64 = Res("tmp64")
        carry = [sb("carry%d" % i, [128, 1], F32) for i in range(2)]
        r_carry = [Res("carry%d" % i) for i in range(2)]

        dbgf = sb("dbgf", [128, 1024], F32) if debug else None
        r_dbg = Res("dbg")
        dbg_ids = []
        ZA = ps("ZA", [128, 1024], F32)
        ZB = ps("ZB", [128, 1024], F32)
        r_Z = [[Res("ZA0"), Res("ZA1")], [Res("ZB0"), Res("ZB1")]]
        Zt = [ZA, ZB]
        Dbanks = [(ZA, 0, r_Z[0][0]), (ZA, 512, r_Z[0][1]), (ZB, 0, r_Z[1][0]), (ZB, 512, r_Z[1][1])]
        Tt = [ps("T0", [128, 1024], BF16), ps("T1", [128, 1024], BF16)]
        r_T = [Res("T0"), Res("T1")]
        Ot = [ps("O0", [128, 512], F32), ps("O1", [128, 512], F32)]
        r_O = [Res("O0"), Res("O1")]

        state = {"d": 0, "o": 0, "stg": 0, "sq": 0, "sg": 0}

        def next_d():
            i = state["d"]
            state["d"] = (i + 1) % 4
            t, off, r = Dbanks[i]
            return t[:, off:off + 512], r

        def next_o():
            i = state["o"]
            state["o"] = (i + 1) % 2
            return Ot[i], r_O[i]

        def ACT(out, in_, func, reads, writes, **kw):
            S.op("act", lambda e: e.activation(out=out, in_=in_, func=func, **kw), reads, writes)

        def MM(out, lhsT, rhs, start, stop, reads, writes, inc):
            S.op("pe", lambda e: e.matmul(out, lhsT=lhsT, rhs=rhs, start=start, stop=stop),
                 reads, writes, inc=inc)

        def TR(out, in_, reads, writes, inc=True):
            S.op("pe", lambda e: e.transpose(out, in_, ident_bf[:]), reads, writes, inc=inc)

        def STT(eng, out, in0, scalar, in1, op0, op1, reads, writes):
            S.op(eng, lambda e: e.scalar_tensor_tensor(out=out, in0=in0, scalar=scalar, in1=in1, op0=op0, op1=op1),
                 reads, writes)

        def TT(eng, out, in0, in1, op, reads, writes):
            S.op(eng, lambda e: e.tensor_tensor(out=out, in0=in0, in1=in1, op=op), reads, writes)

        def TS(eng, out, in0, s1, s2, op0, op1, reads, writes):
            S.op(eng, lambda e: e.tensor_scalar(out=out, in0=in0, scalar1=s1, scalar2=s2, op0=op0, op1=op1),
                 reads, writes)

        def COPY(eng, out, in_, reads, writes):
            S.op(eng, lambda e: e.tensor_copy(out=out, in_=in_), reads, writes)

        def load_w(dst_ap, src_ap, n, dst_res, extra_dst=None):
            i = state["stg"]
            state["stg"] = (i + 1) % NSTG
            S.dma("sp", stg[i][:, :n], src_ap, writes=[r_stg[i]], dst=r_stg[i])
            COPY("pool", dst_ap, stg[i][:, :n], [r_stg[i]], dst_res)
            if extra_dst is not None:
                COPY("pool", extra_dst, stg[i][:, :n], [r_stg[i]], dst_res)

        def wview(r0, a, b):
            return Wall[:, r0 * 2048: r0 * 2048 + a * b].rearrange("p (a b) -> p a b", a=a)

        def pcol(c, n=1):
            return prm[:, c:c + n]

        S.dma("sp", prm[:], params_d[:, 0:NPR], writes=[r_prm], dst=r_prm)
        S.dma("sp", el[0][:, 0:512], params_d[:, PC_LAM:PC_LAM + 512], writes=[r_el[0]], dst=r_el[0])
        S.dma("sp", btl[:], btiles_d, writes=[r_btl], dst=r_btl)
        S.op("pool", lambda e: e.memset(ones_bf[:], 1.0), [], [r_const])
        S.op("pool", lambda e: e.memset(onesrow[:], 1.0), [], [r_const])
        S.op("pool", lambda e: e.affine_select(out=ident_bf[:], in_=ones_bf[:], pattern=[[1, 128]],
                                               compare_op=ALU.is_equal, fill=0.0, base=0,
                                               channel_multiplier=-1), [r_const], [r_const])
        S.op("pool", lambda e: e.memset(bones_bf[:], 0.0), [], [r_const])
        S.op("pool", lambda e: e.memset(bones_bf[0:64, 0:64], 1.0), [], [r_const])
        S.op("pool", lambda e: e.memset(bones_bf[64:128, 64:128], 1.0), [], [r_const])
        for l in range(DEPTH):
            lam_init = 0.8 - 0.6 * math.exp(-0.3 * l)
            base = l * 256
            for k in range(2):
                TT("dve", tmp64[:], el[0][:, base + 128 * k: base + 128 * k + 64],
                   el[0][:, base + 128 * k + 64: base + 128 * k + 128], ALU.mult,
                   [r_el[0]], [r_tmp64])
                S.op("dve", lambda e, k=k, l=l: e.reduce_sum(out=small[:, 2 * l + k:2 * l + k + 1], in_=tmp64[:],
                                                             axis=mybir.AxisListType.X),
                     [r_tmp64], [r_small[2 * l + k]])
                ACT(small[:, 2 * l + k:2 * l + k + 1], small[:, 2 * l + k:2 * l + k + 1], AF.Exp,
                    [r_small[2 * l + k]], [r_small[2 * l + k]])
            STT("dve", lamt[:, l:l + 1], small[:, 2 * l:2 * l + 1], float(lam_init), small[:, 2 * l + 1:2 * l + 2],
                ALU.add, ALU.subtract, [r_small[2 * l], r_small[2 * l + 1]], [r_lamt])
            TS("dve", subg[:, l, :], pcol(PC_SUBLN + l * 128, 128), float(1.0 - lam_init), None, ALU.mult, ALU.bypass,
               [r_prm], [r_lamt])
        ACT(esink[:], pcol(PC_SINK, 16), AF.Exp, [r_prm], [r_lamt])

        def rmsnorm_to_xn(gcol):
            for tt in range(4):
                tsl = slice(tt * 512, (tt + 1) * 512)
                ob, r_ob = next_o()
                for dc in range(8):
                    i = state["sq"]
                    state["sq"] = (i + 1) % 2
                    ACT(sq[i][:], xres[:, dc, tsl], AF.Square, [r_xres[dc][tt]], [r_sq[i]])
                    MM(ob[:], ones_bf[:], sq[i][:], dc == 0, dc == 7, [r_sq[i], r_const], [r_ob], inc=True)
                ACT(rstd[:], ob[:], AF.Ln, [r_ob], [r_rstd], scale=1.0 / D_MODEL, bias=EPS)
                ACT(rstd[:], rstd[:], AF.Exp, [r_rstd], [r_rstd], scale=-0.5)
                for dc in range(8):
                    STT("dve", xn[:, dc, tsl], xres[:, dc, tsl], pcol(gcol + dc), rstd[:], ALU.mult, ALU.mult,
                        [r_xres[dc][tt], r_rstd, r_prm], [r_xn[dc][tt]])

        def ffn(l, which):
            w_in_ap = w_ffn_in[which][l]
            w_out_ap = w_ffn_out[which][l]
            rmsnorm_to_xn(PC_NORM + (l * 3 + (0 if which == 0 else 2)) * 8)
            groups = [(j0, min(j0 + GC, NJ)) for j0 in range(0, NJ, GC)]
            ob4 = [(Ot[0][:], r_O[0]), (Ot[1][:], r_O[1]),
                   (Tt[0][:].bitcast(F32), r_T[0]), (Tt[1][:].bitcast(F32), r_T[1])]
            ost = {"i": 0}

            def views(gi):
                buf = gi % 2
                return (wview(buf * 3 + 0, 8, 256), wview(buf * 3 + 1, 8, 256), wview(buf * 3 + 2, 2, 1024),
                        r_W[buf * 3 + 0], r_W[buf * 3 + 1], r_W[buf * 3 + 2],
                        mrg[:, buf * 4096: buf * 4096 + 4096].rearrange("p (a b) -> p a b", a=2), buf)

            def load_group(gi):
                j0, j1 = groups[gi]
                n = j1 - j0
                wg, wu, wo, rg, ru, ro, actb, buf = views(gi)
                for dc in range(8):
                    load_w(wg[:, dc, :n * 128], w_in_ap[dc * 128:(dc + 1) * 128, j0 * 128:j1 * 128], n * 128, [rg])
                    load_w(wu[:, dc, :n * 128],
                           w_in_ap[dc * 128:(dc + 1) * 128, D_FF + j0 * 128:D_FF + j1 * 128], n * 128, [ru])
                for jj in range(n):
                    for hh in range(2):
                        load_w(wo[:, jj, hh * 512:(hh + 1) * 512],
                               w_out_ap[(j0 + jj) * 128:(j0 + jj + 1) * 128, hh * 512:(hh + 1) * 512], 512, [ro])

            def win_unit(gi, jj, tt):
                wg, wu, wo, rg, ru, ro, actb, buf = views(gi)
                tsl = slice(tt * 512, (tt + 1) * 512)
                r_act = r_mrg[buf * 8 + jj * 4 + tt]
                hg, r_hg = next_d()
                hu, r_hu = next_d()
                for dc in range(8):
                    MM(hg, wg[:, dc, jj * 128:(jj + 1) * 128], xn[:, dc, tsl], dc == 0, dc == 7,
                       [rg, r_xn[dc][tt]], [r_hg], inc=(dc == 7))
                for dc in range(8):
                    MM(hu, wu[:, dc, jj * 128:(jj + 1) * 128], xn[:, dc, tsl], dc == 0, dc == 7,
                       [ru, r_xn[dc][tt]], [r_hu], inc=(dc == 7))
                i = state["sg"]
                state["sg"] = (i + 1) % 2
                ACT(sg[i][:], hg, AF.Silu, [r_hg], [r_sg[i]])
                TT("dve", actb[:, jj, tsl], sg[i][:], hu, ALU.mult, [r_sg[i], r_hu], [r_act])

            def wout_unit(gi, c, tt):
                j0, j1 = groups[gi]
                n = j1 - j0
                wg, wu, wo, rg, ru, ro, actb, buf = views(gi)
                tsl = slice(tt * 512, (tt + 1) * 512)
                ob, r_ob = ob4[ost["i"]]
                ost["i"] = (ost["i"] + 1) % 4
                for jj in range(n):
                    MM(ob, wo[:, jj, c * 128:(c + 1) * 128], actb[:, jj, tsl], jj == 0, jj == n - 1,
                       [ro, r_mrg[buf * 8 + jj * 4 + tt]], [r_ob], inc=(jj == n - 1))
                STT("dve", xres[:, c, tsl], ob, 0.5, xres[:, c, tsl], ALU.mult, ALU.add,
                    [r_ob, r_xres[c][tt]], [r_xres[c][tt]])

            pending = []
            for gi in range(len(groups)):
                j0, j1 = groups[gi]
                n = j1 - j0
                load_group(gi)
                wins = [(jj, tt) for jj in range(n) for tt in range(4)]
                per = (len(pending) + len(wins) - 1) // len(wins) if pending else 0
                for (jj, tt) in wins:
                    win_unit(gi, jj, tt)
                    for _ in range(per):
                        if pending:
                            g2, c, t2 = pending.pop(0)
                            wout_unit(g2, c, t2)
                while pending:
                    g2, c, t2 = pending.pop(0)
                    wout_unit(g2, c, t2)
                pending = [(gi, c, tt) for c in range(8) for tt in range(4)]
            while pending:
                g2, c, t2 = pending.pop(0)
                wout_unit(g2, c, t2)

        def proj_fm_units(wv_, rw, col0, dst, r_dst_fn, tok0, ntile, mode, gain_col=None, alt=0):
            def unit(ti):
                t0 = tok0 + ti * 512
                tt = t0 // 512
                d, r_d = next_d()
                for dc in range(8):
                    MM(d, wv_[:, dc, col0:col0 + 128], xn[:, dc, t0:t0 + 512], dc == 0, dc == 7,
                       [r_xn[dc][tt]] + rw, [r_d], inc=(dc == 7))
                dsl = dst[:, ti * 512:(ti + 1) * 512]
                rd = r_dst_fn(ti)
                if mode == "copy":
                    if (ti + alt) % 2 == 0:
                        ACT(dsl, d, AF.Copy, [r_d], [rd])
                    else:
                        COPY("dve", dsl, d, [r_d], [rd])
                else:
                    i = state["sq"]
                    state["sq"] = (i + 1) % 2
                    ACT(sq[i][:], d, AF.Square, [r_d], [r_sq[i]])
                    d2, r_d2 = next_d()
                    MM(d2, bones_bf[:], sq[i][:], True, True, [r_sq[i], r_const], [r_d2], inc=True)
                    j = state["sg"]
                    state["sg"] = (j + 1) % 2
                    ACT(sg[j][:], d2, AF.Ln, [r_d2], [r_sg[j]], scale=1.0 / 64.0, bias=EPS)
                    ACT(sg[j][:], sg[j][:], AF.Exp, [r_sg[j]], [r_sg[j]], scale=-0.5)
                    STT("dve", dsl, d, pcol(gain_col), sg[j][:], ALU.mult, ALU.mult, [r_d, r_sg[j], r_prm], [rd])
            return [(lambda ti=ti: unit(ti)) for ti in range(ntile)]

        def proj_v_units(wv_, rw, col0, ncol, kb0, kb1, vv, r_vv):
            def unit(g0):
                d, r_d = next_d()
                nb = min(4, kb1 - g0)
                for bi in range(nb):
                    kb = g0 + bi
                    for dc in range(8):
                        MM(d[:, bi * 128: bi * 128 + ncol], xn[:, dc, kb * 128:(kb + 1) * 128],
                           wv_[:, dc, col0:col0 + ncol], dc == 0, dc == 7,
                           [r_xn[dc][kb // 4]] + rw, [r_d], inc=(dc == 7 and bi == nb - 1))
                src = d.rearrange("p (a b) -> p a b", a=4)[:, :nb, :ncol]
                COPY("dve", vv[:, g0:g0 + nb, :ncol], src, [r_d], [r_vv[g0 // 4]])
            return [(lambda g0=g0: unit(g0)) for g0 in range(kb0, kb1, 4)]

        pipe_items = []

        def pipe_run_step(k):
            for off, name in ((2, "A"), (1, "B1"), (0, "B2"), (-1, "T"), (-2, "CP"), (-3, "PV")):
                j = k + off
                if 0 <= j < len(pipe_items):
                    it, fns = pipe_items[j]
                    fns[name](it, j % 2)

        def pipe_push(it, fns):
            pipe_items.append((it, fns))
            pipe_run_step(len(pipe_items) - 3)

        def pipe_flush():
            g = len(pipe_items)
            for k in range(g - 2, g + 3):
                pipe_run_step(k)
            del pipe_items[:]

        def attention(kind, l, hp, q0, qT, r_qT, kT, r_kT, vv, r_vv):
            items = []
            for qi in range(8):
                qb = q0 // 128 + qi
                kstart = qb * 128
                kend = SEQ if kind != "swa" else min(SEQ, kstart + 256)
                segs = []
                k0 = kstart
                while k0 < kend:
                    n = min(1024, kend - k0)
                    segs.append((k0, n))
                    k0 += n
                if kind == "diff":
                    subs = [0, 1]
                else:
                    subs = [0, 1]
                for sidx, sub in enumerate(subs):
                    for si, (k0, n) in enumerate(segs):
                        items.append(dict(qi=qi, qb=qb, sub=sub, si=si, k0=k0, n=n, nseg=len(segs),
                                          first_of_qb=(sidx == 0 and si == 0),
                                          last_of_qb=(sidx == len(subs) - 1 and si == len(segs) - 1)))
            ostate = {}

            def stage_A(it, ib):
                base = it["sub"] * 64
                qsl = qT[base:base + 64, it["qi"] * 128:(it["qi"] + 1) * 128]
                n, k0 = it["n"], it["k0"]
                for c0 in range(0, n, 512):
                    cn = min(512, n - c0)
                    MM(Zt[ib][:, c0:c0 + cn], qsl, kT[base:base + 64, k0 + c0:k0 + c0 + cn], True, True,
                       [r_qT[it["qi"] // 4]] + r_kT[(k0 + c0) // 512:(k0 + c0 + cn - 1) // 512 + 1], [r_Z[ib][c0 // 512]], inc=True)

            def scols(it):
                return 8 + it["sub"] * 4 + (it["qi"] % 4) * 8

            def stage_B1(it, ib):
                n, k0, si = it["n"], it["k0"], it["si"]
                rz = r_Z[ib][:(n + 511) // 512]
                Z = Zt[ib]
                if kind == "sb":
                    ACT(el[ib][:, :n], Z[:, :n], AF.Exp, rz, [r_el[ib]], scale=SCALE)
                    ACT(el[ib][:, :n], el[ib][:, :n], AF.Ln, [r_el[ib]], [r_el[ib]], bias=1.0)
                    if si == 0:
                        S.op("pool", lambda e: e.affine_select(out=el[ib][:, 0:128], in_=el[ib][:, 0:128],
                                                               pattern=[[1, 128]], compare_op=ALU.is_gt, fill=0.0,
                                                               base=0, channel_multiplier=-1),
                             [r_el[ib]], [r_el[ib]])
                        init = 0.0
                        rinit = []
                    else:
                        init = carry[it["sub"]][:, 0:1]
                        rinit = [r_carry[it["sub"]]]
                    S.op("dve", lambda e: e.tensor_tensor_scan(out=ca[ib][:, :n], data0=onesrow[:, :n],
                                                               data1=el[ib][:, :n], initial=init,
                                                               op0=ALU.mult, op1=ALU.add),
                         [r_el[ib], r_const] + rinit, [r_ca[ib]])
                    if si < it["nseg"] - 1:
                        COPY("dve", carry[it["sub"]][:, 0:1], ca[ib][:, n - 1:n], [r_ca[ib]], [r_carry[it["sub"]]])
                    STT("dve", el[ib][:, :n], Z[:, :n], SCALE, ca[ib][:, :n], ALU.mult, ALU.subtract,
                        rz + [r_ca[ib]], [r_el[ib]])
                else:
                    head = hp if kind == "diff" else 4 + hp * 2 + it["sub"]
                    bt = btl[:, head, :]
                    if si == 0:
                        nn = min(256, n)
                        STT("dve", el[ib][:, :nn], Z[:, :nn], SCALE, bt[:, :nn], ALU.mult, ALU.add,
                            rz[:1] + [r_btl], [r_el[ib]])
                        if n > nn:
                            TS("dve", el[ib][:, nn:n], Z[:, nn:n], SCALE, pcol(PC_CFAR + head), ALU.mult, ALU.add,
                               rz + [r_prm], [r_el[ib]])
                    else:
                        TS("dve", el[ib][:, :n], Z[:, :n], SCALE, pcol(PC_CFAR + head), ALU.mult, ALU.add,
                           rz + [r_prm], [r_el[ib]])

            def stage_B2(it, ib):
                n, si = it["n"], it["si"]
                if kind == "sb":
                    ACT(wb[ib][:, :n], el[ib][:, :n], AF.Exp, [r_el[ib]], [r_wb[ib]])
                    if si == 0:
                        S.op("pool", lambda e: e.affine_select(out=wb[ib][:, 0:128], in_=wb[ib][:, 0:128],
                                                               pattern=[[1, 128]], compare_op=ALU.is_gt, fill=0.0,
                                                               base=0, channel_multiplier=-1),
                             [r_wb[ib]], [r_wb[ib]])
                else:
                    scol0 = scols(it)
                    if si == 0:
                        nn = min(256, n)
                        S.op("act", lambda e: e.activation(out=wb[ib][:, :nn], in_=el[ib][:, :nn], func=AF.Exp,
                                                           accum_out=small[:, scol0:scol0 + 1]),
                             [r_el[ib]], [r_wb[ib], r_small[scol0]])
                        if n > nn:
                            S.op("act", lambda e: e.activation(out=wb[ib][:, nn:n], in_=el[ib][:, nn:n], func=AF.Exp,
                                                               accum_out=small[:, scol0 + 1:scol0 + 2]),
                                 [r_el[ib]], [r_wb[ib], r_small[scol0 + 1]])
                    else:
                        S.op("act", lambda e: e.activation(out=wb[ib][:, :n], in_=el[ib][:, :n], func=AF.Exp,
                                                           accum_out=small[:, scol0 + 1 + si:scol0 + 2 + si]),
                             [r_el[ib]], [r_wb[ib], r_small[scol0 + 1 + si]])

            def stage_T(it, ib):
                nb = it["n"] // 128
                for jb in range(nb):
                    TR(Tt[ib][:, jb * 128:(jb + 1) * 128], wb[ib][:, jb * 128:(jb + 1) * 128],
                       [r_wb[ib], r_const], [r_T[ib]], inc=(jb == nb - 1))

            def stage_CP(it, ib):
                n = it["n"]
                ACT(wT[ib][:, :n], Tt[ib][:, :n], AF.Copy, [r_T[ib]], [r_wT[ib]])

            def stage_PV(it, ib):
                n, k0 = it["n"], it["k0"]
                nb = n // 128
                qi = it["qi"]
                if it["first_of_qb"]:
                    if kind == "diff":
                        ostate["o"] = (0, 1)
                    else:
                        oi = state["o"]
                        state["o"] = (oi + 1) % 2
                        ostate["o"] = (oi, oi)
                if kind == "diff":
                    oi = ostate["o"][it["sub"]]
                    ocols = slice(0, 128)
                    vcols = slice(0, 128)
                elif kind == "sb":
                    oi = ostate["o"][0]
                    ocols = slice(it["sub"] * 64, it["sub"] * 64 + 64)
                    vcols = ocols
                else:
                    oi = ostate["o"][0]
                    ocols = slice(it["sub"] * 64, it["sub"] * 64 + 64)
                    g = hp // 2
                    vcols = slice(g * 64, g * 64 + 64)
                for jb in range(nb):
                    kb = k0 // 128 + jb
                    MM(Ot[oi][:, ocols], wT[ib][:, jb * 128:(jb + 1) * 128], vv[:, kb, vcols],
                       it["si"] == 0 and jb == 0, it["si"] == it["nseg"] - 1 and jb == nb - 1,
                       [r_wT[ib], r_vv[kb // 4]], [r_O[oi]], inc=(jb == nb - 1))
                if kind == "swa":
                    sc = scols(it)
                    head = hp * 2 + it["sub"]
                    fb = qi % 2
                    TT("dve", small[:, sc + 2:sc + 3], small[:, sc:sc + 1], esink[:, l * 8 + head:l * 8 + head + 1],
                       ALU.add, [r_small[sc], r_lamt], [r_small[sc + 2]])
                    S.op("dve", lambda e: e.reciprocal(out=small[:, sc + 2:sc + 3], in_=small[:, sc + 2:sc + 3]),
                         [r_small[sc + 2]], [r_small[sc + 2]])
                    ACT(osb[fb][:, ocols], Ot[oi][:, ocols], AF.Copy, [r_O[oi], r_small[sc + 2]], [r_osb[fb]],
                        scale=small[:, sc + 2:sc + 3])
                if it["last_of_qb"]:
                    stage_F(it, qi)

            def stage_F(it, qi):
                fb = qi % 2
                oi = ostate["o"][0]
                if kind == "sb":
                    ACT(osb[fb][:], Ot[oi][:, 0:128], AF.Copy, [r_O[oi]], [r_osb[fb]])
                elif kind == "diff":
                    nseg = it["nseg"]
                    for c in range(2):
                        sc = 8 + c * 4 + (qi % 4) * 8
                        tot = 40 + c
                        n0 = it_n0[qi]
                        cols = [sc]
                        if n0 > 256:
                            cols.append(sc + 1)
                        if nseg > 1:
                            cols.append(sc + 2)
                        if len(cols) == 1:
                            COPY("dve", small[:, tot:tot + 1], small[:, cols[0]:cols[0] + 1], [r_small[cols[0]]],
                                 [r_small[tot]])
                        else:
                            TT("dve", small[:, tot:tot + 1], small[:, cols[0]:cols[0] + 1],
                               small[:, cols[1]:cols[1] + 1], ALU.add, [r_small[cols[0]], r_small[cols[1]]],
                               [r_small[tot]])
                            if len(cols) == 3:
                                TT("dve", small[:, tot:tot + 1], small[:, tot:tot + 1],
                                   small[:, cols[2]:cols[2] + 1], ALU.add, [r_small[tot], r_small[cols[2]]],
                                   [r_small[tot]])
                        S.op("dve", lambda e, tot=tot: e.reciprocal(out=small[:, tot:tot + 1], in_=small[:, tot:tot + 1]),
                             [r_small[tot]], [r_small[tot]])
                    STT("dve", small[:, 42:43], small[:, 41:42], -1.0, lamt[:, l:l + 1], ALU.mult, ALU.mult,
                        [r_small[41], r_lamt], [r_small[42]])
                    ACT(of32[fb][:], Ot[0][:, 0:128], AF.Copy, [r_O[0], r_small[40]], [r_of32[fb]],
                        scale=small[:, 40:41])
                    STT("dve", of32[fb][:], Ot[1][:, 0:128], small[:, 42:43], of32[fb][:], ALU.mult, ALU.add,
                        [r_O[1], r_small[42], r_of32[fb]], [r_of32[fb]])
                    S.op("act", lambda e: e.activation(out=sqj[:], in_=of32[fb][:],
                                                       func=AF.Square, accum_out=small[:, 43:44]),
                         [r_of32[fb]], [r_tmp64, r_small[43]])
                    ACT(small[:, 43:44], small[:, 43:44], AF.Ln, [r_small[43]], [r_small[43]],
                        scale=1.0 / 128.0, bias=EPS)
                    ACT(small[:, 43:44], small[:, 43:44], AF.Exp, [r_small[43]], [r_small[43]], scale=-0.5)
                    STT("dve", osb[fb][:], of32[fb][:], small[:, 43:44], subg[:, l, :], ALU.mult, ALU.mult,
                        [r_of32[fb], r_small[43], r_lamt], [r_osb[fb]])
                tdst = Ot[oi][:, 256:320].bitcast(BF16)
                TR(tdst, osb[fb][:], [r_osb[fb], r_const], [r_O[oi]], inc=True)
                COPY("dve", oT[:, hp, qi * 128:(qi + 1) * 128], tdst, [r_O[oi]], [r_oT[hp][qi]])

            it_n0 = {}
            for it in items:
                if it["si"] == 0:
                    it_n0[it["qi"]] = it["n"]
            fns = dict(A=stage_A, B1=stage_B1, B2=stage_B2, T=stage_T, CP=stage_CP, PV=stage_PV)
            return [(it, fns) for it in items]

        swa_ctr = {"i": 0}

        def attention_swa(l, hp, q0, qT, r_qT, kT, r_kT, vv, r_vv):
            g = hp // 2
            items = []
            for pair in range(4):
                combos = []
                for q in range(2):
                    qi = pair * 2 + q
                    qb = q0 // 128 + qi
                    w = 256 if qb < 15 else 128
                    for sub in range(2):
                        combos.append(dict(qi=qi, qb=qb, sub=sub, w=w, c=q * 2 + sub))
                items.append(dict(pair=pair, combos=combos, idx=swa_ctr["i"]))
                swa_ctr["i"] += 1
            octx = {}

            def sA(it, ib):
                for cb in it["combos"]:
                    base = cb["sub"] * 64
                    c, w, qi, k0 = cb["c"], cb["w"], cb["qi"], cb["qb"] * 128
                    MM(Zt[ib][:, c * 256:c * 256 + w], qT[base:base + 64, qi * 128:(qi + 1) * 128],
                       kT[base:base + 64, k0:k0 + w], True, True,
                       [r_qT[qi // 4]] + r_kT[k0 // 512:(k0 + w - 1) // 512 + 1], [r_Z[ib][c // 2]], inc=True)

            def sB1(it, ib):
                for cb in it["combos"]:
                    c, w = cb["c"], cb["w"]
                    head = 4 + hp * 2 + cb["sub"]
                    STT("dve", el[ib][:, c * 256:c * 256 + w], Zt[ib][:, c * 256:c * 256 + w], SCALE,
                        btl[:, head, 0:w], ALU.mult, ALU.add, [r_Z[ib][c // 2], r_btl], [r_el[ib]])

            def sB2(it, ib):
                sc = 8 + (it["idx"] % 4) * 8
                for cb in it["combos"]:
                    c, w = cb["c"], cb["w"]
                    S.op("act", lambda e, c=c, w=w: e.activation(out=wb[ib][:, c * 256:c * 256 + w],
                                                                 in_=el[ib][:, c * 256:c * 256 + w], func=AF.Exp,
                                                                 accum_out=small[:, sc + c:sc + c + 1]),
                         [r_el[ib]], [r_wb[ib], r_small[sc + c]])
                    if w < 256:
                        S.op("pool", lambda e, c=c, w=w: e.memset(wb[ib][:, c * 256 + w:(c + 1) * 256], 0.0),
                             [], [r_wb[ib]])

            def sT(it, ib):
                for jb in range(8):
                    TR(Tt[ib][:, jb * 128:(jb + 1) * 128], wb[ib][:, jb * 128:(jb + 1) * 128],
                       [r_wb[ib], r_const], [r_T[ib]], inc=(jb == 7))

            def sCP(it, ib):
                ACT(wT[ib][:, :], Tt[ib][:, :], AF.Copy, [r_T[ib]], [r_wT[ib]])

            def sPV(it, ib):
                oi = state["o"]
                state["o"] = (oi + 1) % 2
                sc = 8 + (it["idx"] % 4) * 8
                fb = it["idx"] % 2
                for cb in it["combos"]:
                    c, w, qb = cb["c"], cb["w"], cb["qb"]
                    nb = w // 128
                    for jb in range(nb):
                        kb = qb + jb
                        MM(Ot[oi][:, c * 64:(c + 1) * 64], wT[ib][:, c * 256 + jb * 128:c * 256 + (jb + 1) * 128],
                           vv[:, kb, g * 64:(g + 1) * 64], jb == 0, jb == nb - 1,
                           [r_wT[ib], r_vv[kb // 4]], [r_O[oi]], inc=(jb == nb - 1))
                es = esink[:, l * 8 + hp * 2:l * 8 + hp * 2 + 2]
                for q in range(2):
                    TT("dve", small[:, sc + 4 + q * 2:sc + 6 + q * 2], small[:, sc + q * 2:sc + q * 2 + 2], es, ALU.add,
                       [r_small[sc + q * 2], r_small[sc + q * 2 + 1], r_lamt],
                       [r_small[sc + 4 + q * 2], r_small[sc + 5 + q * 2]])
                S.op("dve", lambda e: e.reciprocal(out=small[:, sc + 4:sc + 8], in_=small[:, sc + 4:sc + 8]),
                     [r_small[sc + 4 + j] for j in range(4)], [r_small[sc + 4 + j] for j in range(4)])
                for c in range(4):
                    ACT(osbw[fb][:, c * 64:(c + 1) * 64], Ot[oi][:, c * 64:(c + 1) * 64], AF.Copy,
                        [r_O[oi], r_small[sc + 4 + c]], [r_osbw[fb]], scale=small[:, sc + 4 + c:sc + 5 + c])
                qi0 = it["pair"] * 2
                for q in range(2):
                    tdst = Ot[oi][:, 256 + q * 64:320 + q * 64].bitcast(BF16)
                    TR(tdst, osbw[fb][:, q * 128:(q + 1) * 128], [r_osbw[fb], r_const], [r_O[oi]], inc=True)
                    COPY("dve", oT[:, hp, (qi0 + q) * 128:(qi0 + q + 1) * 128], tdst, [r_O[oi]], [r_oT[hp][qi0 + q]])

            fns = dict(A=sA, B1=sB1, B2=sB2, T=sT, CP=sCP, PV=sPV)
            return [(it, fns) for it in items]

        def epilogue(bname, l, first, goff):
            wp_ap = w_proj_d[bname][l]
            for ch in range(2):
                buf = ch
                wp = wview(buf * 3 + 0, 4, 512)
                wgt = wview(buf * 3 + 1, 8, 512)
                rp = [r_W[buf * 3 + 0]]
                rgt = [r_W[buf * 3 + 1], r_W[buf * 3 + 2]]
                for k in range(4):
                    load_w(wp[:, k, :], wp_ap[k * 128:(k + 1) * 128, ch * 512:(ch + 1) * 512], 512, rp)
                for dc in range(8):
                    load_w(wgt[:, dc, :], w_in_d[l][dc * 128:(dc + 1) * 128, goff + ch * 512: goff + (ch + 1) * 512],
                           512, rgt)
                for cc in range(4):
                    c = ch * 4 + cc
                    for ti in range(2):
                        t0 = state["tok0"] + ti * 512
                        tt = t0 // 512
                        P, r_P = next_d()
                        G, r_G = next_d()
                        for k in range(4):
                            MM(P, wp[:, k, cc * 128:(cc + 1) * 128], oT[:, k, ti * 512:(ti + 1) * 512], k == 0, k == 3,
                               rp + r_oT[k][ti * 4:(ti + 1) * 4], [r_P], inc=(k == 3))
                        for dc in range(8):
                            MM(G, wgt[:, dc, cc * 128:(cc + 1) * 128], xn[:, dc, t0:t0 + 512], dc == 0, dc == 7,
                               rgt + [r_xn[dc][tt]], [r_G], inc=(dc == 7))
                        i = state["sg"]
                        state["sg"] = (i + 1) % 2
                        ACT(sg[i][:], G, AF.Sigmoid, [r_G], [r_sg[i]])
                        msl = mrg[:, (c * 2 + ti) * 512:(c * 2 + ti + 1) * 512]
                        rm = r_mrg[c * 2 + ti]
                        if first:
                            TT("dve", msl, sg[i][:], P, ALU.mult, [r_sg[i], r_P], [rm])
                        else:
                            TT("dve", sg[i][:], sg[i][:], P, ALU.mult, [r_sg[i], r_P], [r_sg[i]])
                            TT("pool", msl, msl, sg[i][:], ALU.add, [r_sg[i], rm], [rm])

        def mixer_out(l):
            for ch in range(2):
                wo = wview(ch * 3, 8, 512)
                ro = [r_W[ch * 3], r_W[ch * 3 + 1]]
                for c in range(8):
                    load_w(wo[:, c, :], w_out_d[l][c * 128:(c + 1) * 128, ch * 512:(ch + 1) * 512], 512, ro)
                for cc in range(4):
                    c2 = ch * 4 + cc
                    for ti in range(2):
                        t0 = state["tok0"] + ti * 512
                        tt = t0 // 512
                        ob, r_ob = next_o()
                        for c in range(8):
                            MM(ob[:], wo[:, c, cc * 128:(cc + 1) * 128], mrg[:, (c * 2 + ti) * 512:(c * 2 + ti + 1) * 512],
                               c == 0, c == 7, ro + [r_mrg[c * 2 + ti]], [r_ob], inc=(c == 7))
                        TT("dve", xres[:, c2, t0:t0 + 512], ob[:], xres[:, c2, t0:t0 + 512], ALU.add,
                           [r_ob, r_xres[c2][tt]], [r_xres[c2][tt]])

        def mixer(l):
            rmsnorm_to_xn(PC_NORM + (l * 3 + 1) * 8)
            for half in range(2):
                tok0 = half * 1024
                state["tok0"] = tok0
                nkt = (SEQ - tok0) // 512
                first = True
                for bname in branches:
                    if bname == "sb":
                        qoff, koff, voff, goff = OFF_QA, OFF_KA, OFF_VA, OFF_GA
                    elif bname == "diff":
                        qoff, koff, voff, goff = OFF_QD, OFF_KD, OFF_VD, OFF_GD
                    else:
                        qoff, koff, voff, goff = OFF_QS, OFF_KS, OFF_VS, OFF_GS
                    wq = wview(0, 8, 512)
                    wk = wview(2, 8, 512)
                    wv_ = wview(4, 8, 512)
                    rq, rk, rv = [r_W[0], r_W[1]], [r_W[2], r_W[3]], [r_W[4], r_W[5]]
                    for dc in range(8):
                        rows = slice(dc * 128, (dc + 1) * 128)
                        load_w(wq[:, dc, :], w_in_d[l][rows, qoff:qoff + 512], 512, rq)
                        if bname != "swa":
                            load_w(wk[:, dc, :], w_in_d[l][rows, koff:koff + 512], 512, rk)
                            load_w(wv_[:, dc, :], w_in_d[l][rows, voff:voff + 512], 512, rv)
                        else:
                            i = state["stg"]
                            state["stg"] = (i + 1) % NSTG
                            S.dma("sp", stg[i][:, :256], w_in_d[l][rows, koff:koff + 256], writes=[r_stg[i]],
                                  dst=r_stg[i])
                            for g in range(2):
                                for dup in range(2):
                                    COPY("pool", wk[:, dc, g * 128 + dup * 64: g * 128 + dup * 64 + 64],
                                         stg[i][:, g * 64:(g + 1) * 64], [r_stg[i]], rk)
                            COPY("pool", wv_[:, dc, 0:128], stg[i][:, 128:256], [r_stg[i]], rv)
                    def proj_units(hp):
                        b = hp % 2
                        qd_, rqd = qTb[b], r_qTb[b]
                        u = []
                        if bname == "sb" or bname == "diff":
                            kd_, rkd, vd_, rvd = kTb[b], r_kTb[b], vvb[b], r_vvb[b]
                            mode = "copy" if bname == "sb" else "qknorm"
                            gq = None if bname == "sb" else PC_QK + l * 4 + 0
                            gk = None if bname == "sb" else PC_QK + l * 4 + 1
                            u += proj_fm_units(wq, rq, hp * 128, qd_, lambda ti: rqd[ti], tok0, 2, mode,
                                               gain_col=gq, alt=0)
                            u += proj_fm_units(wk, rk, hp * 128, kd_[:, tok0:], lambda ti: rkd[(tok0 // 512) + ti],
                                               tok0, nkt, mode, gain_col=gk, alt=1)
                            u += proj_v_units(wv_, rv, hp * 128, 128, tok0 // 128, 16, vd_, rvd)
                        else:
                            u += proj_fm_units(wq, rq, hp * 128, qd_, lambda ti: rqd[ti], tok0, 2, "qknorm",
                                               gain_col=PC_QK + l * 4 + 2)
                            if hp % 2 == 0:
                                g = hp // 2
                                kd_, rkd = kTb[g % 2], r_kTb[g % 2]
                                u += proj_fm_units(wk, rk, g * 128, kd_[:, tok0:],
                                                   lambda ti: rkd[(tok0 // 512) + ti], tok0, nkt, "qknorm",
                                                   gain_col=PC_QK + l * 4 + 3)
                            if hp == 0:
                                u += proj_v_units(wv_, rv, 0, 128, tok0 // 128, 16, vvb[0], r_vvb[0])
                        return u

                    for hp2 in (0, 2):
                        for hp in (hp2, hp2 + 1):
                            for u in proj_units(hp):
                                u()
                        for hp in (hp2, hp2 + 1):
                            b = hp % 2
                            if bname == "swa":
                                kb_ = (hp // 2) % 2
                                swa_fn = attention_swa if SWA_BATCHED else (
                                    lambda l_, hp_, *a: attention("swa", l_, hp_, *a))
                                its = swa_fn(l, hp, tok0, qTb[b], r_qTb[b], kTb[kb_], r_kTb[kb_],
                                             vvb[0], r_vvb[0])
                            else:
                                its = attention(bname, l, hp, tok0, qTb[b], r_qTb[b], kTb[b], r_kTb[b],
                                                vvb[b], r_vvb[b])
                            for (it, fns) in its:
                                pipe_push(it, fns)
                        pipe_flush()
                    if debug == "oT" and half == 1:
                        for c4 in range(4):
                            COPY("dve", dbgf[:], oT[:, c4, :], r_oT[c4], [r_dbg])
                            dbg_ids.append(S.dma("sp", dbg_d[:, c4 * 1024:(c4 + 1) * 1024], dbgf[:], reads=[r_dbg],
                                                 dst=r_dbg))
                    epilogue(bname, l, first, goff)
                    first = False
                mixer_out(l)

        out_ids = []
        r_out = [Res("yout%d" % i) for i in range(8)]
        for s in range(n_seq):
            for dc in range(8):
                S.dma("sp", xres[:, dc, :], x_d[s, dc], writes=r_xres[dc], dst=r_xres[dc][0])
                for t in range(1, 4):
                    r_xres[dc][t].w = r_xres[dc][0].w
            for l in layers:
                if "ffn1" in phases:
                    ffn(l, 0)
                if "mix" in phases:
                    mixer(l)
                if "ffn2" in phases:
                    ffn(l, 1)
            for dc in range(8):
                out_ids.append(S.dma("sp", y_d[s, dc], xres[:, dc, :], reads=r_xres[dc], dst=r_out[dc]))
        S.wait_all("sp", out_ids[-8:] + dbg_ids)
        S.emit_all()
        stats = dict(n_instr=S.n_instr, n_wait=S.n_wait, nsem=S.nsem)
    return nc, stats


def _t5_bucket_np(n):
    n = np.maximum(n, 0)
    max_exact = 16
    nf = np.maximum(n, 1).astype(np.float32)
    large = max_exact + (np.log(nf / np.float32(max_exact)) / np.float32(math.log(128 / max_exact))
                         * np.float32(32 - max_exact)).astype(np.int32)
    large = np.minimum(large, 31)
    return np.where(n < max_exact, n, large)


def _host_layout(inputs):
    f32 = np.float32
    prm = np.zeros((128, NP), f32)
    p = np.arange(128)
    norms = [inputs["ffn1_norm"], inputs["mix_norm"], inputs["ffn2_norm"]]
    for l in range(DEPTH):
        for which in range(3):
            g = np.asarray(norms[which][l], f32).reshape(8, 128)
            prm[:, PC_NORM + (l * 3 + which) * 8: PC_NORM + (l * 3 + which) * 8 + 8] = g.T
        for k, name in enumerate(["q_norm_diff", "k_norm_diff", "q_norm_swa", "k_norm_swa"]):
            prm[:, PC_QK + l * 4 + k] = np.asarray(inputs[name][l], f32)[p % 64]
        prm[:, PC_SINK + l * 8: PC_SINK + l * 8 + 8] = np.asarray(inputs["swa_sinks"][l], f32)[None, :]
        prm[:, PC_LAM + l * 256: PC_LAM + (l + 1) * 256] = np.asarray(inputs["diff_lambda"][l], f32).reshape(1, 256)
        prm[:, PC_SUBLN + l * 128: PC_SUBLN + (l + 1) * 128] = np.asarray(inputs["diff_subln"][l], f32)[None, :]
    rb = np.asarray(inputs["rel_bias"], f32)
    prm[:, PC_CFAR: PC_CFAR + 4] = rb[31, 0:4][None, :]
    i = np.arange(128)[:, None]
    j = np.arange(128)[None, :]
    d0 = j - i
    d1 = 128 + j - i
    b0 = _t5_bucket_np(d0)
    b1 = _t5_bucket_np(d1)
    bt = np.zeros((128, 12, 256), f32)
    for h in range(12):
        t0 = rb[b0, h]
        t0 = np.where(d0 >= 0, t0, f32(MASKV))
        t1 = rb[b1, h]
        if h >= 4:
            t1 = np.where(d1 < 128, t1, f32(MASKV))
        bt[:, h, 0:128] = t0
        bt[:, h, 128:256] = t1
    return prm, bt


_CACHE = {}


def kernel(**inputs):
    x = np.asarray(inputs["x"], np.float32)
    B = x.shape[0]
    prm, bt = _host_layout(inputs)
    if "nc" not in _CACHE:
        _CACHE["nc"] = build_program()[0]
    nc = _CACHE["nc"]
    shared = {k: np.ascontiguousarray(np.asarray(inputs[k], np.float32)) for k in
              ["ffn1_w_in", "ffn1_w_out", "ffn2_w_in", "ffn2_w_out", "w_in", "w_proj_sb", "w_proj_diff",
               "w_proj_swa", "w_out"]}
    shared["params"] = prm
    shared["btiles"] = bt
    in_maps = []
    for c in range(N_CORES):
        xs = x[c * SEQ_PER_CORE:(c + 1) * SEQ_PER_CORE]
        xs = xs[:, ::-1, :]
        xt = np.ascontiguousarray(xs.transpose(0, 2, 1)).reshape(SEQ_PER_CORE, 8, 128, SEQ)
        m = dict(shared)
        m["x"] = xt
        in_maps.append(m)
    res = run_bass_kernel_spmd(nc, in_maps, core_ids=list(range(N_CORES)))
    out = np.empty((B, SEQ, D_MODEL), np.float32)
    for c in range(N_CORES):
        y = np.asarray(res.results[c]["y"]).reshape(SEQ_PER_CORE, D_MODEL, SEQ)
        out[c * SEQ_PER_CORE:(c + 1) * SEQ_PER_CORE] = y.transpose(0, 2, 1)[:, ::-1, :]
    return out
```

```python
import contextlib
import math
import numpy as np
import concourse.bass as bass
import concourse.mybir as mybir
from concourse.bass_utils import run_bass_kernel_spmd

F32 = mybir.dt.float32
BF16 = mybir.dt.bfloat16
AF = mybir.ActivationFunctionType
ALU = mybir.AluOpType

D_MODEL = 1024
SEQ = 2048
DEPTH = 2
D_FF = 2816
NJ = D_FF // 128
IN_W = 6912
EPS = 1e-6
N_CORES = 8
SEQ_PER_CORE = 2
MASKV = -30000.0
SCALE = 0.125
SWA_BATCHED = False

OFF_QA, OFF_KA, OFF_VA = 0, 512, 1024
OFF_QD, OFF_KD, OFF_VD = 1536, 2048, 2560
OFF_QS, OFF_KS, OFF_VS = 3072, 3584, 3712
OFF_GA, OFF_GD, OFF_GS = 3840, 4864, 5888

PC_NORM = 0
PC_QK = 48
PC_SINK = 56
PC_CFAR = 72
PC_SUBLN = 76
PC_LAM = 332
NP = 844
NPR = 332


class Res:
    __slots__ = ("name", "w", "r", "sem", "semval")

    def __init__(self, name):
        self.name = name
        self.w = None
        self.r = {}
        self.sem = None
        self.semval = 0


class Sched:
    ROT = 30000
    ENGS = ("pe", "act", "dve", "pool", "sp")

    def __init__(self, nc, stack):
        self.nc = nc
        self.stack = stack
        self.ops = {e: [] for e in self.ENGS}
        self.cnt = {e: 0 for e in self.ENGS}
        self.epoch = {e: 0 for e in self.ENGS}
        self.waited = {e: {} for e in self.ENGS}
        self.semh = {}
        self.nsem = 0
        self.pending_noinc = {e: False for e in self.ENGS}
        self.n_instr = 0
        self.n_wait = 0

    def _sem(self, key):
        h = self.semh.get(key)
        if h is None:
            h = self.stack.enter_context(self.nc.semaphore("s%d" % self.nsem))
            self.nsem += 1
            self.semh[key] = h
        return h

    def _next_id(self, eng):
        if self.cnt[eng] >= self.ROT and not self.pending_noinc[eng]:
            self.epoch[eng] += 1
            self.cnt[eng] = 0
        return (("e", eng, self.epoch[eng]), self.cnt[eng] + 1)

    def _collect(self, eng, reads, writes):
        deps = {}

        def add(d):
            if d is None:
                return
            k, v = d
            if eng == "pe" and k[0] == "e" and k[1] == "pe":
                return
            if deps.get(k, 0) < v:
                deps[k] = v
        for r in reads:
            add(r.w)
        for w in writes:
            add(w.w)
            for d in w.r.items():
                add(d)
        out = []
        wd = self.waited[eng]
        for k, v in deps.items():
            if wd.get(k, 0) >= v:
                continue
            wd[k] = v
            out.append((k, v))
        return out

    def op(self, eng, emit, reads=(), writes=(), inc=True):
        waits = self._collect(eng, reads, writes)
        myid = self._next_id(eng)
        if inc:
            self.cnt[eng] += 1
            self.pending_noinc[eng] = False
        else:
            self.pending_noinc[eng] = True
        for r in reads:
            if r.r.get(myid[0], 0) < myid[1]:
                r.r[myid[0]] = myid[1]
        for w in writes:
            w.w = myid
            w.r = {}
        self.ops[eng].append((waits, emit, myid[0] if inc else None, 1))
        self.n_instr += 1
        self.n_wait += len(waits)

    def dma(self, queue, out_ap, in_ap, reads=(), writes=(), dst=None):
        waits = self._collect(queue, reads, writes)
        if dst.sem is None:
            dst.sem = ("d", dst.name, id(dst))
        dst.semval += 16
        myid = (dst.sem, dst.semval)
        for r in reads:
            if r.r.get(myid[0], 0) < myid[1]:
                r.r[myid[0]] = myid[1]
        for w in writes:
            w.w = myid
            w.r = {}
        self.ops[queue].append((waits, (lambda e: e.dma_start(out=out_ap, in_=in_ap)), dst.sem, 16))
        self.n_instr += 1
        self.n_wait += len(waits)
        return myid

    def wait_all(self, eng, ids):
        self.ops[eng].append((list(ids), None, None, 0))

    def emit_all(self):
        nc = self.nc
        for e in self.ENGS:
            for waits, emit, inck, incv in self.ops[e]:
                for k, v in waits:
                    self._sem(k)
                if inck is not None:
                    self._sem(inck)
        ops = self.ops
        semh = self.semh

        def run(ename, e):
            for waits, emit, inck, incv in ops[ename]:
                for k, v in waits:
                    e.wait_ge(semh[k], v)
                if emit is not None:
                    ins = emit(e)
                    if inck is not None:
                        ins.then_inc(semh[inck], incv)

        with nc.Block() as block:
            if ops["pe"]:
                @block.tensor
                def _(e):
                    run("pe", e)
            if ops["act"]:
                @block.scalar
                def _(e):
                    run("act", e)
            if ops["dve"]:
                @block.vector
                def _(e):
                    run("dve", e)
            if ops["pool"]:
                @block.gpsimd
                def _(e):
                    run("pool", e)
            if ops["sp"]:
                @block.sync
                def _(e):
                    run("sp", e)


def build_program(n_seq=SEQ_PER_CORE, layers=(0, 1), phases=("ffn1", "mix", "ffn2"),
                  branches=("sb", "diff", "swa"), GC=2, debug=None):
    nc = bass.Bass("TRN2", target_bir_lowering=False, dynamic_dma_scratch_size=512)
    dram = {}

    def din(name, shape):
        dram[name] = nc.dram_tensor(name, list(shape), F32, kind="ExternalInput").ap()
        return dram[name]

    x_d = din("x", [n_seq, 8, 128, SEQ])
    w_ffn_in = [din("ffn1_w_in", [DEPTH, D_MODEL, 2 * D_FF]), din("ffn2_w_in", [DEPTH, D_MODEL, 2 * D_FF])]
    w_ffn_out = [din("ffn1_w_out", [DEPTH, D_FF, D_MODEL]), din("ffn2_w_out", [DEPTH, D_FF, D_MODEL])]
    w_in_d = din("w_in", [DEPTH, D_MODEL, IN_W])
    w_proj_d = {"sb": din("w_proj_sb", [DEPTH, 512, D_MODEL]),
                "diff": din("w_proj_diff", [DEPTH, 512, D_MODEL]),
                "swa": din("w_proj_swa", [DEPTH, 512, D_MODEL])}
    w_out_d = din("w_out", [DEPTH, D_MODEL, D_MODEL])
    params_d = din("params", [128, NP])
    btiles_d = din("btiles", [128, 12, 256])
    y_d = nc.dram_tensor("y", [n_seq, 8, 128, SEQ], F32, kind="ExternalOutput").ap()
    dbg_d = nc.dram_tensor("dbg", [128, 4096], F32, kind="ExternalOutput").ap() if debug else None

    with contextlib.ExitStack() as st:
        S = Sched(nc, st)

        def sb(name, shape, dt):
            return st.enter_context(nc.sbuf_tensor(name, list(shape), dt))

        def ps(name, shape, dt):
            return st.enter_context(nc.psum_tensor(name, list(shape), dt))

        xres = sb("xres", [128, 8, SEQ], F32)
        r_xres = [[Res("xres%d_%d" % (c, t)) for t in range(4)] for c in range(8)]
        xn = sb("xn", [128, 8, SEQ], BF16)
        r_xn = [[Res("xn%d_%d" % (c, t)) for t in range(4)] for c in range(8)]
        mrg = sb("mrg", [128, 8192], BF16)
        r_mrg = [Res("mrg%d" % i) for i in range(16)]
        oT = sb("oT", [128, 4, 1024], BF16)
        r_oT = [[Res("oT%d_%d" % (c, q)) for q in range(8)] for c in range(4)]
        Wall = sb("Wall", [128, 6 * 2048], BF16)
        r_W = [Res("W%d" % i) for i in range(6)]
        qTb = [sb("qT%d" % b, [128, 1024], BF16) for b in range(2)]
        r_qTb = [[Res("qT%d_%d" % (b, i)) for i in range(2)] for b in range(2)]
        kTb = [sb("kT%d" % b, [128, SEQ], BF16) for b in range(2)]
        r_kTb = [[Res("kT%d_%d" % (b, i)) for i in range(4)] for b in range(2)]
        vvb = [sb("vv%d" % b, [128, 16, 128], BF16) for b in range(2)]
        r_vvb = [[Res("vv%d_%d" % (b, i)) for i in range(4)] for b in range(2)]
        NSTG = 3
        stg = [sb("stg%d" % i, [128, 512], F32) for i in range(NSTG)]
        r_stg = [Res("stg%d" % i) for i in range(NSTG)]
        el = [sb("el%d" % i, [128, 1024], F32) for i in range(2)]
        r_el = [Res("el%d" % i) for i in range(2)]
        ca = [sb("ca%d" % i, [128, 1024], F32) for i in range(2)]
        r_ca = [Res("ca%d" % i) for i in range(2)]
        wb = [sb("wb%d" % i, [128, 1024], BF16) for i in range(2)]
        r_wb = [Res("wb%d" % i) for i in range(2)]
        wT = [sb("wT%d" % i, [128, 1024], BF16) for i in range(2)]
        r_wT = [Res("wT%d" % i) for i in range(2)]
        sq = [sb("sq%d" % i, [128, 512], BF16) for i in range(2)]
        r_sq = [Res("sq%d" % i) for i in range(2)]
        sg = [sb("sg%d" % i, [128, 512], F32) for i in range(2)]
        r_sg = [Res("sg%d" % i) for i in range(2)]
        rstd = sb("rstd", [128, 512], F32)
        r_rstd = Res("rstd")
        osb = [sb("osb%d" % i, [128, 128], BF16) for i in range(2)]
        r_osb = [Res("osb%d" % i) for i in range(2)]
        osbw = [sb("osbw%d" % i, [128, 256], BF16) for i in range(2)]
        r_osbw = [Res("osbw%d" % i) for i in range(2)]
        of32 = [sb("of32_%d" % i, [128, 128], F32) for i in range(2)]
        r_of32 = [Res("of32_%d" % i) for i in range(2)]
        btl = sb("btl", [128, 12, 256], F32)
        r_btl = Res("btl")
        prm = sb("prm", [128, NPR], F32)
        r_prm = Res("prm")
        ones_bf = sb("ones_bf", [128, 128], BF16)
        bones_bf = sb("bones_bf", [128, 128], BF16)
        ident_bf = sb("ident_bf", [128, 128], BF16)
        onesrow = sb("onesrow", [128, 1024], BF16)
        r_const = Res("const")
        small = sb("small", [128, 64], F32)
        r_small = [Res("small%d" % i) for i in range(64)]
        lamt = sb("lamt", [128, 8], F32)
        r_lamt = Res("lamt")
        esink = sb("esink", [128, 16], F32)
        subg = sb("subg", [128, 2, 128], F32)
        tmp64 = sb("tmp64", [128, 64], F32)
        sqj = sb("sqj", [128, 128], F32)
        r_tmp64 = Res("tmp64")
        carry = [sb("carry%d" % i, [128, 1], F32) for i in range(2)]
        r_carry = [Res("carry%d" % i) for i in range(2)]

        dbgf = sb("dbgf", [128, 1024], F32) if debug else None
        r_dbg = Res("dbg")
        dbg_ids = []
        ZA = ps("ZA", [128, 1024], F32)
        ZB = ps("ZB", [128, 1024], F32)
        r_Z = [[Res("ZA0"), Res("ZA1")], [Res("ZB0"), Res("ZB1")]]
        Zt = [ZA, ZB]
        Dbanks = [(ZA, 0, r_Z[0][0]), (ZA, 512, r_Z[0][1]), (ZB, 0, r_Z[1][0]), (ZB, 512, r_Z[1][1])]
        Tt = [ps("T0", [128, 1024], BF16), ps("T1", [128, 1024], BF16)]
        r_T = [Res("T0"), Res("T1")]
        Ot = [ps("O0", [128, 512], F32), ps("O1", [128, 512], F32)]
        r_O = [Res("O0"), Res("O1")]

        state = {"d": 0, "o": 0, "stg": 0, "sq": 0, "sg": 0}

        def next_d():
            i = state["d"]
            state["d"] = (i + 1) % 4
            t, off, r = Dbanks[i]
            return t[:, off:off + 512], r

        def next_o():
            i = state["o"]
            state["o"] = (i + 1) % 2
            return Ot[i], r_O[i]

        def ACT(out, in_, func, reads, writes, **kw):
            S.op("act", lambda e: e.activation(out=out, in_=in_, func=func, **kw), reads, writes)

        def MM(out, lhsT, rhs, start, stop, reads, writes, inc):
            S.op("pe", lambda e: e.matmul(out, lhsT=lhsT, rhs=rhs, start=start, stop=stop),
                 reads, writes, inc=inc)

        def TR(out, in_, reads, writes, inc=True):
            S.op("pe", lambda e: e.transpose(out, in_, ident_bf[:]), reads, writes, inc=inc)

        def STT(eng, out, in0, scalar, in1, op0, op1, reads, writes):
            S.op(eng, lambda e: e.scalar_tensor_tensor(out=out, in0=in0, scalar=scalar, in1=in1, op0=op0, op1=op1),
                 reads, writes)

        def TT(eng, out, in0, in1, op, reads, writes):
            S.op(eng, lambda e: e.tensor_tensor(out=out, in0=in0, in1=in1, op=op), reads, writes)

        def TS(eng, out, in0, s1, s2, op0, op1, reads, writes):
            S.op(eng, lambda e: e.tensor_scalar(out=out, in0=in0, scalar1=s1, scalar2=s2, op0=op0, op1=op1),
                 reads, writes)

        def COPY(eng, out, in_, reads, writes):
            S.op(eng, lambda e: e.tensor_copy(out=out, in_=in_), reads, writes)

        def load_w(dst_ap, src_ap, n, dst_res, extra_dst=None):
            i = state["stg"]
            state["stg"] = (i + 1) % NSTG
            S.dma("sp", stg[i][:, :n], src_ap, writes=[r_stg[i]], dst=r_stg[i])
            TS("pool", dst_ap, stg[i][:, :n], 1.0, 0.0, ALU.mult, ALU.add, [r_stg[i]], dst_res)
            if extra_dst is not None:
                COPY("pool", extra_dst, stg[i][:, :n], [r_stg[i]], dst_res)

        def wview(r0, a, b):
            return Wall[:, r0 * 2048: r0 * 2048 + a * b].rearrange("p (a b) -> p a b", a=a)

        def pcol(c, n=1):
            return prm[:, c:c + n]

        S.dma("sp", prm[:], params_d[:, 0:NPR], writes=[r_prm], dst=r_prm)
        S.dma("sp", el[0][:, 0:512], params_d[:, PC_LAM:PC_LAM + 512], writes=[r_el[0]], dst=r_el[0])
        S.dma("sp", btl[:], btiles_d, writes=[r_btl], dst=r_btl)
        S.op("pool", lambda e: e.memset(ones_bf[:], 1.0), [], [r_const])
        S.op("pool", lambda e: e.memset(onesrow[:], 1.0), [], [r_const])
        S.op("pool", lambda e: e.affine_select(out=ident_bf[:], in_=ones_bf[:], pattern=[[1, 128]],
                                               compare_op=ALU.is_equal, fill=0.0, base=0,
                                               channel_multiplier=-1), [r_const], [r_const])
        S.op("pool", lambda e: e.memset(bones_bf[:], 0.0), [], [r_const])
        S.op("pool", lambda e: e.memset(bones_bf[0:64, 0:64], 1.0), [], [r_const])
        S.op("pool", lambda e: e.memset(bones_bf[64:128, 64:128], 1.0), [], [r_const])
        for l in range(DEPTH):
            lam_init = 0.8 - 0.6 * math.exp(-0.3 * l)
            base = l * 256
            for k in range(2):
                TT("dve", tmp64[:], el[0][:, base + 128 * k: base + 128 * k + 64],
                   el[0][:, base + 128 * k + 64: base + 128 * k + 128], ALU.mult,
                   [r_el[0]], [r_tmp64])
                S.op("dve", lambda e, k=k, l=l: e.reduce_sum(out=small[:, 2 * l + k:2 * l + k + 1], in_=tmp64[:],
                                                             axis=mybir.AxisListType.X),
                     [r_tmp64], [r_small[2 * l + k]])
                ACT(small[:, 2 * l + k:2 * l + k + 1], small[:, 2 * l + k:2 * l + k + 1], AF.Exp,
                    [r_small[2 * l + k]], [r_small[2 * l + k]])
            STT("dve", lamt[:, l:l + 1], small[:, 2 * l:2 * l + 1], float(lam_init), small[:, 2 * l + 1:2 * l + 2],
                ALU.add, ALU.subtract, [r_small[2 * l], r_small[2 * l + 1]], [r_lamt])
            TS("dve", subg[:, l, :], pcol(PC_SUBLN + l * 128, 128), float(1.0 - lam_init), None, ALU.mult, ALU.bypass,
               [r_prm], [r_lamt])
        ACT(esink[:], pcol(PC_SINK, 16), AF.Exp, [r_prm], [r_lamt])

        def rmsnorm_to_xn(gcol):
            for tt in range(4):
                tsl = slice(tt * 512, (tt + 1) * 512)
                ob, r_ob = next_o()
                for dc in range(8):
                    i = state["sq"]
                    state["sq"] = (i + 1) % 2
                    ACT(sq[i][:], xres[:, dc, tsl], AF.Square, [r_xres[dc][tt]], [r_sq[i]])
                    MM(ob[:], ones_bf[:], sq[i][:], dc == 0, dc == 7, [r_sq[i], r_const], [r_ob], inc=True)
                ACT(rstd[:], ob[:], AF.Ln, [r_ob], [r_rstd], scale=1.0 / D_MODEL, bias=EPS)
                ACT(rstd[:], rstd[:], AF.Exp, [r_rstd], [r_rstd], scale=-0.5)
                for dc in range(8):
                    STT("dve", xn[:, dc, tsl], xres[:, dc, tsl], pcol(gcol + dc), rstd[:], ALU.mult, ALU.mult,
                        [r_xres[dc][tt], r_rstd, r_prm], [r_xn[dc][tt]])

        def ffn(l, which):
            w_in_ap = w_ffn_in[which][l]
            w_out_ap = w_ffn_out[which][l]
            rmsnorm_to_xn(PC_NORM + (l * 3 + (0 if which == 0 else 2)) * 8)
            groups = [(j0, min(j0 + GC, NJ)) for j0 in range(0, NJ, GC)]
            ob4 = [(Ot[0][:], r_O[0]), (Ot[1][:], r_O[1]),
                   (Tt[0][:].bitcast(F32), r_T[0]), (Tt[1][:].bitcast(F32), r_T[1])]
            ost = {"i": 0}

            def views(gi):
                buf = gi % 2
                return (wview(buf * 3 + 0, 8, 256), wview(buf * 3 + 1, 8, 256), wview(buf * 3 + 2, 2, 1024),
                        r_W[buf * 3 + 0], r_W[buf * 3 + 1], r_W[buf * 3 + 2],
                        mrg[:, buf * 4096: buf * 4096 + 4096].rearrange("p (a b) -> p a b", a=2), buf)

            def load_group(gi):
                j0, j1 = groups[gi]
                n = j1 - j0
                wg, wu, wo, rg, ru, ro, actb, buf = views(gi)
                for dc in range(8):
                    load_w(wg[:, dc, :n * 128], w_in_ap[dc * 128:(dc + 1) * 128, j0 * 128:j1 * 128], n * 128, [rg])
                    load_w(wu[:, dc, :n * 128],
                           w_in_ap[dc * 128:(dc + 1) * 128, D_FF + j0 * 128:D_FF + j1 * 128], n * 128, [ru])
                for jj in range(n):
                    for hh in range(2):
                        load_w(wo[:, jj, hh * 512:(hh + 1) * 512],
                               w_out_ap[(j0 + jj) * 128:(j0 + jj + 1) * 128, hh * 512:(hh + 1) * 512], 512, [ro])

            def win_unit(gi, jj, tt):
                wg, wu, wo, rg, ru, ro, actb, buf = views(gi)
                tsl = slice(tt * 512, (tt + 1) * 512)
                r_act = r_mrg[buf * 8 + jj * 4 + tt]
                hg, r_hg = next_d()
                hu, r_hu = next_d()
                for dc in range(8):
                    MM(hg, wg[:, dc, jj * 128:(jj + 1) * 128], xn[:, dc, tsl], dc == 0, dc == 7,
                       [rg, r_xn[dc][tt]], [r_hg], inc=(dc == 7))
                for dc in range(8):
                    MM(hu, wu[:, dc, jj * 128:(jj + 1) * 128], xn[:, dc, tsl], dc == 0, dc == 7,
                       [ru, r_xn[dc][tt]], [r_hu], inc=(dc == 7))
                i = state["sg"]
                state["sg"] = (i + 1) % 2
                ACT(sg[i][:], hg, AF.Silu, [r_hg], [r_sg[i]])
                TT("dve", actb[:, jj, tsl], sg[i][:], hu, ALU.mult, [r_sg[i], r_hu], [r_act])

            def wout_unit(gi, c, tt):
                j0, j1 = groups[gi]
                n = j1 - j0
                wg, wu, wo, rg, ru, ro, actb, buf = views(gi)
                tsl = slice(tt * 512, (tt + 1) * 512)
                ob, r_ob = ob4[ost["i"]]
                ost["i"] = (ost["i"] + 1) % 4
                for jj in range(n):
                    MM(ob, wo[:, jj, c * 128:(c + 1) * 128], actb[:, jj, tsl], jj == 0, jj == n - 1,
                       [ro, r_mrg[buf * 8 + jj * 4 + tt]], [r_ob], inc=(jj == n - 1))
                STT("dve", xres[:, c, tsl], ob, 0.5, xres[:, c, tsl], ALU.mult, ALU.add,
                    [r_ob, r_xres[c][tt]], [r_xres[c][tt]])

            pending = []
            for gi in range(len(groups)):
                j0, j1 = groups[gi]
                n = j1 - j0
                load_group(gi)
                wins = [(jj, tt) for jj in range(n) for tt in range(4)]
                per = (len(pending) + len(wins) - 1) // len(wins) if pending else 0
                for (jj, tt) in wins:
                    win_unit(gi, jj, tt)
                    for _ in range(per):
                        if pending:
                            g2, c, t2 = pending.pop(0)
                            wout_unit(g2, c, t2)
                while pending:
                    g2, c, t2 = pending.pop(0)
                    wout_unit(g2, c, t2)
                pending = [(gi, c, tt) for c in range(8) for tt in range(4)]
            while pending:
                g2, c, t2 = pending.pop(0)
                wout_unit(g2, c, t2)

        def proj_fm_units(wv_, rw, col0, dst, r_dst_fn, tok0, ntile, mode, gain_col=None, alt=0):
            def unit(ti):
                t0 = tok0 + ti * 512
                tt = t0 // 512
                d, r_d = next_d()
                for dc in range(8):
                    MM(d, wv_[:, dc, col0:col0 + 128], xn[:, dc, t0:t0 + 512], dc == 0, dc == 7,
                       [r_xn[dc][tt]] + rw, [r_d], inc=(dc == 7))
                dsl = dst[:, ti * 512:(ti + 1) * 512]
                rd = r_dst_fn(ti)
                if mode == "copy":
                    if (ti + alt) % 2 == 0:
                        ACT(dsl, d, AF.Copy, [r_d], [rd])
                    else:
                        COPY("dve", dsl, d, [r_d], [rd])
                else:
                    i = state["sq"]
                    state["sq"] = (i + 1) % 2
                    ACT(sq[i][:], d, AF.Square, [r_d], [r_sq[i]])
                    d2, r_d2 = next_d()
                    MM(d2, bones_bf[:], sq[i][:], True, True, [r_sq[i], r_const], [r_d2], inc=True)
                    j = state["sg"]
                    state["sg"] = (j + 1) % 2
                    ACT(sg[j][:], d2, AF.Ln, [r_d2], [r_sg[j]], scale=1.0 / 64.0, bias=EPS)
                    ACT(sg[j][:], sg[j][:], AF.Exp, [r_sg[j]], [r_sg[j]], scale=-0.5)
                    STT("dve", dsl, d, pcol(gain_col), sg[j][:], ALU.mult, ALU.mult, [r_d, r_sg[j], r_prm], [rd])
            return [(lambda ti=ti: unit(ti)) for ti in range(ntile)]

        def proj_v_units(wv_, rw, col0, ncol, kb0, kb1, vv, r_vv):
            def unit(g0):
                d, r_d = next_d()
                nb = min(4, kb1 - g0)
                for bi in range(nb):
                    kb = g0 + bi
                    for dc in range(8):
                        MM(d[:, bi * 128: bi * 128 + ncol], xn[:, dc, kb * 128:(kb + 1) * 128],
                           wv_[:, dc, col0:col0 + ncol], dc == 0, dc == 7,
                           [r_xn[dc][kb // 4]] + rw, [r_d], inc=(dc == 7 and bi == nb - 1))
                src = d.rearrange("p (a b) -> p a b", a=4)[:, :nb, :ncol]
                COPY("dve", vv[:, g0:g0 + nb, :ncol], src, [r_d], [r_vv[g0 // 4]])
            return [(lambda g0=g0: unit(g0)) for g0 in range(kb0, kb1, 4)]

        pipe_items = []

        def pipe_run_step(k):
            for off, name in ((2, "A"), (1, "B1"), (0, "B2"), (-1, "T"), (-2, "CP"), (-3, "PV")):
                j = k + off
                if 0 <= j < len(pipe_items):
                    it, fns = pipe_items[j]
                    fns[name](it, j % 2)

        def pipe_push(it, fns):
            pipe_items.append((it, fns))
            pipe_run_step(len(pipe_items) - 3)

        def pipe_flush():
            g = len(pipe_items)
            for k in range(g - 2, g + 3):
                pipe_run_step(k)
            del pipe_items[:]

        def attention(kind, l, hp, q0, qT, r_qT, kT, r_kT, vv, r_vv):
            items = []
            for qi in range(8):
                qb = q0 // 128 + qi
                kstart = qb * 128
                kend = SEQ if kind != "swa" else min(SEQ, kstart + 256)
                segs = []
                k0 = kstart
                while k0 < kend:
                    n = min(1024, kend - k0)
                    segs.append((k0, n))
                    k0 += n
                if kind == "diff":
                    subs = [0, 1]
                else:
                    subs = [0, 1]
                for sidx, sub in enumerate(subs):
                    for si, (k0, n) in enumerate(segs):
                        items.append(dict(qi=qi, qb=qb, sub=sub, si=si, k0=k0, n=n, nseg=len(segs),
                                          first_of_qb=(sidx == 0 and si == 0),
                                          last_of_qb=(sidx == len(subs) - 1 and si == len(segs) - 1)))
            ostate = {}

            def stage_A(it, ib):
                base = it["sub"] * 64
                qsl = qT[base:base + 64, it["qi"] * 128:(it["qi"] + 1) * 128]
                n, k0 = it["n"], it["k0"]
                for c0 in range(0, n, 512):
                    cn = min(512, n - c0)
                    MM(Zt[ib][:, c0:c0 + cn], qsl, kT[base:base + 64, k0 + c0:k0 + c0 + cn], True, True,
                       [r_qT[it["qi"] // 4]] + r_kT[(k0 + c0) // 512:(k0 + c0 + cn - 1) // 512 + 1], [r_Z[ib][c0 // 512]], inc=True)

            def scols(it):
                return 8 + it["sub"] * 4 + (it["qi"] % 4) * 8

            def stage_B1(it, ib):
                n, k0, si = it["n"], it["k0"], it["si"]
                rz = r_Z[ib][:(n + 511) // 512]
                Z = Zt[ib]
                if kind == "sb":
                    ACT(el[ib][:, :n], Z[:, :n], AF.Exp, rz, [r_el[ib]], scale=SCALE)
                    ACT(el[ib][:, :n], el[ib][:, :n], AF.Ln, [r_el[ib]], [r_el[ib]], bias=1.0)
                    if si == 0:
                        S.op("pool", lambda e: e.affine_select(out=el[ib][:, 0:128], in_=el[ib][:, 0:128],
                                                               pattern=[[1, 128]], compare_op=ALU.is_gt, fill=0.0,
                                                               base=0, channel_multiplier=-1),
                             [r_el[ib]], [r_el[ib]])
                        init = 0.0
                        rinit = []
                    else:
                        init = carry[it["sub"]][:, 0:1]
                        rinit = [r_carry[it["sub"]]]
                    S.op("dve", lambda e: e.tensor_tensor_scan(out=ca[ib][:, :n], data0=onesrow[:, :n],
                                                               data1=el[ib][:, :n], initial=init,
                                                               op0=ALU.mult, op1=ALU.add),
                         [r_el[ib], r_const] + rinit, [r_ca[ib]])
                    if si < it["nseg"] - 1:
                        COPY("dve", carry[it["sub"]][:, 0:1], ca[ib][:, n - 1:n], [r_ca[ib]], [r_carry[it["sub"]]])
                    STT("dve", el[ib][:, :n], Z[:, :n], SCALE, ca[ib][:, :n], ALU.mult, ALU.subtract,
                        rz + [r_ca[ib]], [r_el[ib]])
                else:
                    head = hp if kind == "diff" else 4 + hp * 2 + it["sub"]
                    bt = btl[:, head, :]
                    if si == 0:
                        nn = min(256, n)
                        STT("dve", el[ib][:, :nn], Z[:, :nn], SCALE, bt[:, :nn], ALU.mult, ALU.add,
                            rz[:1] + [r_btl], [r_el[ib]])
                        if n > nn:
                            TS("dve", el[ib][:, nn:n], Z[:, nn:n], SCALE, pcol(PC_CFAR + head), ALU.mult, ALU.add,
                               rz + [r_prm], [r_el[ib]])
                    else:
                        TS("dve", el[ib][:, :n], Z[:, :n], SCALE, pcol(PC_CFAR + head), ALU.mult, ALU.add,
                           rz + [r_prm], [r_el[ib]])

            def stage_B2(it, ib):
                n, si = it["n"], it["si"]
                if kind == "sb":
                    ACT(wb[ib][:, :n], el[ib][:, :n], AF.Exp, [r_el[ib]], [r_wb[ib]])
                    if si == 0:
                        S.op("pool", lambda e: e.affine_select(out=wb[ib][:, 0:128], in_=wb[ib][:, 0:128],
                                                               pattern=[[1, 128]], compare_op=ALU.is_gt, fill=0.0,
                                                               base=0, channel_multiplier=-1),
                             [r_wb[ib]], [r_wb[ib]])
                else:
                    scol0 = scols(it)
                    if si == 0:
                        nn = min(256, n)
                        S.op("act", lambda e: e.activation(out=wb[ib][:, :nn], in_=el[ib][:, :nn], func=AF.Exp,
                                                           accum_out=small[:, scol0:scol0 + 1]),
                             [r_el[ib]], [r_wb[ib], r_small[scol0]])
                        if n > nn:
                            S.op("act", lambda e: e.activation(out=wb[ib][:, nn:n], in_=el[ib][:, nn:n], func=AF.Exp,
                                                               accum_out=small[:, scol0 + 1:scol0 + 2]),
                                 [r_el[ib]], [r_wb[ib], r_small[scol0 + 1]])
                    else:
                        S.op("act", lambda e: e.activation(out=wb[ib][:, :n], in_=el[ib][:, :n], func=AF.Exp,
                                                           accum_out=small[:, scol0 + 1 + si:scol0 + 2 + si]),
                             [r_el[ib]], [r_wb[ib], r_small[scol0 + 1 + si]])

            def stage_T(it, ib):
                nb = it["n"] // 128
                for jb in range(nb):
                    TR(Tt[ib][:, jb * 128:(jb + 1) * 128], wb[ib][:, jb * 128:(jb + 1) * 128],
                       [r_wb[ib], r_const], [r_T[ib]], inc=(jb == nb - 1))

            def stage_CP(it, ib):
                n = it["n"]
                ACT(wT[ib][:, :n], Tt[ib][:, :n], AF.Copy, [r_T[ib]], [r_wT[ib]])

            def stage_PV(it, ib):
                n, k0 = it["n"], it["k0"]
                nb = n // 128
                qi = it["qi"]
                if it["first_of_qb"]:
                    if kind == "diff":
                        ostate["o"] = (0, 1)
                    else:
                        oi = state["o"]
                        state["o"] = (oi + 1) % 2
                        ostate["o"] = (oi, oi)
                if kind == "diff":
                    oi = ostate["o"][it["sub"]]
                    ocols = slice(0, 128)
                    vcols = slice(0, 128)
                elif kind == "sb":
                    oi = ostate["o"][0]
                    ocols = slice(it["sub"] * 64, it["sub"] * 64 + 64)
                    vcols = ocols
                else:
                    oi = ostate["o"][0]
                    ocols = slice(it["sub"] * 64, it["sub"] * 64 + 64)
                    g = hp // 2
                    vcols = slice(g * 64, g * 64 + 64)
                for jb in range(nb):
                    kb = k0 // 128 + jb
                    MM(Ot[oi][:, ocols], wT[ib][:, jb * 128:(jb + 1) * 128], vv[:, kb, vcols],
                       it["si"] == 0 and jb == 0, it["si"] == it["nseg"] - 1 and jb == nb - 1,
                       [r_wT[ib], r_vv[kb // 4]], [r_O[oi]], inc=(jb == nb - 1))
                if kind == "swa":
                    sc = scols(it)
                    head = hp * 2 + it["sub"]
                    fb = qi % 2
                    TT("dve", small[:, sc + 2:sc + 3], small[:, sc:sc + 1], esink[:, l * 8 + head:l * 8 + head + 1],
                       ALU.add, [r_small[sc], r_lamt], [r_small[sc + 2]])
                    S.op("dve", lambda e: e.reciprocal(out=small[:, sc + 2:sc + 3], in_=small[:, sc + 2:sc + 3]),
                         [r_small[sc + 2]], [r_small[sc + 2]])
                    ACT(osb[fb][:, ocols], Ot[oi][:, ocols], AF.Copy, [r_O[oi], r_small[sc + 2]], [r_osb[fb]],
                        scale=small[:, sc + 2:sc + 3])
                if it["last_of_qb"]:
                    stage_F(it, qi)

            def stage_F(it, qi):
                fb = qi % 2
                oi = ostate["o"][0]
                if kind == "sb":
                    ACT(osb[fb][:], Ot[oi][:, 0:128], AF.Copy, [r_O[oi]], [r_osb[fb]])
                elif kind == "diff":
                    nseg = it["nseg"]
                    for c in range(2):
                        sc = 8 + c * 4 + (qi % 4) * 8
                        tot = 40 + c
                        n0 = it_n0[qi]
                        cols = [sc]
                        if n0 > 256:
                            cols.append(sc + 1)
                        if nseg > 1:
                            cols.append(sc + 2)
                        if len(cols) == 1:
                            COPY("dve", small[:, tot:tot + 1], small[:, cols[0]:cols[0] + 1], [r_small[cols[0]]],
                                 [r_small[tot]])
                        else:
                            TT("dve", small[:, tot:tot + 1], small[:, cols[0]:cols[0] + 1],
                               small[:, cols[1]:cols[1] + 1], ALU.add, [r_small[cols[0]], r_small[cols[1]]],
                               [r_small[tot]])
                            if len(cols) == 3:
                                TT("dve", small[:, tot:tot + 1], small[:, tot:tot + 1],
                                   small[:, cols[2]:cols[2] + 1], ALU.add, [r_small[tot], r_small[cols[2]]],
                                   [r_small[tot]])
                        S.op("dve", lambda e, tot=tot: e.reciprocal(out=small[:, tot:tot + 1], in_=small[:, tot:tot + 1]),
                             [r_small[tot]], [r_small[tot]])
                    STT("dve", small[:, 42:43], small[:, 41:42], -1.0, lamt[:, l:l + 1], ALU.mult, ALU.mult,
                        [r_small[41], r_lamt], [r_small[42]])
                    ACT(of32[fb][:], Ot[0][:, 0:128], AF.Copy, [r_O[0], r_small[40]], [r_of32[fb]],
                        scale=small[:, 40:41])
                    STT("dve", of32[fb][:], Ot[1][:, 0:128], small[:, 42:43], of32[fb][:], ALU.mult, ALU.add,
                        [r_O[1], r_small[42], r_of32[fb]], [r_of32[fb]])
                    S.op("act", lambda e: e.activation(out=sqj[:], in_=of32[fb][:],
                                                       func=AF.Square, accum_out=small[:, 43:44]),
                         [r_of32[fb]], [r_tmp64, r_small[43]])
                    ACT(small[:, 43:44], small[:, 43:44], AF.Ln, [r_small[43]], [r_small[43]],
                        scale=1.0 / 128.0, bias=EPS)
                    ACT(small[:, 43:44], small[:, 43:44], AF.Exp, [r_small[43]], [r_small[43]], scale=-0.5)
                    STT("dve", osb[fb][:], of32[fb][:], small[:, 43:44], subg[:, l, :], ALU.mult, ALU.mult,
                        [r_of32[fb], r_small[43], r_lamt], [r_osb[fb]])
                tdst = Ot[oi][:, 256:320].bitcast(BF16)
                TR(tdst, osb[fb][:], [r_osb[fb], r_const], [r_O[oi]], inc=True)
                COPY("dve", oT[:, hp, qi * 128:(qi + 1) * 128], tdst, [r_O[oi]], [r_oT[hp][qi]])

            it_n0 = {}
            for it in items:
                if it["si"] == 0:
                    it_n0[it["qi"]] = it["n"]
            fns = dict(A=stage_A, B1=stage_B1, B2=stage_B2, T=stage_T, CP=stage_CP, PV=stage_PV)
            return [(it, fns) for it in items]

        swa_ctr = {"i": 0}

        def attention_swa(l, hp, q0, qT, r_qT, kT, r_kT, vv, r_vv):
            g = hp // 2
            items = []
            for pair in range(4):
                combos = []
                for q in range(2):
                    qi = pair * 2 + q
                    qb = q0 // 128 + qi
                    w = 256 if qb < 15 else 128
                    for sub in range(2):
                        combos.append(dict(qi=qi, qb=qb, sub=sub, w=w, c=q * 2 + sub))
                items.append(dict(pair=pair, combos=combos, idx=swa_ctr["i"]))
                swa_ctr["i"] += 1
            octx = {}

            def sA(it, ib):
                for cb in it["combos"]:
                    base = cb["sub"] * 64
                    c, w, qi, k0 = cb["c"], cb["w"], cb["qi"], cb["qb"] * 128
                    MM(Zt[ib][:, c * 256:c * 256 + w], qT[base:base + 64, qi * 128:(qi + 1) * 128],
                       kT[base:base + 64, k0:k0 + w], True, True,
                       [r_qT[qi // 4]] + r_kT[k0 // 512:(k0 + w - 1) // 512 + 1], [r_Z[ib][c // 2]], inc=True)

            def sB1(it, ib):
                for cb in it["combos"]:
                    c, w = cb["c"], cb["w"]
                    head = 4 + hp * 2 + cb["sub"]
                    STT("dve", el[ib][:, c * 256:c * 256 + w], Zt[ib][:, c * 256:c * 256 + w], SCALE,
                        btl[:, head, 0:w], ALU.mult, ALU.add, [r_Z[ib][c // 2], r_btl], [r_el[ib]])

            def sB2(it, ib):
                sc = 8 + (it["idx"] % 4) * 8
                for cb in it["combos"]:
                    c, w = cb["c"], cb["w"]
                    S.op("act", lambda e, c=c, w=w: e.activation(out=wb[ib][:, c * 256:c * 256 + w],
                                                                 in_=el[ib][:, c * 256:c * 256 + w], func=AF.Exp,
                                                                 accum_out=small[:, sc + c:sc + c + 1]),
                         [r_el[ib]], [r_wb[ib], r_small[sc + c]])
                    if w < 256:
                        S.op("pool", lambda e, c=c, w=w: e.memset(wb[ib][:, c * 256 + w:(c + 1) * 256], 0.0),
                             [], [r_wb[ib]])

            def sT(it, ib):
                for jb in range(8):
                    TR(Tt[ib][:, jb * 128:(jb + 1) * 128], wb[ib][:, jb * 128:(jb + 1) * 128],
                       [r_wb[ib], r_const], [r_T[ib]], inc=(jb == 7))

            def sCP(it, ib):
                ACT(wT[ib][:, :], Tt[ib][:, :], AF.Copy, [r_T[ib]], [r_wT[ib]])

            def sPV(it, ib):
                oi = state["o"]
                state["o"] = (oi + 1) % 2
                sc = 8 + (it["idx"] % 4) * 8
                fb = it["idx"] % 2
                for cb in it["combos"]:
                    c, w, qb = cb["c"], cb["w"], cb["qb"]
                    nb = w // 128
                    for jb in range(nb):
                        kb = qb + jb
                        MM(Ot[oi][:, c * 64:(c + 1) * 64], wT[ib][:, c * 256 + jb * 128:c * 256 + (jb + 1) * 128],
                           vv[:, kb, g * 64:(g + 1) * 64], jb == 0, jb == nb - 1,
                           [r_wT[ib], r_vv[kb // 4]], [r_O[oi]], inc=(jb == nb - 1))
                es = esink[:, l * 8 + hp * 2:l * 8 + hp * 2 + 2]
                for q in range(2):
                    TT("dve", small[:, sc + 4 + q * 2:sc + 6 + q * 2], small[:, sc + q * 2:sc + q * 2 + 2], es, ALU.add,
                       [r_small[sc + q * 2], r_small[sc + q * 2 + 1], r_lamt],
                       [r_small[sc + 4 + q * 2], r_small[sc + 5 + q * 2]])
                S.op("dve", lambda e: e.reciprocal(out=small[:, sc + 4:sc + 8], in_=small[:, sc + 4:sc + 8]),
                     [r_small[sc + 4 + j] for j in range(4)], [r_small[sc + 4 + j] for j in range(4)])
                for c in range(4):
                    ACT(osbw[fb][:, c * 64:(c + 1) * 64], Ot[oi][:, c * 64:(c + 1) * 64], AF.Copy,
                        [r_O[oi], r_small[sc + 4 + c]], [r_osbw[fb]], scale=small[:, sc + 4 + c:sc + 5 + c])
                qi0 = it["pair"] * 2
                for q in range(2):
                    tdst = Ot[oi][:, 256 + q * 64:320 + q * 64].bitcast(BF16)
                    TR(tdst, osbw[fb][:, q * 128:(q + 1) * 128], [r_osbw[fb], r_const], [r_O[oi]], inc=True)
                    COPY("dve", oT[:, hp, (qi0 + q) * 128:(qi0 + q + 1) * 128], tdst, [r_O[oi]], [r_oT[hp][qi0 + q]])

            fns = dict(A=sA, B1=sB1, B2=sB2, T=sT, CP=sCP, PV=sPV)
            return [(it, fns) for it in items]

        def epilogue(bname, l, first, goff):
            wp_ap = w_proj_d[bname][l]
            for ch in range(2):
                buf = ch
                wp = wview(buf * 3 + 0, 4, 512)
                wgt = wview(buf * 3 + 1, 8, 512)
                rp = [r_W[buf * 3 + 0]]
                rgt = [r_W[buf * 3 + 1], r_W[buf * 3 + 2]]
                for k in range(4):
                    load_w(wp[:, k, :], wp_ap[k * 128:(k + 1) * 128, ch * 512:(ch + 1) * 512], 512, rp)
                for dc in range(8):
                    load_w(wgt[:, dc, :], w_in_d[l][dc * 128:(dc + 1) * 128, goff + ch * 512: goff + (ch + 1) * 512],
                           512, rgt)
                for cc in range(4):
                    c = ch * 4 + cc
                    for ti in range(2):
                        t0 = state["tok0"] + ti * 512
                        tt = t0 // 512
                        P, r_P = next_d()
                        G, r_G = next_d()
                        for k in range(4):
                            MM(P, wp[:, k, cc * 128:(cc + 1) * 128], oT[:, k, ti * 512:(ti + 1) * 512], k == 0, k == 3,
                               rp + r_oT[k][ti * 4:(ti + 1) * 4], [r_P], inc=(k == 3))
                        for dc in range(8):
                            MM(G, wgt[:, dc, cc * 128:(cc + 1) * 128], xn[:, dc, t0:t0 + 512], dc == 0, dc == 7,
                               rgt + [r_xn[dc][tt]], [r_G], inc=(dc == 7))
                        i = state["sg"]
                        state["sg"] = (i + 1) % 2
                        ACT(sg[i][:], G, AF.Sigmoid, [r_G], [r_sg[i]])
                        msl = mrg[:, (c * 2 + ti) * 512:(c * 2 + ti + 1) * 512]
                        rm = r_mrg[c * 2 + ti]
                        if first:
                            TT("dve", msl, sg[i][:], P, ALU.mult, [r_sg[i], r_P], [rm])
                        else:
                            TT("dve", sg[i][:], sg[i][:], P, ALU.mult, [r_sg[i], r_P], [r_sg[i]])
                            TT("pool", msl, msl, sg[i][:], ALU.add, [r_sg[i], rm], [rm])

        def mixer_out(l):
            for ch in range(2):
                wo = wview(ch * 3, 8, 512)
                ro = [r_W[ch * 3], r_W[ch * 3 + 1]]
                for c in range(8):
                    load_w(wo[:, c, :], w_out_d[l][c * 128:(c + 1) * 128, ch * 512:(ch + 1) * 512], 512, ro)
                for cc in range(4):
                    c2 = ch * 4 + cc
                    for ti in range(2):
                        t0 = state["tok0"] + ti * 512
                        tt = t0 // 512
                        ob, r_ob = next_o()
                        for c in range(8):
                            MM(ob[:], wo[:, c, cc * 128:(cc + 1) * 128], mrg[:, (c * 2 + ti) * 512:(c * 2 + ti + 1) * 512],
                               c == 0, c == 7, ro + [r_mrg[c * 2 + ti]], [r_ob], inc=(c == 7))
                        TT("dve", xres[:, c2, t0:t0 + 512], ob[:], xres[:, c2, t0:t0 + 512], ALU.add,
                           [r_ob, r_xres[c2][tt]], [r_xres[c2][tt]])

        def mixer(l):
            rmsnorm_to_xn(PC_NORM + (l * 3 + 1) * 8)
            for half in range(2):
                tok0 = half * 1024
                state["tok0"] = tok0
                nkt = (SEQ - tok0) // 512
                first = True
                for bname in branches:
                    if bname == "sb":
                        qoff, koff, voff, goff = OFF_QA, OFF_KA, OFF_VA, OFF_GA
                    elif bname == "diff":
                        qoff, koff, voff, goff = OFF_QD, OFF_KD, OFF_VD, OFF_GD
                    else:
                        qoff, koff, voff, goff = OFF_QS, OFF_KS, OFF_VS, OFF_GS
                    wq = wview(0, 8, 512)
                    wk = wview(2, 8, 512)
                    wv_ = wview(4, 8, 512)
                    rq, rk, rv = [r_W[0], r_W[1]], [r_W[2], r_W[3]], [r_W[4], r_W[5]]
                    for dc in range(8):
                        rows = slice(dc * 128, (dc + 1) * 128)
                        load_w(wq[:, dc, :], w_in_d[l][rows, qoff:qoff + 512], 512, rq)
                        if bname != "swa":
                            load_w(wk[:, dc, :], w_in_d[l][rows, koff:koff + 512], 512, rk)
                            load_w(wv_[:, dc, :], w_in_d[l][rows, voff:voff + 512], 512, rv)
                        else:
                            i = state["stg"]
                            state["stg"] = (i + 1) % NSTG
                            S.dma("sp", stg[i][:, :256], w_in_d[l][rows, koff:koff + 256], writes=[r_stg[i]],
                                  dst=r_stg[i])
                            for g in range(2):
                                for dup in range(2):
                                    COPY("pool", wk[:, dc, g * 128 + dup * 64: g * 128 + dup * 64 + 64],
                                         stg[i][:, g * 64:(g + 1) * 64], [r_stg[i]], rk)
                            COPY("pool", wv_[:, dc, 0:128], stg[i][:, 128:256], [r_stg[i]], rv)
                    def proj_units(hp):
                        b = hp % 2
                        qd_, rqd = qTb[b], r_qTb[b]
                        u = []
                        if bname == "sb" or bname == "diff":
                            kd_, rkd, vd_, rvd = kTb[b], r_kTb[b], vvb[b], r_vvb[b]
                            mode = "copy" if bname == "sb" else "qknorm"
                            gq = None if bname == "sb" else PC_QK + l * 4 + 0
                            gk = None if bname == "sb" else PC_QK + l * 4 + 1
                            u += proj_fm_units(wq, rq, hp * 128, qd_, lambda ti: rqd[ti], tok0, 2, mode,
                                               gain_col=gq, alt=0)
                            u += proj_fm_units(wk, rk, hp * 128, kd_[:, tok0:], lambda ti: rkd[(tok0 // 512) + ti],
                                               tok0, nkt, mode, gain_col=gk, alt=1)
                            u += proj_v_units(wv_, rv, hp * 128, 128, tok0 // 128, 16, vd_, rvd)
                        else:
                            u += proj_fm_units(wq, rq, hp * 128, qd_, lambda ti: rqd[ti], tok0, 2, "qknorm",
                                               gain_col=PC_QK + l * 4 + 2)
                            if hp % 2 == 0:
                                g = hp // 2
                                kd_, rkd = kTb[g % 2], r_kTb[g % 2]
                                u += proj_fm_units(wk, rk, g * 128, kd_[:, tok0:],
                                                   lambda ti: rkd[(tok0 // 512) + ti], tok0, nkt, "qknorm",
                                                   gain_col=PC_QK + l * 4 + 3)
                            if hp == 0:
                                u += proj_v_units(wv_, rv, 0, 128, tok0 // 128, 16, vvb[0], r_vvb[0])
                        return u

                    for hp2 in (0, 2):
                        for hp in (hp2, hp2 + 1):
                            for u in proj_units(hp):
                                u()
                        for hp in (hp2, hp2 + 1):
                            b = hp % 2
                            if bname == "swa":
                                kb_ = (hp // 2) % 2
                                swa_fn = attention_swa if SWA_BATCHED else (
                                    lambda l_, hp_, *a: attention("swa", l_, hp_, *a))
                                its = swa_fn(l, hp, tok0, qTb[b], r_qTb[b], kTb[kb_], r_kTb[kb_],
                                             vvb[0], r_vvb[0])
                            else:
                                its = attention(bname, l, hp, tok0, qTb[b], r_qTb[b], kTb[b], r_kTb[b],
                                                vvb[b], r_vvb[b])
                            for (it, fns) in its:
                                pipe_push(it, fns)
                        pipe_flush()
                    if debug == "oT" and half == 1:
                        for c4 in range(4):
                            COPY("dve", dbgf[:], oT[:, c4, :], r_oT[c4], [r_dbg])
                            dbg_ids.append(S.dma("sp", dbg_d[:, c4 * 1024:(c4 + 1) * 1024], dbgf[:], reads=[r_dbg],
                                                 dst=r_dbg))
                    epilogue(bname, l, first, goff)
                    first = False
                mixer_out(l)

        out_ids = []
        r_out = [Res("yout%d" % i) for i in range(8)]
        for s in range(n_seq):
            for dc in range(8):
                S.dma("sp", xres[:, dc, :], x_d[s, dc], writes=r_xres[dc], dst=r_xres[dc][0])
                for t in range(1, 4):
                    r_xres[dc][t].w = r_xres[dc][0].w
            for l in layers:
                if "ffn1" in phases:
                    ffn(l, 0)
                if "mix" in phases:
                    mixer(l)
                if "ffn2" in phases:
                    ffn(l, 1)
            for dc in range(8):
                out_ids.append(S.dma("sp", y_d[s, dc], xres[:, dc, :], reads=r_xres[dc], dst=r_out[dc]))
        S.wait_all("sp", out_ids[-8:] + dbg_ids)
        S.emit_all()
        stats = dict(n_instr=S.n_instr, n_wait=S.n_wait, nsem=S.nsem)
    return nc, stats


def _t5_bucket_np(n):
    n = np.maximum(n, 0)
    max_exact = 16
    nf = np.maximum(n, 1).astype(np.float32)
    large = max_exact + (np.log(nf / np.float32(max_exact)) / np.float32(math.log(128 / max_exact))
                         * np.float32(32 - max_exact)).astype(np.int32)
    large = np.minimum(large, 31)
    return np.where(n < max_exact, n, large)


def _host_layout(inputs):
    f32 = np.float32
    prm = np.zeros((128, NP), f32)
    p = np.arange(128)
    norms = [inputs["ffn1_norm"], inputs["mix_norm"], inputs["ffn2_norm"]]
    for l in range(DEPTH):
        for which in range(3):
            g = np.asarray(norms[which][l], f32).reshape(8, 128)
            prm[:, PC_NORM + (l * 3 + which) * 8: PC_NORM + (l * 3 + which) * 8 + 8] = g.T
        for k, name in enumerate(["q_norm_diff", "k_norm_diff", "q_norm_swa", "k_norm_swa"]):
            prm[:, PC_QK + l * 4 + k] = np.asarray(inputs[name][l], f32)[p % 64]
        prm[:, PC_SINK + l * 8: PC_SINK + l * 8 + 8] = np.asarray(inputs["swa_sinks"][l], f32)[None, :]
        prm[:, PC_LAM + l * 256: PC_LAM + (l + 1) * 256] = np.asarray(inputs["diff_lambda"][l], f32).reshape(1, 256)
        prm[:, PC_SUBLN + l * 128: PC_SUBLN + (l + 1) * 128] = np.asarray(inputs["diff_subln"][l], f32)[None, :]
    rb = np.asarray(inputs["rel_bias"], f32)
    prm[:, PC_CFAR: PC_CFAR + 4] = rb[31, 0:4][None, :]
    i = np.arange(128)[:, None]
    j = np.arange(128)[None, :]
    d0 = j - i
    d1 = 128 + j - i
    b0 = _t5_bucket_np(d0)
    b1 = _t5_bucket_np(d1)
    bt = np.zeros((128, 12, 256), f32)
    for h in range(12):
        t0 = rb[b0, h]
        t0 = np.where(d0 >= 0, t0, f32(MASKV))
        t1 = rb[b1, h]
        if h >= 4:
            t1 = np.where(d1 < 128, t1, f32(MASKV))
        bt[:, h, 0:128] = t0
        bt[:, h, 128:256] = t1
    return prm, bt


_CACHE = {}


def kernel(**inputs):
    x = np.asarray(inputs["x"], np.float32)
    B = x.shape[0]
    prm, bt = _host_layout(inputs)
    if "nc" not in _CACHE:
        _CACHE["nc"] = build_program()[0]
    nc = _CACHE["nc"]
    shared = {k: np.ascontiguousarray(np.asarray(inputs[k], np.float32)) for k in
              ["ffn1_w_in", "ffn1_w_out", "ffn2_w_in", "ffn2_w_out", "w_in", "w_proj_sb", "w_proj_diff",
               "w_proj_swa", "w_out"]}
    shared["params"] = prm
    shared["btiles"] = bt
    in_maps = []
    for c in range(N_CORES):
        xs = x[c * SEQ_PER_CORE:(c + 1) * SEQ_PER_CORE]
        xs = xs[:, ::-1, :]
        xt = np.ascontiguousarray(xs.transpose(0, 2, 1)).reshape(SEQ_PER_CORE, 8, 128, SEQ)
        m = dict(shared)
        m["x"] = xt
        in_maps.append(m)
    res = run_bass_kernel_spmd(nc, in_maps, core_ids=list(range(N_CORES)))
    out = np.empty((B, SEQ, D_MODEL), np.float32)
    for c in range(N_CORES):
        y = np.asarray(res.results[c]["y"]).reshape(SEQ_PER_CORE, D_MODEL, SEQ)
        out[c * SEQ_PER_CORE:(c + 1) * SEQ_PER_CORE] = y.transpose(0, 2, 1)[:, ::-1, :]
    return out
```

```python
import contextlib
import math
import numpy as np
import concourse.bass as bass
import concourse.mybir as mybir
from concourse.bass_utils import run_bass_kernel_spmd

F32 = mybir.dt.float32
BF16 = mybir.dt.bfloat16
AF = mybir.ActivationFunctionType
ALU = mybir.AluOpType

D_MODEL = 1024
SEQ = 2048
DEPTH = 2
D_FF = 2816
NJ = D_FF // 128
IN_W = 6912
EPS = 1e-6
N_CORES = 8
SEQ_PER_CORE = 2
MASKV = -30000.0
SCALE = 0.125
SWA_BATCHED = False

OFF_QA, OFF_KA, OFF_VA = 0, 512, 1024
OFF_QD, OFF_KD, OFF_VD = 1536, 2048, 2560
OFF_QS, OFF_KS, OFF_VS = 3072, 3584, 3712
OFF_GA, OFF_GD, OFF_GS = 3840, 4864, 5888

PC_NORM = 0
PC_QK = 48
PC_SINK = 56
PC_CFAR = 72
PC_SUBLN = 76
PC_LAM = 332
NP = 844
NPR = 332


class Res:
    __slots__ = ("name", "w", "r", "sem", "semval")

    def __init__(self, name):
        self.name = name
        self.w = None
        self.r = {}
        self.sem = None
        self.semval = 0


class Sched:
    ROT = 30000
    ENGS = ("pe", "act", "dve", "pool", "sp")

    def __init__(self, nc, stack):
        self.nc = nc
        self.stack = stack
        self.ops = {e: [] for e in self.ENGS}
        self.cnt = {e: 0 for e in self.ENGS}
        self.epoch = {e: 0 for e in self.ENGS}
        self.waited = {e: {} for e in self.ENGS}
        self.semh = {}
        self.nsem = 0
        self.pending_noinc = {e: False for e in self.ENGS}
        self.n_instr = 0
        self.n_wait = 0

    def _sem(self, key):
        h = self.semh.get(key)
        if h is None:
            h = self.stack.enter_context(self.nc.semaphore("s%d" % self.nsem))
            self.nsem += 1
            self.semh[key] = h
        return h

    def _next_id(self, eng):
        if self.cnt[eng] >= self.ROT and not self.pending_noinc[eng]:
            self.epoch[eng] += 1
            self.cnt[eng] = 0
        return (("e", eng, self.epoch[eng]), self.cnt[eng] + 1)

    def _collect(self, eng, reads, writes):
        deps = {}

        def add(d):
            if d is None:
                return
            k, v = d
            if eng == "pe" and k[0] == "e" and k[1] == "pe":
                return
            if deps.get(k, 0) < v:
                deps[k] = v
        for r in reads:
            add(r.w)
        for w in writes:
            add(w.w)
            for d in w.r.items():
                add(d)
        out = []
        wd = self.waited[eng]
        for k, v in deps.items():
            if wd.get(k, 0) >= v:
                continue
            wd[k] = v
            out.append((k, v))
        return out

    def op(self, eng, emit, reads=(), writes=(), inc=True):
        waits = self._collect(eng, reads, writes)
        myid = self._next_id(eng)
        if inc:
            self.cnt[eng] += 1
            self.pending_noinc[eng] = False
        else:
            self.pending_noinc[eng] = True
        for r in reads:
            if r.r.get(myid[0], 0) < myid[1]:
                r.r[myid[0]] = myid[1]
        for w in writes:
            w.w = myid
            w.r = {}
        self.ops[eng].append((waits, emit, myid[0] if inc else None, 1))
        self.n_instr += 1
        self.n_wait += len(waits)

    def dma(self, queue, out_ap, in_ap, reads=(), writes=(), dst=None):
        waits = self._collect(queue, reads, writes)
        if dst.sem is None:
            dst.sem = ("d", dst.name, id(dst))
        dst.semval += 16
        myid = (dst.sem, dst.semval)
        for r in reads:
            if r.r.get(myid[0], 0) < myid[1]:
                r.r[myid[0]] = myid[1]
        for w in writes:
            w.w = myid
            w.r = {}
        self.ops[queue].append((waits, (lambda e: e.dma_start(out=out_ap, in_=in_ap)), dst.sem, 16))
        self.n_instr += 1
        self.n_wait += len(waits)
        return myid

    def wait_all(self, eng, ids):
        self.ops[eng].append((list(ids), None, None, 0))

    def emit_all(self):
        nc = self.nc
        for e in self.ENGS:
            for waits, emit, inck, incv in self.ops[e]:
                for k, v in waits:
                    self._sem(k)
                if inck is not None:
                    self._sem(inck)
        ops = self.ops
        semh = self.semh

        def run(ename, e):
            for waits, emit, inck, incv in ops[ename]:
                for k, v in waits:
                    e.wait_ge(semh[k], v)
                if emit is not None:
                    ins = emit(e)
                    if inck is not None:
                        ins.then_inc(semh[inck], incv)

        with nc.Block() as block:
            if ops["pe"]:
                @block.tensor
                def _(e):
                    run("pe", e)
            if ops["act"]:
                @block.scalar
                def _(e):
                    run("act", e)
            if ops["dve"]:
                @block.vector
                def _(e):
                    run("dve", e)
            if ops["pool"]:
                @block.gpsimd
                def _(e):
                    run("pool", e)
            if ops["sp"]:
                @block.sync
                def _(e):
                    run("sp", e)


def build_program(n_seq=SEQ_PER_CORE, layers=(0, 1), phases=("ffn1", "mix", "ffn2"),
                  branches=("sb", "diff", "swa"), GC=2, debug=None):
    nc = bass.Bass("TRN2", target_bir_lowering=False, dynamic_dma_scratch_size=512)
    dram = {}

    def din(name, shape):
        dram[name] = nc.dram_tensor(name, list(shape), F32, kind="ExternalInput").ap()
        return dram[name]

    x_d = din("x", [n_seq, 8, 128, SEQ])
    w_ffn_in = [din("ffn1_w_in", [DEPTH, D_MODEL, 2 * D_FF]), din("ffn2_w_in", [DEPTH, D_MODEL, 2 * D_FF])]
    w_ffn_out = [din("ffn1_w_out", [DEPTH, D_FF, D_MODEL]), din("ffn2_w_out", [DEPTH, D_FF, D_MODEL])]
    w_in_d = din("w_in", [DEPTH, D_MODEL, IN_W])
    w_proj_d = {"sb": din("w_proj_sb", [DEPTH, 512, D_MODEL]),
                "diff": din("w_proj_diff", [DEPTH, 512, D_MODEL]),
                "swa": din("w_proj_swa", [DEPTH, 512, D_MODEL])}
    w_out_d = din("w_out", [DEPTH, D_MODEL, D_MODEL])
    params_d = din("params", [128, NP])
    btiles_d = din("btiles", [128, 12, 256])
    y_d = nc.dram_tensor("y", [n_seq, 8, 128, SEQ], F32, kind="ExternalOutput").ap()
    dbg_d = nc.dram_tensor("dbg", [128, 4096], F32, kind="ExternalOutput").ap() if debug else None

    with contextlib.ExitStack() as st:
        S = Sched(nc, st)

        def sb(name, shape, dt):
            return st.enter_context(nc.sbuf_tensor(name, list(shape), dt))

        def ps(name, shape, dt):
            return st.enter_context(nc.psum_tensor(name, list(shape), dt))

        xres = sb("xres", [128, 8, SEQ], F32)
        r_xres = [[Res("xres%d_%d" % (c, t)) for t in range(4)] for c in range(8)]
        xn = sb("xn", [128, 8, SEQ], BF16)
        r_xn = [[Res("xn%d_%d" % (c, t)) for t in range(4)] for c in range(8)]
        mrg = sb("mrg", [128, 8192], BF16)
        r_mrg = [Res("mrg%d" % i) for i in range(16)]
        oT = sb("oT", [128, 4, 1024], BF16)
        r_oT = [[Res("oT%d_%d" % (c, q)) for q in range(8)] for c in range(4)]
        Wall = sb("Wall", [128, 6 * 2048], BF16)
        r_W = [Res("W%d" % i) for i in range(6)]
        qTb = [sb("qT%d" % b, [128, 1024], BF16) for b in range(2)]
        r_qTb = [[Res("qT%d_%d" % (b, i)) for i in range(2)] for b in range(2)]
        kTb = [sb("kT%d" % b, [128, SEQ], BF16) for b in range(2)]
        r_kTb = [[Res("kT%d_%d" % (b, i)) for i in range(4)] for b in range(2)]
        vvb = [sb("vv%d" % b, [128, 16, 128], BF16) for b in range(2)]
        r_vvb = [[Res("vv%d_%d" % (b, i)) for i in range(4)] for b in range(2)]
        NSTG = 3
        stg = [sb("stg%d" % i, [128, 512], F32) for i in range(NSTG)]
        r_stg = [Res("stg%d" % i) for i in range(NSTG)]
        el = [sb("el%d" % i, [128, 1024], F32) for i in range(2)]
        r_el = [Res("el%d" % i) for i in range(2)]
        ca = [sb("ca%d" % i, [128, 1024], F32) for i in range(2)]
        r_ca = [Res("ca%d" % i) for i in range(2)]
        wb = [sb("wb%d" % i, [128, 1024], BF16) for i in range(2)]
        r_wb = [Res("wb%d" % i) for i in range(2)]
        wT = [sb("wT%d" % i, [128, 1024], BF16) for i in range(2)]
        r_wT = [Res("wT%d" % i) for i in range(2)]
        sq = [sb("sq%d" % i, [128, 512], BF16) for i in range(2)]
        r_sq = [Res("sq%d" % i) for i in range(2)]
        sg = [sb("sg%d" % i, [128, 512], F32) for i in range(2)]
        r_sg = [Res("sg%d" % i) for i in range(2)]
        rstd = sb("rstd", [128, 512], F32)
        r_rstd = Res("rstd")
        osb = [sb("osb%d" % i, [128, 128], BF16) for i in range(2)]
        r_osb = [Res("osb%d" % i) for i in range(2)]
        osbw = [sb("osbw%d" % i, [128, 256], BF16) for i in range(2)]
        r_osbw = [Res("osbw%d" % i) for i in range(2)]
        of32 = [sb("of32_%d" % i, [128, 128], F32) for i in range(2)]
        r_of32 = [Res("of32_%d" % i) for i in range(2)]
        btl = sb("btl", [128, 12, 256], F32)
        r_btl = Res("btl")
        prm = sb("prm", [128, NPR], F32)
        r_prm = Res("prm")
        ones_bf = sb("ones_bf", [128, 128], BF16)
        bones_bf = sb("bones_bf", [128, 128], BF16)
        ident_bf = sb("ident_bf", [128, 128], BF16)
        onesrow = sb("onesrow", [128, 1024], BF16)
        r_const = Res("const")
        small = sb("small", [128, 64], F32)
        r_small = [Res("small%d" % i) for i in range(64)]
        lamt = sb("lamt", [128, 8], F32)
        r_lamt = Res("lamt")
        esink = sb("esink", [128, 16], F32)
        subg = sb("subg", [128, 2, 128], F32)
        tmp64 = sb("tmp64", [128, 64], F32)
        sqj = sb("sqj", [128, 128], F32)
        r_tmp64 = Res("tmp64")
        carry = [sb("carry%d" % i, [128, 1], F32) for i in range(2)]
        r_carry = [Res("carry%d" % i) for i in range(2)]

        dbgf = sb("dbgf", [128, 1024], F32) if debug else None
        r_dbg = Res("dbg")
        dbg_ids = []
        ZA = ps("ZA", [128, 1024], F32)
        ZB = ps("ZB", [128, 1024], F32)
        r_Z = [[Res("ZA0"), Res("ZA1")], [Res("ZB0"), Res("ZB1")]]
        Zt = [ZA, ZB]
        Dbanks = [(ZA, 0, r_Z[0][0]), (ZA, 512, r_Z[0][1]), (ZB, 0, r_Z[1][0]), (ZB, 512, r_Z[1][1])]
        Tt = [ps("T0", [128, 1024], BF16), ps("T1", [128, 1024], BF16)]
        r_T = [Res("T0"), Res("T1")]
        Ot = [ps("O0", [128, 512], F32), ps("O1", [128, 512], F32)]
        r_O = [Res("O0"), Res("O1")]

        state = {"d": 0, "o": 0, "stg": 0, "sq": 0, "sg": 0}

        def next_d():
            i = state["d"]
            state["d"] = (i + 1) % 4
            t, off, r = Dbanks[i]
            return t[:, off:off + 512], r

        def next_o():
            i = state["o"]
            state["o"] = (i + 1) % 2
            return Ot[i], r_O[i]

        def ACT(out, in_, func, reads, writes, **kw):
            S.op("act", lambda e: e.activation(out=out, in_=in_, func=func, **kw), reads, writes)

        def MM(out, lhsT, rhs, start, stop, reads, writes, inc):
            S.op("pe", lambda e: e.matmul(out, lhsT=lhsT, rhs=rhs, start=start, stop=stop),
                 reads, writes, inc=inc)

        def TR(out, in_, reads, writes, inc=True):
            S.op("pe", lambda e: e.transpose(out, in_, ident_bf[:]), reads, writes, inc=inc)

        def STT(eng, out, in0, scalar, in1, op0, op1, reads, writes):
            S.op(eng, lambda e: e.scalar_tensor_tensor(out=out, in0=in0, scalar=scalar, in1=in1, op0=op0, op1=op1),
                 reads, writes)

        def TT(eng, out, in0, in1, op, reads, writes):
            S.op(eng, lambda e: e.tensor_tensor(out=out, in0=in0, in1=in1, op=op), reads, writes)

        def TS(eng, out, in0, s1, s2, op0, op1, reads, writes):
            S.op(eng, lambda e: e.tensor_scalar(out=out, in0=in0, scalar1=s1, scalar2=s2, op0=op0, op1=op1),
                 reads, writes)

        def COPY(eng, out, in_, reads, writes):
            S.op(eng, lambda e: e.tensor_copy(out=out, in_=in_), reads, writes)

        def load_w(dst_ap, src_ap, n, dst_res, extra_dst=None):
            i = state["stg"]
            state["stg"] = (i + 1) % NSTG
            S.dma("sp", stg[i][:, :n], src_ap, writes=[r_stg[i]], dst=r_stg[i])
            TS("pool", dst_ap, stg[i][:, :n], 1.0, 0.0, ALU.mult, ALU.add, [r_stg[i]], dst_res)
            if extra_dst is not None:
                COPY("pool", extra_dst, stg[i][:, :n], [r_stg[i]], dst_res)

        def wview(r0, a, b):
            return Wall[:, r0 * 2048: r0 * 2048 + a * b].rearrange("p (a b) -> p a b", a=a)

        def pcol(c, n=1):
            return prm[:, c:c + n]

        S.dma("sp", prm[:], params_d[:, 0:NPR], writes=[r_prm], dst=r_prm)
        S.dma("sp", el[0][:, 0:512], params_d[:, PC_LAM:PC_LAM + 512], writes=[r_el[0]], dst=r_el[0])
        S.dma("sp", btl[:], btiles_d, writes=[r_btl], dst=r_btl)
        S.op("pool", lambda e: e.memset(ones_bf[:], 1.0), [], [r_const])
        S.op("pool", lambda e: e.memset(onesrow[:], 1.0), [], [r_const])
        S.op("pool", lambda e: e.affine_select(out=ident_bf[:], in_=ones_bf[:], pattern=[[1, 128]],
                                               compare_op=ALU.is_equal, fill=0.0, base=0,
                                               channel_multiplier=-1), [r_const], [r_const])
        S.op("pool", lambda e: e.memset(bones_bf[:], 0.0), [], [r_const])
        S.op("pool", lambda e: e.memset(bones_bf[0:64, 0:64], 1.0), [], [r_const])
        S.op("pool", lambda e: e.memset(bones_bf[64:128, 64:128], 1.0), [], [r_const])
        for l in range(DEPTH):
            lam_init = 0.8 - 0.6 * math.exp(-0.3 * l)
            base = l * 256
            for k in range(2):
                TT("dve", tmp64[:], el[0][:, base + 128 * k: base + 128 * k + 64],
                   el[0][:, base + 128 * k + 64: base + 128 * k + 128], ALU.mult,
                   [r_el[0]], [r_tmp64])
                S.op("dve", lambda e, k=k, l=l: e.reduce_sum(out=small[:, 2 * l + k:2 * l + k + 1], in_=tmp64[:],
                                                             axis=mybir.AxisListType.X),
                     [r_tmp64], [r_small[2 * l + k]])
                ACT(small[:, 2 * l + k:2 * l + k + 1], small[:, 2 * l + k:2 * l + k + 1], AF.Exp,
                    [r_small[2 * l + k]], [r_small[2 * l + k]])
            STT("dve", lamt[:, l:l + 1], small[:, 2 * l:2 * l + 1], float(lam_init), small[:, 2 * l + 1:2 * l + 2],
                ALU.add, ALU.subtract, [r_small[2 * l], r_small[2 * l + 1]], [r_lamt])
            TS("dve", subg[:, l, :], pcol(PC_SUBLN + l * 128, 128), float(1.0 - lam_init), None, ALU.mult, ALU.bypass,
               [r_prm], [r_lamt])
        ACT(esink[:], pcol(PC_SINK, 16), AF.Exp, [r_prm], [r_lamt])

        def rmsnorm_to_xn(gcol):
            for tt in range(4):
                tsl = slice(tt * 512, (tt + 1) * 512)
                ob, r_ob = next_o()
                for dc in range(8):
                    i = state["sq"]
                    state["sq"] = (i + 1) % 2
                    ACT(sq[i][:], xres[:, dc, tsl], AF.Square, [r_xres[dc][tt]], [r_sq[i]])
                    MM(ob[:], ones_bf[:], sq[i][:], dc == 0, dc == 7, [r_sq[i], r_const], [r_ob], inc=True)
                ACT(rstd[:], ob[:], AF.Ln, [r_ob], [r_rstd], scale=1.0 / D_MODEL, bias=EPS)
                ACT(rstd[:], rstd[:], AF.Exp, [r_rstd], [r_rstd], scale=-0.5)
                for dc in range(8):
                    STT("dve", xn[:, dc, tsl], xres[:, dc, tsl], pcol(gcol + dc), rstd[:], ALU.mult, ALU.mult,
                        [r_xres[dc][tt], r_rstd, r_prm], [r_xn[dc][tt]])

        def ffn(l, which):
            w_in_ap = w_ffn_in[which][l]
            w_out_ap = w_ffn_out[which][l]
            rmsnorm_to_xn(PC_NORM + (l * 3 + (0 if which == 0 else 2)) * 8)
            groups = [(j0, min(j0 + GC, NJ)) for j0 in range(0, NJ, GC)]
            ob4 = [(Ot[0][:], r_O[0]), (Ot[1][:], r_O[1]),
                   (Tt[0][:].bitcast(F32), r_T[0]), (Tt[1][:].bitcast(F32), r_T[1])]
            ost = {"i": 0}

            def views(gi):
                buf = gi % 2
                return (wview(buf * 3 + 0, 8, 256), wview(buf * 3 + 1, 8, 256), wview(buf * 3 + 2, 2, 1024),
                        r_W[buf * 3 + 0], r_W[buf * 3 + 1], r_W[buf * 3 + 2],
                        mrg[:, buf * 4096: buf * 4096 + 4096].rearrange("p (a b) -> p a b", a=2), buf)

            def load_group(gi):
                j0, j1 = groups[gi]
                n = j1 - j0
                wg, wu, wo, rg, ru, ro, actb, buf = views(gi)
                for dc in range(8):
                    load_w(wg[:, dc, :n * 128], w_in_ap[dc * 128:(dc + 1) * 128, j0 * 128:j1 * 128], n * 128, [rg])
                    load_w(wu[:, dc, :n * 128],
                           w_in_ap[dc * 128:(dc + 1) * 128, D_FF + j0 * 128:D_FF + j1 * 128], n * 128, [ru])
                for jj in range(n):
                    for hh in range(2):
                        load_w(wo[:, jj, hh * 512:(hh + 1) * 512],
                               w_out_ap[(j0 + jj) * 128:(j0 + jj + 1) * 128, hh * 512:(hh + 1) * 512], 512, [ro])

            def win_unit(gi, jj, tt):
                wg, wu, wo, rg, ru, ro, actb, buf = views(gi)
                tsl = slice(tt * 512, (tt + 1) * 512)
                r_act = r_mrg[buf * 8 + jj * 4 + tt]
                hg, r_hg = next_d()
                hu, r_hu = next_d()
                for dc in range(8):
                    MM(hg, wg[:, dc, jj * 128:(jj + 1) * 128], xn[:, dc, tsl], dc == 0, dc == 7,
                       [rg, r_xn[dc][tt]], [r_hg], inc=(dc == 7))
                for dc in range(8):
                    MM(hu, wu[:, dc, jj * 128:(jj + 1) * 128], xn[:, dc, tsl], dc == 0, dc == 7,
                       [ru, r_xn[dc][tt]], [r_hu], inc=(dc == 7))
                i = state["sg"]
                state["sg"] = (i + 1) % 2
                ACT(sg[i][:], hg, AF.Silu, [r_hg], [r_sg[i]])
                TT("dve", actb[:, jj, tsl], sg[i][:], hu, ALU.mult, [r_sg[i], r_hu], [r_act])

            def wout_unit(gi, c, tt):
                j0, j1 = groups[gi]
                n = j1 - j0
                wg, wu, wo, rg, ru, ro, actb, buf = views(gi)
                tsl = slice(tt * 512, (tt + 1) * 512)
                ob, r_ob = ob4[ost["i"]]
                ost["i"] = (ost["i"] + 1) % 4
                for jj in range(n):
                    MM(ob, wo[:, jj, c * 128:(c + 1) * 128], actb[:, jj, tsl], jj == 0, jj == n - 1,
                       [ro, r_mrg[buf * 8 + jj * 4 + tt]], [r_ob], inc=(jj == n - 1))
                STT("dve", xres[:, c, tsl], ob, 0.5, xres[:, c, tsl], ALU.mult, ALU.add,
                    [r_ob, r_xres[c][tt]], [r_xres[c][tt]])

            pending = []
            for gi in range(len(groups)):
                j0, j1 = groups[gi]
                n = j1 - j0
                load_group(gi)
                wins = [(jj, tt) for jj in range(n) for tt in range(4)]
                per = (len(pending) + len(wins) - 1) // len(wins) if pending else 0
                for (jj, tt) in wins:
                    win_unit(gi, jj, tt)
                    for _ in range(per):
                        if pending:
                            g2, c, t2 = pending.pop(0)
                            wout_unit(g2, c, t2)
                while pending:
                    g2, c, t2 = pending.pop(0)
                    wout_unit(g2, c, t2)
                pending = [(gi, c, tt) for c in range(8) for tt in range(4)]
            while pending:
                g2, c, t2 = pending.pop(0)
                wout_unit(g2, c, t2)

        def proj_fm_units(wv_, rw, col0, dst, r_dst_fn, tok0, ntile, mode, gain_col=None, alt=0):
            def unit(ti):
                t0 = tok0 + ti * 512
                tt = t0 // 512
                d, r_d = next_d()
                for dc in range(8):
                    MM(d, wv_[:, dc, col0:col0 + 128], xn[:, dc, t0:t0 + 512], dc == 0, dc == 7,
                       [r_xn[dc][tt]] + rw, [r_d], inc=(dc == 7))
                dsl = dst[:, ti * 512:(ti + 1) * 512]
                rd = r_dst_fn(ti)
                if mode == "copy":
                    if (ti + alt) % 2 == 0:
                        ACT(dsl, d, AF.Copy, [r_d], [rd])
                    else:
                        COPY("dve", dsl, d, [r_d], [rd])
                else:
                    i = state["sq"]
                    state["sq"] = (i + 1) % 2
                    ACT(sq[i][:], d, AF.Square, [r_d], [r_sq[i]])
                    d2, r_d2 = next_d()
                    MM(d2, bones_bf[:], sq[i][:], True, True, [r_sq[i], r_const], [r_d2], inc=True)
                    j = state["sg"]
                    state["sg"] = (j + 1) % 2
                    ACT(sg[j][:], d2, AF.Ln, [r_d2], [r_sg[j]], scale=1.0 / 64.0, bias=EPS)
                    ACT(sg[j][:], sg[j][:], AF.Exp, [r_sg[j]], [r_sg[j]], scale=-0.5)
                    STT("dve", dsl, d, pcol(gain_col), sg[j][:], ALU.mult, ALU.mult, [r_d, r_sg[j], r_prm], [rd])
            return [(lambda ti=ti: unit(ti)) for ti in range(ntile)]

        def proj_v_units(wv_, rw, col0, ncol, kb0, kb1, vv, r_vv):
            def unit(g0):
                d, r_d = next_d()
                nb = min(4, kb1 - g0)
                for bi in range(nb):
                    kb = g0 + bi
                    for dc in range(8):
                        MM(d[:, bi * 128: bi * 128 + ncol], xn[:, dc, kb * 128:(kb + 1) * 128],
                           wv_[:, dc, col0:col0 + ncol], dc == 0, dc == 7,
                           [r_xn[dc][kb // 4]] + rw, [r_d], inc=(dc == 7 and bi == nb - 1))
                src = d.rearrange("p (a b) -> p a b", a=4)[:, :nb, :ncol]
                COPY("dve", vv[:, g0:g0 + nb, :ncol], src, [r_d], [r_vv[g0 // 4]])
            return [(lambda g0=g0: unit(g0)) for g0 in range(kb0, kb1, 4)]

        pipe_items = []

        def pipe_run_step(k):
            for off, name in ((2, "A"), (1, "B1"), (0, "B2"), (-1, "T"), (-2, "CP"), (-3, "PV")):
                j = k + off
                if 0 <= j < len(pipe_items):
                    it, fns = pipe_items[j]
                    fns[name](it, j % 2)

        def pipe_push(it, fns):
            pipe_items.append((it, fns))
            pipe_run_step(len(pipe_items) - 3)

        def pipe_flush():
            g = len(pipe_items)
            for k in range(g - 2, g + 3):
                pipe_run_step(k)
            del pipe_items[:]

        def attention(kind, l, hp, q0, qT, r_qT, kT, r_kT, vv, r_vv):
            items = []
            for qi in range(8):
                qb = q0 // 128 + qi
                kstart = qb * 128
                kend = SEQ if kind != "swa" else min(SEQ, kstart + 256)
                segs = []
                k0 = kstart
                while k0 < kend:
                    n = min(1024, kend - k0)
                    segs.append((k0, n))
                    k0 += n
                if kind == "diff":
                    subs = [0, 1]
                else:
                    subs = [0, 1]
                for sidx, sub in enumerate(subs):
                    for si, (k0, n) in enumerate(segs):
                        items.append(dict(qi=qi, qb=qb, sub=sub, si=si, k0=k0, n=n, nseg=len(segs),
                                          first_of_qb=(sidx == 0 and si == 0),
                                          last_of_qb=(sidx == len(subs) - 1 and si == len(segs) - 1)))
            ostate = {}

            def stage_A(it, ib):
                base = it["sub"] * 64
                qsl = qT[base:base + 64, it["qi"] * 128:(it["qi"] + 1) * 128]
                n, k0 = it["n"], it["k0"]
                for c0 in range(0, n, 512):
                    cn = min(512, n - c0)
                    MM(Zt[ib][:, c0:c0 + cn], qsl, kT[base:base + 64, k0 + c0:k0 + c0 + cn], True, True,
                       [r_qT[it["qi"] // 4]] + r_kT[(k0 + c0) // 512:(k0 + c0 + cn - 1) // 512 + 1], [r_Z[ib][c0 // 512]], inc=True)

            def scols(it):
                return 8 + it["sub"] * 4 + (it["qi"] % 4) * 8

            def stage_B1(it, ib):
                n, k0, si = it["n"], it["k0"], it["si"]
                rz = r_Z[ib][:(n + 511) // 512]
                Z = Zt[ib]
                if kind == "sb":
                    ACT(el[ib][:, :n], Z[:, :n], AF.Exp, rz, [r_el[ib]], scale=SCALE)
                    ACT(el[ib][:, :n], el[ib][:, :n], AF.Ln, [r_el[ib]], [r_el[ib]], bias=1.0)
                    if si == 0:
                        S.op("pool", lambda e: e.affine_select(out=el[ib][:, 0:128], in_=el[ib][:, 0:128],
                                                               pattern=[[1, 128]], compare_op=ALU.is_gt, fill=0.0,
                                                               base=0, channel_multiplier=-1),
                             [r_el[ib]], [r_el[ib]])
                        init = 0.0
                        rinit = []
                    else:
                        init = carry[it["sub"]][:, 0:1]
                        rinit = [r_carry[it["sub"]]]
                    S.op("dve", lambda e: e.tensor_tensor_scan(out=ca[ib][:, :n], data0=onesrow[:, :n],
                                                               data1=el[ib][:, :n], initial=init,
                                                               op0=ALU.mult, op1=ALU.add),
                         [r_el[ib], r_const] + rinit, [r_ca[ib]])
                    if si < it["nseg"] - 1:
                        COPY("dve", carry[it["sub"]][:, 0:1], ca[ib][:, n - 1:n], [r_ca[ib]], [r_carry[it["sub"]]])
                    STT("dve", el[ib][:, :n], Z[:, :n], SCALE, ca[ib][:, :n], ALU.mult, ALU.subtract,
                        rz + [r_ca[ib]], [r_el[ib]])
                else:
                    head = hp if kind == "diff" else 4 + hp * 2 + it["sub"]
                    bt = btl[:, head, :]
                    if si == 0:
                        nn = min(256, n)
                        STT("dve", el[ib][:, :nn], Z[:, :nn], SCALE, bt[:, :nn], ALU.mult, ALU.add,
                            rz[:1] + [r_btl], [r_el[ib]])
                        if n > nn:
                            TS("dve", el[ib][:, nn:n], Z[:, nn:n], SCALE, pcol(PC_CFAR + head), ALU.mult, ALU.add,
                               rz + [r_prm], [r_el[ib]])
                    else:
                        TS("dve", el[ib][:, :n], Z[:, :n], SCALE, pcol(PC_CFAR + head), ALU.mult, ALU.add,
                           rz + [r_prm], [r_el[ib]])

            def stage_B2(it, ib):
                n, si = it["n"], it["si"]
                if kind == "sb":
                    ACT(wb[ib][:, :n], el[ib][:, :n], AF.Exp, [r_el[ib]], [r_wb[ib]])
                    if si == 0:
                        S.op("pool", lambda e: e.affine_select(out=wb[ib][:, 0:128], in_=wb[ib][:, 0:128],
                                                               pattern=[[1, 128]], compare_op=ALU.is_gt, fill=0.0,
                                                               base=0, channel_multiplier=-1),
                             [r_wb[ib]], [r_wb[ib]])
                else:
                    scol0 = scols(it)
                    if si == 0:
                        nn = min(256, n)
                        S.op("act", lambda e: e.activation(out=wb[ib][:, :nn], in_=el[ib][:, :nn], func=AF.Exp,
                                                           accum_out=small[:, scol0:scol0 + 1]),
                             [r_el[ib]], [r_wb[ib], r_small[scol0]])
                        if n > nn:
                            S.op("act", lambda e: e.activation(out=wb[ib][:, nn:n], in_=el[ib][:, nn:n], func=AF.Exp,
                                                               accum_out=small[:, scol0 + 1:scol0 + 2]),
                                 [r_el[ib]], [r_wb[ib], r_small[scol0 + 1]])
                    else:
                        S.op("act", lambda e: e.activation(out=wb[ib][:, :n], in_=el[ib][:, :n], func=AF.Exp,
                                                           accum_out=small[:, scol0 + 1 + si:scol0 + 2 + si]),
                             [r_el[ib]], [r_wb[ib], r_small[scol0 + 1 + si]])

            def stage_T(it, ib):
                nb = it["n"] // 128
                for jb in range(nb):
                    TR(Tt[ib][:, jb * 128:(jb + 1) * 128], wb[ib][:, jb * 128:(jb + 1) * 128],
                       [r_wb[ib], r_const], [r_T[ib]], inc=(jb == nb - 1))

            def stage_CP(it, ib):
                n = it["n"]
                ACT(wT[ib][:, :n], Tt[ib][:, :n], AF.Copy, [r_T[ib]], [r_wT[ib]])

            def stage_PV(it, ib):
                n, k0 = it["n"], it["k0"]
                nb = n // 128
                qi = it["qi"]
                if it["first_of_qb"]:
                    if kind == "diff":
                        ostate["o"] = (0, 1)
                    else:
                        oi = state["o"]
                        state["o"] = (oi + 1) % 2
                        ostate["o"] = (oi, oi)
                if kind == "diff":
                    oi = ostate["o"][it["sub"]]
                    ocols = slice(0, 128)
                    vcols = slice(0, 128)
                elif kind == "sb":
                    oi = ostate["o"][0]
                    ocols = slice(it["sub"] * 64, it["sub"] * 64 + 64)
                    vcols = ocols
                else:
                    oi = ostate["o"][0]
                    ocols = slice(it["sub"] * 64, it["sub"] * 64 + 64)
                    g = hp // 2
                    vcols = slice(g * 64, g * 64 + 64)
                for jb in range(nb):
                    kb = k0 // 128 + jb
                    MM(Ot[oi][:, ocols], wT[ib][:, jb * 128:(jb + 1) * 128], vv[:, kb, vcols],
                       it["si"] == 0 and jb == 0, it["si"] == it["nseg"] - 1 and jb == nb - 1,
                       [r_wT[ib], r_vv[kb // 4]], [r_O[oi]], inc=(jb == nb - 1))
                if kind == "swa":
                    sc = scols(it)
                    head = hp * 2 + it["sub"]
                    fb = qi % 2
                    TT("dve", small[:, sc + 2:sc + 3], small[:, sc:sc + 1], esink[:, l * 8 + head:l * 8 + head + 1],
                       ALU.add, [r_small[sc], r_lamt], [r_small[sc + 2]])
                    S.op("dve", lambda e: e.reciprocal(out=small[:, sc + 2:sc + 3], in_=small[:, sc + 2:sc + 3]),
                         [r_small[sc + 2]], [r_small[sc + 2]])
                    ACT(osb[fb][:, ocols], Ot[oi][:, ocols], AF.Copy, [r_O[oi], r_small[sc + 2]], [r_osb[fb]],
                        scale=small[:, sc + 2:sc + 3])
                if it["last_of_qb"]:
                    stage_F(it, qi)

            def stage_F(it, qi):
                fb = qi % 2
                oi = ostate["o"][0]
                if kind == "sb":
                    ACT(osb[fb][:], Ot[oi][:, 0:128], AF.Copy, [r_O[oi]], [r_osb[fb]])
                elif kind == "diff":
                    nseg = it["nseg"]
                    for c in range(2):
                        sc = 8 + c * 4 + (qi % 4) * 8
                        tot = 40 + c
                        n0 = it_n0[qi]
                        cols = [sc]
                        if n0 > 256:
                            cols.append(sc + 1)
                        if nseg > 1:
                            cols.append(sc + 2)
                        if len(cols) == 1:
                            COPY("dve", small[:, tot:tot + 1], small[:, cols[0]:cols[0] + 1], [r_small[cols[0]]],
                                 [r_small[tot]])
                        else:
                            TT("dve", small[:, tot:tot + 1], small[:, cols[0]:cols[0] + 1],
                               small[:, cols[1]:cols[1] + 1], ALU.add, [r_small[cols[0]], r_small[cols[1]]],
                               [r_small[tot]])
                            if len(cols) == 3:
                                TT("dve", small[:, tot:tot + 1], small[:, tot:tot + 1],
                                   small[:, cols[2]:cols[2] + 1], ALU.add, [r_small[tot], r_small[cols[2]]],
                                   [r_small[tot]])
                        S.op("dve", lambda e, tot=tot: e.reciprocal(out=small[:, tot:tot + 1], in_=small[:, tot:tot + 1]),
                             [r_small[tot]], [r_small[tot]])
                    STT("dve", small[:, 42:43], small[:, 41:42], -1.0, lamt[:, l:l + 1], ALU.mult, ALU.mult,
                        [r_small[41], r_lamt], [r_small[42]])
                    ACT(of32[fb][:], Ot[0][:, 0:128], AF.Copy, [r_O[0], r_small[40]], [r_of32[fb]],
                        scale=small[:, 40:41])
                    STT("dve", of32[fb][:], Ot[1][:, 0:128], small[:, 42:43], of32[fb][:], ALU.mult, ALU.add,
                        [r_O[1], r_small[42], r_of32[fb]], [r_of32[fb]])
                    S.op("act", lambda e: e.activation(out=sqj[:], in_=of32[fb][:],
                                                       func=AF.Square, accum_out=small[:, 43:44]),
                         [r_of32[fb]], [r_tmp64, r_small[43]])
                    ACT(small[:, 43:44], small[:, 43:44], AF.Ln, [r_small[43]], [r_small[43]],
                        scale=1.0 / 128.0, bias=EPS)
                    ACT(small[:, 43:44], small[:, 43:44], AF.Exp, [r_small[43]], [r_small[43]], scale=-0.5)
                    STT("dve", osb[fb][:], of32[fb][:], small[:, 43:44], subg[:, l, :], ALU.mult, ALU.mult,
                        [r_of32[fb], r_small[43], r_lamt], [r_osb[fb]])
                tdst = Ot[oi][:, 256:320].bitcast(BF16)
                TR(tdst, osb[fb][:], [r_osb[fb], r_const], [r_O[oi]], inc=True)
                COPY("dve", oT[:, hp, qi * 128:(qi + 1) * 128], tdst, [r_O[oi]], [r_oT[hp][qi]])

            it_n0 = {}
            for it in items:
                if it["si"] == 0:
                    it_n0[it["qi"]] = it["n"]
            fns = dict(A=stage_A, B1=stage_B1, B2=stage_B2, T=stage_T, CP=stage_CP, PV=stage_PV)
            return [(it, fns) for it in items]

        swa_ctr = {"i": 0}

        def attention_swa(l, hp, q0, qT, r_qT, kT, r_kT, vv, r_vv):
            g = hp // 2
            items = []
            for pair in range(4):
                combos = []
                for q in range(2):
                    qi = pair * 2 + q
                    qb = q0 // 128 + qi
                    w = 256 if qb < 15 else 128
                    for sub in range(2):
                        combos.append(dict(qi=qi, qb=qb, sub=sub, w=w, c=q * 2 + sub))
                items.append(dict(pair=pair, combos=combos, idx=swa_ctr["i"]))
                swa_ctr["i"] += 1
            octx = {}

            def sA(it, ib):
                for cb in it["combos"]:
                    base = cb["sub"] * 64
                    c, w, qi, k0 = cb["c"], cb["w"], cb["qi"], cb["qb"] * 128
                    MM(Zt[ib][:, c * 256:c * 256 + w], qT[base:base + 64, qi * 128:(qi + 1) * 128],
                       kT[base:base + 64, k0:k0 + w], True, True,
                       [r_qT[qi // 4]] + r_kT[k0 // 512:(k0 + w - 1) // 512 + 1], [r_Z[ib][c // 2]], inc=True)

            def sB1(it, ib):
                for cb in it["combos"]:
                    c, w = cb["c"], cb["w"]
                    head = 4 + hp * 2 + cb["sub"]
                    STT("dve", el[ib][:, c * 256:c * 256 + w], Zt[ib][:, c * 256:c * 256 + w], SCALE,
                        btl[:, head, 0:w], ALU.mult, ALU.add, [r_Z[ib][c // 2], r_btl], [r_el[ib]])

            def sB2(it, ib):
                sc = 8 + (it["idx"] % 4) * 8
                for cb in it["combos"]:
                    c, w = cb["c"], cb["w"]
                    S.op("act", lambda e, c=c, w=w: e.activation(out=wb[ib][:, c * 256:c * 256 + w],
                                                                 in_=el[ib][:, c * 256:c * 256 + w], func=AF.Exp,
                                                                 accum_out=small[:, sc + c:sc + c + 1]),
                         [r_el[ib]], [r_wb[ib], r_small[sc + c]])
                    if w < 256:
                        S.op("pool", lambda e, c=c, w=w: e.memset(wb[ib][:, c * 256 + w:(c + 1) * 256], 0.0),
                             [], [r_wb[ib]])

            def sT(it, ib):
                for jb in range(8):
                    TR(Tt[ib][:, jb * 128:(jb + 1) * 128], wb[ib][:, jb * 128:(jb + 1) * 128],
                       [r_wb[ib], r_const], [r_T[ib]], inc=(jb == 7))

            def sCP(it, ib):
                ACT(wT[ib][:, :], Tt[ib][:, :], AF.Copy, [r_T[ib]], [r_wT[ib]])

            def sPV(it, ib):
                oi = state["o"]
                state["o"] = (oi + 1) % 2
                sc = 8 + (it["idx"] % 4) * 8
                fb = it["idx"] % 2
                for cb in it["combos"]:
                    c, w, qb = cb["c"], cb["w"], cb["qb"]
                    nb = w // 128
                    for jb in range(nb):
                        kb = qb + jb
                        MM(Ot[oi][:, c * 64:(c + 1) * 64], wT[ib][:, c * 256 + jb * 128:c * 256 + (jb + 1) * 128],
                           vv[:, kb, g * 64:(g + 1) * 64], jb == 0, jb == nb - 1,
                           [r_wT[ib], r_vv[kb // 4]], [r_O[oi]], inc=(jb == nb - 1))
                es = esink[:, l * 8 + hp * 2:l * 8 + hp * 2 + 2]
                for q in range(2):
                    TT("dve", small[:, sc + 4 + q * 2:sc + 6 + q * 2], small[:, sc + q * 2:sc + q * 2 + 2], es, ALU.add,
                       [r_small[sc + q * 2], r_small[sc + q * 2 + 1], r_lamt],
                       [r_small[sc + 4 + q * 2], r_small[sc + 5 + q * 2]])
                S.op("dve", lambda e: e.reciprocal(out=small[:, sc + 4:sc + 8], in_=small[:, sc + 4:sc + 8]),
                     [r_small[sc + 4 + j] for j in range(4)], [r_small[sc + 4 + j] for j in range(4)])
                for c in range(4):
                    ACT(osbw[fb][:, c * 64:(c + 1) * 64], Ot[oi][:, c * 64:(c + 1) * 64], AF.Copy,
                        [r_O[oi], r_small[sc + 4 + c]], [r_osbw[fb]], scale=small[:, sc + 4 + c:sc + 5 + c])
                qi0 = it["pair"] * 2
                for q in range(2):
                    tdst = Ot[oi][:, 256 + q * 64:320 + q * 64].bitcast(BF16)
                    TR(tdst, osbw[fb][:, q * 128:(q + 1) * 128], [r_osbw[fb], r_const], [r_O[oi]], inc=True)
                    COPY("dve", oT[:, hp, (qi0 + q) * 128:(qi0 + q + 1) * 128], tdst, [r_O[oi]], [r_oT[hp][qi0 + q]])

            fns = dict(A=sA, B1=sB1, B2=sB2, T=sT, CP=sCP, PV=sPV)
            return [(it, fns) for it in items]

        def epilogue(bname, l, first, goff):
            wp_ap = w_proj_d[bname][l]
            for ch in range(2):
                buf = ch
                wp = wview(buf * 3 + 0, 4, 512)
                wgt = wview(buf * 3 + 1, 8, 512)
                rp = [r_W[buf * 3 + 0]]
                rgt = [r_W[buf * 3 + 1], r_W[buf * 3 + 2]]
                for k in range(4):
                    load_w(wp[:, k, :], wp_ap[k * 128:(k + 1) * 128, ch * 512:(ch + 1) * 512], 512, rp)
                for dc in range(8):
                    load_w(wgt[:, dc, :], w_in_d[l][dc * 128:(dc + 1) * 128, goff + ch * 512: goff + (ch + 1) * 512],
                           512, rgt)
                for cc in range(4):
                    c = ch * 4 + cc
                    for ti in range(2):
                        t0 = state["tok0"] + ti * 512
                        tt = t0 // 512
                        P, r_P = next_d()
                        G, r_G = next_d()
                        for k in range(4):
                            MM(P, wp[:, k, cc * 128:(cc + 1) * 128], oT[:, k, ti * 512:(ti + 1) * 512], k == 0, k == 3,
                               rp + r_oT[k][ti * 4:(ti + 1) * 4], [r_P], inc=(k == 3))
                        for dc in range(8):
                            MM(G, wgt[:, dc, cc * 128:(cc + 1) * 128], xn[:, dc, t0:t0 + 512], dc == 0, dc == 7,
                               rgt + [r_xn[dc][tt]], [r_G], inc=(dc == 7))
                        i = state["sg"]
                        state["sg"] = (i + 1) % 2
                        ACT(sg[i][:], G, AF.Sigmoid, [r_G], [r_sg[i]])
                        msl = mrg[:, (c * 2 + ti) * 512:(c * 2 + ti + 1) * 512]
                        rm = r_mrg[c * 2 + ti]
                        if first:
                            TT("dve", msl, sg[i][:], P, ALU.mult, [r_sg[i], r_P], [rm])
                        else:
                            TT("dve", sg[i][:], sg[i][:], P, ALU.mult, [r_sg[i], r_P], [r_sg[i]])
                            TT("pool", msl, msl, sg[i][:], ALU.add, [r_sg[i], rm], [rm])

        def mixer_out(l):
            for ch in range(2):
                wo = wview(ch * 3, 8, 512)
                ro = [r_W[ch * 3], r_W[ch * 3 + 1]]
                for c in range(8):
                    load_w(wo[:, c, :], w_out_d[l][c * 128:(c + 1) * 128, ch * 512:(ch + 1) * 512], 512, ro)
                for cc in range(4):
                    c2 = ch * 4 + cc
                    for ti in range(2):
                        t0 = state["tok0"] + ti * 512
                        tt = t0 // 512
                        ob, r_ob = next_o()
                        for c in range(8):
                            MM(ob[:], wo[:, c, cc * 128:(cc + 1) * 128], mrg[:, (c * 2 + ti) * 512:(c * 2 + ti + 1) * 512],
                               c == 0, c == 7, ro + [r_mrg[c * 2 + ti]], [r_ob], inc=(c == 7))
                        TT("dve", xres[:, c2, t0:t0 + 512], ob[:], xres[:, c2, t0:t0 + 512], ALU.add,
                           [r_ob, r_xres[c2][tt]], [r_xres[c2][tt]])

        def mixer(l):
            rmsnorm_to_xn(PC_NORM + (l * 3 + 1) * 8)
            for half in range(2):
                tok0 = half * 1024
                state["tok0"] = tok0
                nkt = (SEQ - tok0) // 512
                first = True
                for bname in branches:
                    if bname == "sb":
                        qoff, koff, voff, goff = OFF_QA, OFF_KA, OFF_VA, OFF_GA
                    elif bname == "diff":
                        qoff, koff, voff, goff = OFF_QD, OFF_KD, OFF_VD, OFF_GD
                    else:
                        qoff, koff, voff, goff = OFF_QS, OFF_KS, OFF_VS, OFF_GS
                    wq = wview(0, 8, 512)
                    wk = wview(2, 8, 512)
                    wv_ = wview(4, 8, 512)
                    rq, rk, rv = [r_W[0], r_W[1]], [r_W[2], r_W[3]], [r_W[4], r_W[5]]
                    for dc in range(8):
                        rows = slice(dc * 128, (dc + 1) * 128)
                        load_w(wq[:, dc, :], w_in_d[l][rows, qoff:qoff + 512], 512, rq)
                        if bname != "swa":
                            load_w(wk[:, dc, :], w_in_d[l][rows, koff:koff + 512], 512, rk)
                            load_w(wv_[:, dc, :], w_in_d[l][rows, voff:voff + 512], 512, rv)
                        else:
                            i = state["stg"]
                            state["stg"] = (i + 1) % NSTG
                            S.dma("sp", stg[i][:, :256], w_in_d[l][rows, koff:koff + 256], writes=[r_stg[i]],
                                  dst=r_stg[i])
                            for g in range(2):
                                for dup in range(2):
                                    TS("pool", wk[:, dc, g * 128 + dup * 64: g * 128 + dup * 64 + 64],
                                       stg[i][:, g * 64:(g + 1) * 64], 1.0, 0.0, ALU.mult, ALU.add, [r_stg[i]], rk)
                            TS("pool", wv_[:, dc, 0:128], stg[i][:, 128:256], 1.0, 0.0, ALU.mult, ALU.add,
                               [r_stg[i]], rv)
                    def proj_units(hp):
                        b = hp % 2
                        qd_, rqd = qTb[b], r_qTb[b]
                        u = []
                        if bname == "sb" or bname == "diff":
                            kd_, rkd, vd_, rvd = kTb[b], r_kTb[b], vvb[b], r_vvb[b]
                            mode = "copy" if bname == "sb" else "qknorm"
                            gq = None if bname == "sb" else PC_QK + l * 4 + 0
                            gk = None if bname == "sb" else PC_QK + l * 4 + 1
                            u += proj_fm_units(wq, rq, hp * 128, qd_, lambda ti: rqd[ti], tok0, 2, mode,
                                               gain_col=gq, alt=0)
                            u += proj_fm_units(wk, rk, hp * 128, kd_[:, tok0:], lambda ti: rkd[(tok0 // 512) + ti],
                                               tok0, nkt, mode, gain_col=gk, alt=1)
                            u += proj_v_units(wv_, rv, hp * 128, 128, tok0 // 128, 16, vd_, rvd)
                        else:
                            u += proj_fm_units(wq, rq, hp * 128, qd_, lambda ti: rqd[ti], tok0, 2, "qknorm",
                                               gain_col=PC_QK + l * 4 + 2)
                            if hp % 2 == 0:
                                g = hp // 2
                                kd_, rkd = kTb[g % 2], r_kTb[g % 2]
                                u += proj_fm_units(wk, rk, g * 128, kd_[:, tok0:],
                                                   lambda ti: rkd[(tok0 // 512) + ti], tok0, nkt, "qknorm",
                                                   gain_col=PC_QK + l * 4 + 3)
                            if hp == 0:
                                u += proj_v_units(wv_, rv, 0, 128, tok0 // 128, 16, vvb[0], r_vvb[0])
                        return u

                    for hp2 in (0, 2):
                        for hp in (hp2, hp2 + 1):
                            for u in proj_units(hp):
                                u()
                        for hp in (hp2, hp2 + 1):
                            b = hp % 2
                            if bname == "swa":
                                kb_ = (hp // 2) % 2
                                swa_fn = attention_swa if SWA_BATCHED else (
                                    lambda l_, hp_, *a: attention("swa", l_, hp_, *a))
                                its = swa_fn(l, hp, tok0, qTb[b], r_qTb[b], kTb[kb_], r_kTb[kb_],
                                             vvb[0], r_vvb[0])
                            else:
                                its = attention(bname, l, hp, tok0, qTb[b], r_qTb[b], kTb[b], r_kTb[b],
                                                vvb[b], r_vvb[b])
                            for (it, fns) in its:
                                pipe_push(it, fns)
                        pipe_flush()
                    if debug == "oT" and half == 1:
                        for c4 in range(4):
                            COPY("dve", dbgf[:], oT[:, c4, :], r_oT[c4], [r_dbg])
                            dbg_ids.append(S.dma("sp", dbg_d[:, c4 * 1024:(c4 + 1) * 1024], dbgf[:], reads=[r_dbg],
                                                 dst=r_dbg))
                    epilogue(bname, l, first, goff)
                    first = False
                mixer_out(l)

        out_ids = []
        r_out = [Res("yout%d" % i) for i in range(8)]
        for s in range(n_seq):
            for dc in range(8):
                S.dma("sp", xres[:, dc, :], x_d[s, dc], writes=r_xres[dc], dst=r_xres[dc][0])
                for t in range(1, 4):
                    r_xres[dc][t].w = r_xres[dc][0].w
            for l in layers:
                if "ffn1" in phases:
                    ffn(l, 0)
                if "mix" in phases:
                    mixer(l)
                if "ffn2" in phases:
                    ffn(l, 1)
            for dc in range(8):
                out_ids.append(S.dma("sp", y_d[s, dc], xres[:, dc, :], reads=r_xres[dc], dst=r_out[dc]))
        S.wait_all("sp", out_ids[-8:] + dbg_ids)
        S.emit_all()
        stats = dict(n_instr=S.n_instr, n_wait=S.n_wait, nsem=S.nsem)
    return nc, stats


def _t5_bucket_np(n):
    n = np.maximum(n, 0)
    max_exact = 16
    nf = np.maximum(n, 1).astype(np.float32)
    large = max_exact + (np.log(nf / np.float32(max_exact)) / np.float32(math.log(128 / max_exact))
                         * np.float32(32 - max_exact)).astype(np.int32)
    large = np.minimum(large, 31)
    return np.where(n < max_exact, n, large)


def _host_layout(inputs):
    f32 = np.float32
    prm = np.zeros((128, NP), f32)
    p = np.arange(128)
    norms = [inputs["ffn1_norm"], inputs["mix_norm"], inputs["ffn2_norm"]]
    for l in range(DEPTH):
        for which in range(3):
            g = np.asarray(norms[which][l], f32).reshape(8, 128)
            prm[:, PC_NORM + (l * 3 + which) * 8: PC_NORM + (l * 3 + which) * 8 + 8] = g.T
        for k, name in enumerate(["q_norm_diff", "k_norm_diff", "q_norm_swa", "k_norm_swa"]):
            prm[:, PC_QK + l * 4 + k] = np.asarray(inputs[name][l], f32)[p % 64]
        prm[:, PC_SINK + l * 8: PC_SINK + l * 8 + 8] = np.asarray(inputs["swa_sinks"][l], f32)[None, :]
        prm[:, PC_LAM + l * 256: PC_LAM + (l + 1) * 256] = np.asarray(inputs["diff_lambda"][l], f32).reshape(1, 256)
        prm[:, PC_SUBLN + l * 128: PC_SUBLN + (l + 1) * 128] = np.asarray(inputs["diff_subln"][l], f32)[None, :]
    rb = np.asarray(inputs["rel_bias"], f32)
    prm[:, PC_CFAR: PC_CFAR + 4] = rb[31, 0:4][None, :]
    i = np.arange(128)[:, None]
    j = np.arange(128)[None, :]
    d0 = j - i
    d1 = 128 + j - i
    b0 = _t5_bucket_np(d0)
    b1 = _t5_bucket_np(d1)
    bt = np.zeros((128, 12, 256), f32)
    for h in range(12):
        t0 = rb[b0, h]
        t0 = np.where(d0 >= 0, t0, f32(MASKV))
        t1 = rb[b1, h]
        if h >= 4:
            t1 = np.where(d1 < 128, t1, f32(MASKV))
        bt[:, h, 0:128] = t0
        bt[:, h, 128:256] = t1
    return prm, bt


_CACHE = {}


def kernel(**inputs):
    x = np.asarray(inputs["x"], np.float32)
    B = x.shape[0]
    prm, bt = _host_layout(inputs)
    if "nc" not in _CACHE:
        _CACHE["nc"] = build_program()[0]
    nc = _CACHE["nc"]
    shared = {k: np.ascontiguousarray(np.asarray(inputs[k], np.float32)) for k in
              ["ffn1_w_in", "ffn1_w_out", "ffn2_w_in", "ffn2_w_out", "w_in", "w_proj_sb", "w_proj_diff",
               "w_proj_swa", "w_out"]}
    shared["params"] = prm
    shared["btiles"] = bt
    in_maps = []
    for c in range(N_CORES):
        xs = x[c * SEQ_PER_CORE:(c + 1) * SEQ_PER_CORE]
        xs = xs[:, ::-1, :]
        xt = np.ascontiguousarray(xs.transpose(0, 2, 1)).reshape(SEQ_PER_CORE, 8, 128, SEQ)
        m = dict(shared)
        m["x"] = xt
        in_maps.append(m)
    res = run_bass_kernel_spmd(nc, in_maps, core_ids=list(range(N_CORES)))
    out = np.empty((B, SEQ, D_MODEL), np.float32)
    for c in range(N_CORES):
        y = np.asarray(res.results[c]["y"]).reshape(SEQ_PER_CORE, D_MODEL, SEQ)
        out[c * SEQ_PER_CORE:(c + 1) * SEQ_PER_CORE] = y.transpose(0, 2, 1)[:, ::-1, :]
    return out
```

```python
import contextlib
import math
import numpy as np
import concourse.bass as bass
import concourse.mybir as mybir
from concourse.bass_utils import run_bass_kernel_spmd

F32 = mybir.dt.float32
BF16 = mybir.dt.bfloat16
AF = mybir.ActivationFunctionType
ALU = mybir.AluOpType

D_MODEL = 1024
SEQ = 2048
DEPTH = 2
D_FF = 2816
NJ = D_FF // 128
IN_W = 6912
EPS = 1e-6
N_CORES = 8
SEQ_PER_CORE = 2
MASKV = -30000.0
SCALE = 0.125
SWA_BATCHED = False

OFF_QA, OFF_KA, OFF_VA = 0, 512, 1024
OFF_QD, OFF_KD, OFF_VD = 1536, 2048, 2560
OFF_QS, OFF_KS, OFF_VS = 3072, 3584, 3712
OFF_GA, OFF_GD, OFF_GS = 3840, 4864, 5888

PC_NORM = 0
PC_QK = 48
PC_SINK = 56
PC_CFAR = 72
PC_SUBLN = 76
PC_LAM = 332
NP = 844
NPR = 332


class Res:
    __slots__ = ("name", "w", "r", "sem", "semval")

    def __init__(self, name):
        self.name = name
        self.w = None
        self.r = {}
        self.sem = None
        self.semval = 0


class Sched:
    ROT = 30000
    ENGS = ("pe", "act", "dve", "pool", "sp")

    def __init__(self, nc, stack):
        self.nc = nc
        self.stack = stack
        self.ops = {e: [] for e in self.ENGS}
        self.cnt = {e: 0 for e in self.ENGS}
        self.epoch = {e: 0 for e in self.ENGS}
        self.waited = {e: {} for e in self.ENGS}
        self.semh = {}
        self.nsem = 0
        self.pending_noinc = {e: False for e in self.ENGS}
        self.n_instr = 0
        self.n_wait = 0

    def _sem(self, key):
        h = self.semh.get(key)
        if h is None:
            h = self.stack.enter_context(self.nc.semaphore("s%d" % self.nsem))
            self.nsem += 1
            self.semh[key] = h
        return h

    def _next_id(self, eng):
        if self.cnt[eng] >= self.ROT and not self.pending_noinc[eng]:
            self.epoch[eng] += 1
            self.cnt[eng] = 0
        return (("e", eng, self.epoch[eng]), self.cnt[eng] + 1)

    def _collect(self, eng, reads, writes):
        deps = {}

        def add(d):
            if d is None:
                return
            k, v = d
            if eng == "pe" and k[0] == "e" and k[1] == "pe":
                return
            if deps.get(k, 0) < v:
                deps[k] = v
        for r in reads:
            add(r.w)
        for w in writes:
            add(w.w)
            for d in w.r.items():
                add(d)
        out = []
        wd = self.waited[eng]
        for k, v in deps.items():
            if wd.get(k, 0) >= v:
                continue
            wd[k] = v
            out.append((k, v))
        return out

    def op(self, eng, emit, reads=(), writes=(), inc=True):
        waits = self._collect(eng, reads, writes)
        myid = self._next_id(eng)
        if inc:
            self.cnt[eng] += 1
            self.pending_noinc[eng] = False
        else:
            self.pending_noinc[eng] = True
        for r in reads:
            if r.r.get(myid[0], 0) < myid[1]:
                r.r[myid[0]] = myid[1]
        for w in writes:
            w.w = myid
            w.r = {}
        self.ops[eng].append((waits, emit, myid[0] if inc else None, 1))
        self.n_instr += 1
        self.n_wait += len(waits)

    def dma(self, queue, out_ap, in_ap, reads=(), writes=(), dst=None):
        waits = self._collect(queue, reads, writes)
        if dst.sem is None:
            dst.sem = ("d", dst.name, id(dst))
        dst.semval += 16
        myid = (dst.sem, dst.semval)
        for r in reads:
            if r.r.get(myid[0], 0) < myid[1]:
                r.r[myid[0]] = myid[1]
        for w in writes:
            w.w = myid
            w.r = {}
        self.ops[queue].append((waits, (lambda e: e.dma_start(out=out_ap, in_=in_ap)), dst.sem, 16))
        self.n_instr += 1
        self.n_wait += len(waits)
        return myid

    def wait_all(self, eng, ids):
        self.ops[eng].append((list(ids), None, None, 0))

    def emit_all(self):
        nc = self.nc
        for e in self.ENGS:
            for waits, emit, inck, incv in self.ops[e]:
                for k, v in waits:
                    self._sem(k)
                if inck is not None:
                    self._sem(inck)
        ops = self.ops
        semh = self.semh

        def run(ename, e):
            for waits, emit, inck, incv in ops[ename]:
                for k, v in waits:
                    e.wait_ge(semh[k], v)
                if emit is not None:
                    ins = emit(e)
                    if inck is not None:
                        ins.then_inc(semh[inck], incv)

        with nc.Block() as block:
            if ops["pe"]:
                @block.tensor
                def _(e):
                    run("pe", e)
            if ops["act"]:
                @block.scalar
                def _(e):
                    run("act", e)
            if ops["dve"]:
                @block.vector
                def _(e):
                    run("dve", e)
            if ops["pool"]:
                @block.gpsimd
                def _(e):
                    run("pool", e)
            if ops["sp"]:
                @block.sync
                def _(e):
                    run("sp", e)


def build_program(n_seq=SEQ_PER_CORE, layers=(0, 1), phases=("ffn1", "mix", "ffn2"),
                  branches=("sb", "diff", "swa"), GC=2, debug=None):
    nc = bass.Bass("TRN2", target_bir_lowering=False, dynamic_dma_scratch_size=512)
    dram = {}

    def din(name, shape):
        dram[name] = nc.dram_tensor(name, list(shape), F32, kind="ExternalInput").ap()
        return dram[name]

    x_d = din("x", [n_seq, 8, 128, SEQ])
    w_ffn_in = [din("ffn1_w_in", [DEPTH, D_MODEL, 2 * D_FF]), din("ffn2_w_in", [DEPTH, D_MODEL, 2 * D_FF])]
    w_ffn_out = [din("ffn1_w_out", [DEPTH, D_FF, D_MODEL]), din("ffn2_w_out", [DEPTH, D_FF, D_MODEL])]
    w_in_d = din("w_in", [DEPTH, D_MODEL, IN_W])
    w_proj_d = {"sb": din("w_proj_sb", [DEPTH, 512, D_MODEL]),
                "diff": din("w_proj_diff", [DEPTH, 512, D_MODEL]),
                "swa": din("w_proj_swa", [DEPTH, 512, D_MODEL])}
    w_out_d = din("w_out", [DEPTH, D_MODEL, D_MODEL])
    params_d = din("params", [128, NP])
    btiles_d = din("btiles", [128, 12, 256])
    y_d = nc.dram_tensor("y", [n_seq, 8, 128, SEQ], F32, kind="ExternalOutput").ap()
    dbg_d = nc.dram_tensor("dbg", [128, 4096], F32, kind="ExternalOutput").ap() if debug else None

    with contextlib.ExitStack() as st:
        S = Sched(nc, st)

        def sb(name, shape, dt):
            return st.enter_context(nc.sbuf_tensor(name, list(shape), dt))

        def ps(name, shape, dt):
            return st.enter_context(nc.psum_tensor(name, list(shape), dt))

        xres = sb("xres", [128, 8, SEQ], F32)
        r_xres = [[Res("xres%d_%d" % (c, t)) for t in range(4)] for c in range(8)]
        xn = sb("xn", [128, 8, SEQ], BF16)
        r_xn = [[Res("xn%d_%d" % (c, t)) for t in range(4)] for c in range(8)]
        mrg = sb("mrg", [128, 8192], BF16)
        r_mrg = [Res("mrg%d" % i) for i in range(16)]
        oT = sb("oT", [128, 4, 1024], BF16)
        r_oT = [[Res("oT%d_%d" % (c, q)) for q in range(8)] for c in range(4)]
        Wall = sb("Wall", [128, 6 * 2048], BF16)
        r_W = [Res("W%d" % i) for i in range(6)]
        qTb = [sb("qT%d" % b, [128, 1024], BF16) for b in range(2)]
        r_qTb = [[Res("qT%d_%d" % (b, i)) for i in range(2)] for b in range(2)]
        kTb = [sb("kT%d" % b, [128, SEQ], BF16) for b in range(2)]
        r_kTb = [[Res("kT%d_%d" % (b, i)) for i in range(4)] for b in range(2)]
        vvb = [sb("vv%d" % b, [128, 16, 128], BF16) for b in range(2)]
        r_vvb = [[Res("vv%d_%d" % (b, i)) for i in range(4)] for b in range(2)]
        NSTG = 3
        stg = [sb("stg%d" % i, [128, 512], F32) for i in range(NSTG)]
        r_stg = [Res("stg%d" % i) for i in range(NSTG)]
        el = [sb("el%d" % i, [128, 1024], F32) for i in range(2)]
        r_el = [Res("el%d" % i) for i in range(2)]
        ca = [sb("ca%d" % i, [128, 1024], F32) for i in range(2)]
        r_ca = [Res("ca%d" % i) for i in range(2)]
        wb = [sb("wb%d" % i, [128, 1024], BF16) for i in range(2)]
        r_wb = [Res("wb%d" % i) for i in range(2)]
        wT = [sb("wT%d" % i, [128, 1024], BF16) for i in range(2)]
        r_wT = [Res("wT%d" % i) for i in range(2)]
        sq = [sb("sq%d" % i, [128, 512], BF16) for i in range(2)]
        r_sq = [Res("sq%d" % i) for i in range(2)]
        sg = [sb("sg%d" % i, [128, 512], F32) for i in range(2)]
        r_sg = [Res("sg%d" % i) for i in range(2)]
        rstd = sb("rstd", [128, 512], F32)
        r_rstd = Res("rstd")
        osb = [sb("osb%d" % i, [128, 128], BF16) for i in range(2)]
        r_osb = [Res("osb%d" % i) for i in range(2)]
        osbw = [sb("osbw%d" % i, [128, 256], BF16) for i in range(2)]
        r_osbw = [Res("osbw%d" % i) for i in range(2)]
        of32 = [sb("of32_%d" % i, [128, 128], F32) for i in range(2)]
        r_of32 = [Res("of32_%d" % i) for i in range(2)]
        btl = sb("btl", [128, 12, 256], F32)
        r_btl = Res("btl")
        prm = sb("prm", [128, NPR], F32)
        r_prm = Res("prm")
        ones_bf = sb("ones_bf", [128, 128], BF16)
        bones_bf = sb("bones_bf", [128, 128], BF16)
        ident_bf = sb("ident_bf", [128, 128], BF16)
        onesrow = sb("onesrow", [128, 1024], BF16)
        r_const = Res("const")
        small = sb("small", [128, 64], F32)
        r_small = [Res("small%d" % i) for i in range(64)]
        lamt = sb("lamt", [128, 8], F32)
        r_lamt = Res("lamt")
        esink = sb("esink", [128, 16], F32)
        subg = sb("subg", [128, 2, 128], F32)
        tmp64 = sb("tmp64", [128, 64], F32)
        sqj = sb("sqj", [128, 128], F32)
        r_tmp64 = Res("tmp64")
        carry = [sb("carry%d" % i, [128, 1], F32) for i in range(2)]
        r_carry = [Res("carry%d" % i) for i in range(2)]

        dbgf = sb("dbgf", [128, 1024], F32) if debug else None
        r_dbg = Res("dbg")
        dbg_ids = []
        ZA = ps("ZA", [128, 1024], F32)
        ZB = ps("ZB", [128, 1024], F32)
        r_Z = [[Res("ZA0"), Res("ZA1")], [Res("ZB0"), Res("ZB1")]]
        Zt = [ZA, ZB]
        Dbanks = [(ZA, 0, r_Z[0][0]), (ZA, 512, r_Z[0][1]), (ZB, 0, r_Z[1][0]), (ZB, 512, r_Z[1][1])]
        Tt = [ps("T0", [128, 1024], BF16), ps("T1", [128, 1024], BF16)]
        r_T = [Res("T0"), Res("T1")]
        Ot = [ps("O0", [128, 512], F32), ps("O1", [128, 512], F32)]
        r_O = [Res("O0"), Res("O1")]

        state = {"d": 0, "o": 0, "stg": 0, "sq": 0, "sg": 0}

        def next_d():
            i = state["d"]
            state["d"] = (i + 1) % 4
            t, off, r = Dbanks[i]
            return t[:, off:off + 512], r

        def next_o():
            i = state["o"]
            state["o"] = (i + 1) % 2
            return Ot[i], r_O[i]

        def ACT(out, in_, func, reads, writes, **kw):
            S.op("act", lambda e: e.activation(out=out, in_=in_, func=func, **kw), reads, writes)

        def MM(out, lhsT, rhs, start, stop, reads, writes, inc):
            S.op("pe", lambda e: e.matmul(out, lhsT=lhsT, rhs=rhs, start=start, stop=stop),
                 reads, writes, inc=inc)

        def TR(out, in_, reads, writes, inc=True):
            S.op("pe", lambda e: e.transpose(out, in_, ident_bf[:]), reads, writes, inc=inc)

        def STT(eng, out, in0, scalar, in1, op0, op1, reads, writes):
            S.op(eng, lambda e: e.scalar_tensor_tensor(out=out, in0=in0, scalar=scalar, in1=in1, op0=op0, op1=op1),
                 reads, writes)

        def TT(eng, out, in0, in1, op, reads, writes):
            S.op(eng, lambda e: e.tensor_tensor(out=out, in0=in0, in1=in1, op=op), reads, writes)

        def TS(eng, out, in0, s1, s2, op0, op1, reads, writes):
            S.op(eng, lambda e: e.tensor_scalar(out=out, in0=in0, scalar1=s1, scalar2=s2, op0=op0, op1=op1),
                 reads, writes)

        def COPY(eng, out, in_, reads, writes):
            S.op(eng, lambda e: e.tensor_copy(out=out, in_=in_), reads, writes)

        def load_w(dst_ap, src_ap, n, dst_res, extra_dst=None):
            i = state["stg"]
            state["stg"] = (i + 1) % NSTG
            S.dma("sp", stg[i][:, :n], src_ap, writes=[r_stg[i]], dst=r_stg[i])
            TS("pool", dst_ap, stg[i][:, :n], 1.0, 0.0, ALU.mult, ALU.add, [r_stg[i]], dst_res)
            if extra_dst is not None:
                COPY("pool", extra_dst, stg[i][:, :n], [r_stg[i]], dst_res)

        def wview(r0, a, b):
            return Wall[:, r0 * 2048: r0 * 2048 + a * b].rearrange("p (a b) -> p a b", a=a)

        def pcol(c, n=1):
            return prm[:, c:c + n]

        S.dma("sp", prm[:], params_d[:, 0:NPR], writes=[r_prm], dst=r_prm)
        S.dma("sp", el[0][:, 0:512], params_d[:, PC_LAM:PC_LAM + 512], writes=[r_el[0]], dst=r_el[0])
        S.dma("sp", btl[:], btiles_d, writes=[r_btl], dst=r_btl)
        S.op("pool", lambda e: e.memset(ones_bf[:], 1.0), [], [r_const])
        S.op("pool", lambda e: e.memset(onesrow[:], 1.0), [], [r_const])
        S.op("pool", lambda e: e.affine_select(out=ident_bf[:], in_=ones_bf[:], pattern=[[1, 128]],
                                               compare_op=ALU.is_equal, fill=0.0, base=0,
                                               channel_multiplier=-1), [r_const], [r_const])
        S.op("pool", lambda e: e.memset(bones_bf[:], 0.0), [], [r_const])
        S.op("pool", lambda e: e.memset(bones_bf[0:64, 0:64], 1.0), [], [r_const])
        S.op("pool", lambda e: e.memset(bones_bf[64:128, 64:128], 1.0), [], [r_const])
        for l in range(DEPTH):
            lam_init = 0.8 - 0.6 * math.exp(-0.3 * l)
            base = l * 256
            for k in range(2):
                TT("dve", tmp64[:], el[0][:, base + 128 * k: base + 128 * k + 64],
                   el[0][:, base + 128 * k + 64: base + 128 * k + 128], ALU.mult,
                   [r_el[0]], [r_tmp64])
                S.op("dve", lambda e, k=k, l=l: e.reduce_sum(out=small[:, 2 * l + k:2 * l + k + 1], in_=tmp64[:],
                                                             axis=mybir.AxisListType.X),
                     [r_tmp64], [r_small[2 * l + k]])
                ACT(small[:, 2 * l + k:2 * l + k + 1], small[:, 2 * l + k:2 * l + k + 1], AF.Exp,
                    [r_small[2 * l + k]], [r_small[2 * l + k]])
            STT("dve", lamt[:, l:l + 1], small[:, 2 * l:2 * l + 1], float(lam_init), small[:, 2 * l + 1:2 * l + 2],
                ALU.add, ALU.subtract, [r_small[2 * l], r_small[2 * l + 1]], [r_lamt])
            TS("dve", subg[:, l, :], pcol(PC_SUBLN + l * 128, 128), float(1.0 - lam_init), None, ALU.mult, ALU.bypass,
               [r_prm], [r_lamt])
        ACT(esink[:], pcol(PC_SINK, 16), AF.Exp, [r_prm], [r_lamt])

        def rmsnorm_to_xn(gcol):
            for tt in range(4):
                tsl = slice(tt * 512, (tt + 1) * 512)
                ob, r_ob = next_o()
                for dc in range(8):
                    i = state["sq"]
                    state["sq"] = (i + 1) % 2
                    ACT(sq[i][:], xres[:, dc, tsl], AF.Square, [r_xres[dc][tt]], [r_sq[i]])
                    MM(ob[:], ones_bf[:], sq[i][:], dc == 0, dc == 7, [r_sq[i], r_const], [r_ob], inc=True)
                ACT(rstd[:], ob[:], AF.Ln, [r_ob], [r_rstd], scale=1.0 / D_MODEL, bias=EPS)
                ACT(rstd[:], rstd[:], AF.Exp, [r_rstd], [r_rstd], scale=-0.5)
                for dc in range(8):
                    STT("dve", xn[:, dc, tsl], xres[:, dc, tsl], pcol(gcol + dc), rstd[:], ALU.mult, ALU.mult,
                        [r_xres[dc][tt], r_rstd, r_prm], [r_xn[dc][tt]])

        def ffn(l, which):
            w_in_ap = w_ffn_in[which][l]
            w_out_ap = w_ffn_out[which][l]
            rmsnorm_to_xn(PC_NORM + (l * 3 + (0 if which == 0 else 2)) * 8)
            groups = [(j0, min(j0 + GC, NJ)) for j0 in range(0, NJ, GC)]
            ob4 = [(Ot[0][:], r_O[0]), (Ot[1][:], r_O[1]),
                   (Tt[0][:].bitcast(F32), r_T[0]), (Tt[1][:].bitcast(F32), r_T[1])]
            ost = {"i": 0}

            def views(gi):
                buf = gi % 2
                return (wview(buf * 3 + 0, 8, 256), wview(buf * 3 + 1, 8, 256), wview(buf * 3 + 2, 2, 1024),
                        r_W[buf * 3 + 0], r_W[buf * 3 + 1], r_W[buf * 3 + 2],
                        mrg[:, buf * 4096: buf * 4096 + 4096].rearrange("p (a b) -> p a b", a=2), buf)

            def load_group(gi):
                j0, j1 = groups[gi]
                n = j1 - j0
                wg, wu, wo, rg, ru, ro, actb, buf = views(gi)
                for dc in range(8):
                    load_w(wg[:, dc, :n * 128], w_in_ap[dc * 128:(dc + 1) * 128, j0 * 128:j1 * 128], n * 128, [rg])
                    load_w(wu[:, dc, :n * 128],
                           w_in_ap[dc * 128:(dc + 1) * 128, D_FF + j0 * 128:D_FF + j1 * 128], n * 128, [ru])
                for jj in range(n):
                    for hh in range(2):
                        load_w(wo[:, jj, hh * 512:(hh + 1) * 512],
                               w_out_ap[(j0 + jj) * 128:(j0 + jj + 1) * 128, hh * 512:(hh + 1) * 512], 512, [ro])

            def win_unit(gi, jj, tt):
                wg, wu, wo, rg, ru, ro, actb, buf = views(gi)
                tsl = slice(tt * 512, (tt + 1) * 512)
                r_act = r_mrg[buf * 8 + jj * 4 + tt]
                hg, r_hg = next_d()
                hu, r_hu = next_d()
                for dc in range(8):
                    MM(hg, wg[:, dc, jj * 128:(jj + 1) * 128], xn[:, dc, tsl], dc == 0, dc == 7,
                       [rg, r_xn[dc][tt]], [r_hg], inc=(dc == 7))
                for dc in range(8):
                    MM(hu, wu[:, dc, jj * 128:(jj + 1) * 128], xn[:, dc, tsl], dc == 0, dc == 7,
                       [ru, r_xn[dc][tt]], [r_hu], inc=(dc == 7))
                i = state["sg"]
                state["sg"] = (i + 1) % 2
                ACT(sg[i][:], hg, AF.Silu, [r_hg], [r_sg[i]])
                TT("dve", actb[:, jj, tsl], sg[i][:], hu, ALU.mult, [r_sg[i], r_hu], [r_act])

            def wout_unit(gi, c, tt):
                j0, j1 = groups[gi]
                n = j1 - j0
                wg, wu, wo, rg, ru, ro, actb, buf = views(gi)
                tsl = slice(tt * 512, (tt + 1) * 512)
                ob, r_ob = ob4[ost["i"]]
                ost["i"] = (ost["i"] + 1) % 4
                for jj in range(n):
                    MM(ob, wo[:, jj, c * 128:(c + 1) * 128], actb[:, jj, tsl], jj == 0, jj == n - 1,
                       [ro, r_mrg[buf * 8 + jj * 4 + tt]], [r_ob], inc=(jj == n - 1))
                STT("dve", xres[:, c, tsl], ob, 0.5, xres[:, c, tsl], ALU.mult, ALU.add,
                    [r_ob, r_xres[c][tt]], [r_xres[c][tt]])

            pending = []
            for gi in range(len(groups)):
                j0, j1 = groups[gi]
                n = j1 - j0
                load_group(gi)
                wins = [(jj, tt) for jj in range(n) for tt in range(4)]
                per = (len(pending) + len(wins) - 1) // len(wins) if pending else 0
                for (jj, tt) in wins:
                    win_unit(gi, jj, tt)
                    for _ in range(per):
                        if pending:
                            g2, c, t2 = pending.pop(0)
                            wout_unit(g2, c, t2)
                while pending:
                    g2, c, t2 = pending.pop(0)
                    wout_unit(g2, c, t2)
                pending = [(gi, c, tt) for c in range(8) for tt in range(4)]
            while pending:
                g2, c, t2 = pending.pop(0)
                wout_unit(g2, c, t2)

        def proj_fm_units(wv_, rw, col0, dst, r_dst_fn, tok0, ntile, mode, gain_col=None, alt=0):
            def unit(ti):
                t0 = tok0 + ti * 512
                tt = t0 // 512
                d, r_d = next_d()
                for dc in range(8):
                    MM(d, wv_[:, dc, col0:col0 + 128], xn[:, dc, t0:t0 + 512], dc == 0, dc == 7,
                       [r_xn[dc][tt]] + rw, [r_d], inc=(dc == 7))
                dsl = dst[:, ti * 512:(ti + 1) * 512]
                rd = r_dst_fn(ti)
                if mode == "copy":
                    if (ti + alt) % 2 == 0:
                        ACT(dsl, d, AF.Copy, [r_d], [rd])
                    else:
                        COPY("dve", dsl, d, [r_d], [rd])
                else:
                    i = state["sq"]
                    state["sq"] = (i + 1) % 2
                    ACT(sq[i][:], d, AF.Square, [r_d], [r_sq[i]])
                    d2, r_d2 = next_d()
                    MM(d2, bones_bf[:], sq[i][:], True, True, [r_sq[i], r_const], [r_d2], inc=True)
                    j = state["sg"]
                    state["sg"] = (j + 1) % 2
                    ACT(sg[j][:], d2, AF.Ln, [r_d2], [r_sg[j]], scale=1.0 / 64.0, bias=EPS)
                    ACT(sg[j][:], sg[j][:], AF.Exp, [r_sg[j]], [r_sg[j]], scale=-0.5)
                    STT("dve", dsl, d, pcol(gain_col), sg[j][:], ALU.mult, ALU.mult, [r_d, r_sg[j], r_prm], [rd])
            return [(lambda ti=ti: unit(ti)) for ti in range(ntile)]

        def proj_v_units(wv_, rw, col0, ncol, kb0, kb1, vv, r_vv):
            def unit(g0):
                d, r_d = next_d()
                nb = min(4, kb1 - g0)
                for bi in range(nb):
                    kb = g0 + bi
                    for dc in range(8):
                        MM(d[:, bi * 128: bi * 128 + ncol], xn[:, dc, kb * 128:(kb + 1) * 128],
                           wv_[:, dc, col0:col0 + ncol], dc == 0, dc == 7,
                           [r_xn[dc][kb // 4]] + rw, [r_d], inc=(dc == 7 and bi == nb - 1))
                src = d.rearrange("p (a b) -> p a b", a=4)[:, :nb, :ncol]
                COPY("dve", vv[:, g0:g0 + nb, :ncol], src, [r_d], [r_vv[g0 // 4]])
            return [(lambda g0=g0: unit(g0)) for g0 in range(kb0, kb1, 4)]

        pipe_items = []

        def pipe_run_step(k):
            for off, name in ((2, "A"), (1, "B1"), (0, "B2"), (-1, "T"), (-2, "CP"), (-3, "PV")):
                j = k + off
                if 0 <= j < len(pipe_items):
                    it, fns = pipe_items[j]
                    fns[name](it, j % 2)

        def pipe_push(it, fns):
            pipe_items.append((it, fns))
            pipe_run_step(len(pipe_items) - 3)

        def pipe_flush():
            g = len(pipe_items)
            for k in range(g - 2, g + 3):
                pipe_run_step(k)
            del pipe_items[:]

        def attention(kind, l, hp, q0, qT, r_qT, kT, r_kT, vv, r_vv):
            items = []
            for qi in range(8):
                qb = q0 // 128 + qi
                kstart = qb * 128
                kend = SEQ if kind != "swa" else min(SEQ, kstart + 256)
                segs = []
                k0 = kstart
                while k0 < kend:
                    n = min(1024, kend - k0)
                    segs.append((k0, n))
                    k0 += n
                if kind == "diff":
                    subs = [0, 1]
                else:
                    subs = [0, 1]
                for sidx, sub in enumerate(subs):
                    for si, (k0, n) in enumerate(segs):
                        items.append(dict(qi=qi, qb=qb, sub=sub, si=si, k0=k0, n=n, nseg=len(segs),
                                          first_of_qb=(sidx == 0 and si == 0),
                                          last_of_qb=(sidx == len(subs) - 1 and si == len(segs) - 1)))
            ostate = {}

            def stage_A(it, ib):
                base = it["sub"] * 64
                qsl = qT[base:base + 64, it["qi"] * 128:(it["qi"] + 1) * 128]
                n, k0 = it["n"], it["k0"]
                for c0 in range(0, n, 512):
                    cn = min(512, n - c0)
                    MM(Zt[ib][:, c0:c0 + cn], qsl, kT[base:base + 64, k0 + c0:k0 + c0 + cn], True, True,
                       [r_qT[it["qi"] // 4]] + r_kT[(k0 + c0) // 512:(k0 + c0 + cn - 1) // 512 + 1], [r_Z[ib][c0 // 512]], inc=True)

            def scols(it):
                return 8 + it["sub"] * 4 + (it["qi"] % 4) * 8

            def stage_B1(it, ib):
                n, k0, si = it["n"], it["k0"], it["si"]
                rz = r_Z[ib][:(n + 511) // 512]
                Z = Zt[ib]
                if kind == "sb":
                    ACT(el[ib][:, :n], Z[:, :n], AF.Exp, rz, [r_el[ib]], scale=SCALE)
                    ACT(el[ib][:, :n], el[ib][:, :n], AF.Ln, [r_el[ib]], [r_el[ib]], bias=1.0)
                    if si == 0:
                        S.op("pool", lambda e: e.affine_select(out=el[ib][:, 0:128], in_=el[ib][:, 0:128],
                                                               pattern=[[1, 128]], compare_op=ALU.is_gt, fill=0.0,
                                                               base=0, channel_multiplier=-1),
                             [r_el[ib]], [r_el[ib]])
                        init = 0.0
                        rinit = []
                    else:
                        init = carry[it["sub"]][:, 0:1]
                        rinit = [r_carry[it["sub"]]]
                    S.op("dve", lambda e: e.tensor_tensor_scan(out=ca[ib][:, :n], data0=onesrow[:, :n],
                                                               data1=el[ib][:, :n], initial=init,
                                                               op0=ALU.mult, op1=ALU.add),
                         [r_el[ib], r_const] + rinit, [r_ca[ib]])
                    if si < it["nseg"] - 1:
                        COPY("dve", carry[it["sub"]][:, 0:1], ca[ib][:, n - 1:n], [r_ca[ib]], [r_carry[it["sub"]]])
                    STT("dve", el[ib][:, :n], Z[:, :n], SCALE, ca[ib][:, :n], ALU.mult, ALU.subtract,
                        rz + [r_ca[ib]], [r_el[ib]])
                else:
                    head = hp if kind == "diff" else 4 + hp * 2 + it["sub"]
                    bt = btl[:, head, :]
                    if si == 0:
                        nn = min(256, n)
                        STT("dve", el[ib][:, :nn], Z[:, :nn], SCALE, bt[:, :nn], ALU.mult, ALU.add,
                            rz[:1] + [r_btl], [r_el[ib]])
                        if n > nn:
                            TS("dve", el[ib][:, nn:n], Z[:, nn:n], SCALE, pcol(PC_CFAR + head), ALU.mult, ALU.add,
                               rz + [r_prm], [r_el[ib]])
                    else:
                        TS("dve", el[ib][:, :n], Z[:, :n], SCALE, pcol(PC_CFAR + head), ALU.mult, ALU.add,
                           rz + [r_prm], [r_el[ib]])

            def stage_B2(it, ib):
                n, si = it["n"], it["si"]
                if kind == "sb":
                    ACT(wb[ib][:, :n], el[ib][:, :n], AF.Exp, [r_el[ib]], [r_wb[ib]])
                    if si == 0:
                        S.op("pool", lambda e: e.affine_select(out=wb[ib][:, 0:128], in_=wb[ib][:, 0:128],
                                                               pattern=[[1, 128]], compare_op=ALU.is_gt, fill=0.0,
                                                               base=0, channel_multiplier=-1),
                             [r_wb[ib]], [r_wb[ib]])
                else:
                    scol0 = scols(it)
                    if si == 0:
                        nn = min(256, n)
                        S.op("act", lambda e: e.activation(out=wb[ib][:, :nn], in_=el[ib][:, :nn], func=AF.Exp,
                                                           accum_out=small[:, scol0:scol0 + 1]),
                             [r_el[ib]], [r_wb[ib], r_small[scol0]])
                        if n > nn:
                            S.op("act", lambda e: e.activation(out=wb[ib][:, nn:n], in_=el[ib][:, nn:n], func=AF.Exp,
                                                               accum_out=small[:, scol0 + 1:scol0 + 2]),
                                 [r_el[ib]], [r_wb[ib], r_small[scol0 + 1]])
                    else:
                        S.op("act", lambda e: e.activation(out=wb[ib][:, :n], in_=el[ib][:, :n], func=AF.Exp,
                                                           accum_out=small[:, scol0 + 1 + si:scol0 + 2 + si]),
                             [r_el[ib]], [r_wb[ib], r_small[scol0 + 1 + si]])

            def stage_T(it, ib):
                nb = it["n"] // 128
                for jb in range(nb):
                    TR(Tt[ib][:, jb * 128:(jb + 1) * 128], wb[ib][:, jb * 128:(jb + 1) * 128],
                       [r_wb[ib], r_const], [r_T[ib]], inc=(jb == nb - 1))

            def stage_CP(it, ib):
                n = it["n"]
                ACT(wT[ib][:, :n], Tt[ib][:, :n], AF.Copy, [r_T[ib]], [r_wT[ib]])

            def stage_PV(it, ib):
                n, k0 = it["n"], it["k0"]
                nb = n // 128
                qi = it["qi"]
                if it["first_of_qb"]:
                    if kind == "diff":
                        ostate["o"] = (0, 1)
                    else:
                        oi = state["o"]
                        state["o"] = (oi + 1) % 2
                        ostate["o"] = (oi, oi)
                if kind == "diff":
                    oi = ostate["o"][it["sub"]]
                    ocols = slice(0, 128)
                    vcols = slice(0, 128)
                elif kind == "sb":
                    oi = ostate["o"][0]
                    ocols = slice(it["sub"] * 64, it["sub"] * 64 + 64)
                    vcols = ocols
                else:
                    oi = ostate["o"][0]
                    ocols = slice(it["sub"] * 64, it["sub"] * 64 + 64)
                    g = hp // 2
                    vcols = slice(g * 64, g * 64 + 64)
                for jb in range(nb):
                    kb = k0 // 128 + jb
                    MM(Ot[oi][:, ocols], wT[ib][:, jb * 128:(jb + 1) * 128], vv[:, kb, vcols],
                       it["si"] == 0 and jb == 0, it["si"] == it["nseg"] - 1 and jb == nb - 1,
                       [r_wT[ib], r_vv[kb // 4]], [r_O[oi]], inc=(jb == nb - 1))
                if kind == "swa":
                    sc = scols(it)
                    head = hp * 2 + it["sub"]
                    fb = qi % 2
                    TT("dve", small[:, sc + 2:sc + 3], small[:, sc:sc + 1], esink[:, l * 8 + head:l * 8 + head + 1],
                       ALU.add, [r_small[sc], r_lamt], [r_small[sc + 2]])
                    S.op("dve", lambda e: e.reciprocal(out=small[:, sc + 2:sc + 3], in_=small[:, sc + 2:sc + 3]),
                         [r_small[sc + 2]], [r_small[sc + 2]])
                    ACT(osb[fb][:, ocols], Ot[oi][:, ocols], AF.Copy, [r_O[oi], r_small[sc + 2]], [r_osb[fb]],
                        scale=small[:, sc + 2:sc + 3])
                if it["last_of_qb"]:
                    stage_F(it, qi)

            def stage_F(it, qi):
                fb = qi % 2
                oi = ostate["o"][0]
                if kind == "sb":
                    ACT(osb[fb][:], Ot[oi][:, 0:128], AF.Copy, [r_O[oi]], [r_osb[fb]])
                elif kind == "diff":
                    nseg = it["nseg"]
                    for c in range(2):
                        sc = 8 + c * 4 + (qi % 4) * 8
                        tot = 40 + c
                        n0 = it_n0[qi]
                        cols = [sc]
                        if n0 > 256:
                            cols.append(sc + 1)
                        if nseg > 1:
                            cols.append(sc + 2)
                        if len(cols) == 1:
                            COPY("dve", small[:, tot:tot + 1], small[:, cols[0]:cols[0] + 1], [r_small[cols[0]]],
                                 [r_small[tot]])
                        else:
                            TT("dve", small[:, tot:tot + 1], small[:, cols[0]:cols[0] + 1],
                               small[:, cols[1]:cols[1] + 1], ALU.add, [r_small[cols[0]], r_small[cols[1]]],
                               [r_small[tot]])
                            if len(cols) == 3:
                                TT("dve", small[:, tot:tot + 1], small[:, tot:tot + 1],
                                   small[:, cols[2]:cols[2] + 1], ALU.add, [r_small[tot], r_small[cols[2]]],
                                   [r_small[tot]])
                        S.op("dve", lambda e, tot=tot: e.reciprocal(out=small[:, tot:tot + 1], in_=small[:, tot:tot + 1]),
                             [r_small[tot]], [r_small[tot]])
                    STT("dve", small[:, 42:43], small[:, 41:42], -1.0, lamt[:, l:l + 1], ALU.mult, ALU.mult,
                        [r_small[41], r_lamt], [r_small[42]])
                    ACT(of32[fb][:], Ot[0][:, 0:128], AF.Copy, [r_O[0], r_small[40]], [r_of32[fb]],
                        scale=small[:, 40:41])
                    STT("dve", of32[fb][:], Ot[1][:, 0:128], small[:, 42:43], of32[fb][:], ALU.mult, ALU.add,
                        [r_O[1], r_small[42], r_of32[fb]], [r_of32[fb]])
                    S.op("act", lambda e: e.activation(out=sqj[:], in_=of32[fb][:],
                                                       func=AF.Square, accum_out=small[:, 43:44]),
                         [r_of32[fb]], [r_tmp64, r_small[43]])
                    ACT(small[:, 43:44], small[:, 43:44], AF.Ln, [r_small[43]], [r_small[43]],
                        scale=1.0 / 128.0, bias=EPS)
                    ACT(small[:, 43:44], small[:, 43:44], AF.Exp, [r_small[43]], [r_small[43]], scale=-0.5)
                    STT("dve", osb[fb][:], of32[fb][:], small[:, 43:44], subg[:, l, :], ALU.mult, ALU.mult,
                        [r_of32[fb], r_small[43], r_lamt], [r_osb[fb]])
                tdst = Ot[oi][:, 256:320].bitcast(BF16)
                TR(tdst, osb[fb][:], [r_osb[fb], r_const], [r_O[oi]], inc=True)
                COPY("dve", oT[:, hp, qi * 128:(qi + 1) * 128], tdst, [r_O[oi]], [r_oT[hp][qi]])

            it_n0 = {}
            for it in items:
                if it["si"] == 0:
                    it_n0[it["qi"]] = it["n"]
            fns = dict(A=stage_A, B1=stage_B1, B2=stage_B2, T=stage_T, CP=stage_CP, PV=stage_PV)
            return [(it, fns) for it in items]

        swa_ctr = {"i": 0}

        def attention_swa(l, hp, q0, qT, r_qT, kT, r_kT, vv, r_vv):
            g = hp // 2
            items = []
            for pair in range(4):
                combos = []
                for q in range(2):
                    qi = pair * 2 + q
                    qb = q0 // 128 + qi
                    w = 256 if qb < 15 else 128
                    for sub in range(2):
                        combos.append(dict(qi=qi, qb=qb, sub=sub, w=w, c=q * 2 + sub))
                items.append(dict(pair=pair, combos=combos, idx=swa_ctr["i"]))
                swa_ctr["i"] += 1
            octx = {}

            def sA(it, ib):
                for cb in it["combos"]:
                    base = cb["sub"] * 64
                    c, w, qi, k0 = cb["c"], cb["w"], cb["qi"], cb["qb"] * 128
                    MM(Zt[ib][:, c * 256:c * 256 + w], qT[base:base + 64, qi * 128:(qi + 1) * 128],
                       kT[base:base + 64, k0:k0 + w], True, True,
                       [r_qT[qi // 4]] + r_kT[k0 // 512:(k0 + w - 1) // 512 + 1], [r_Z[ib][c // 2]], inc=True)

            def sB1(it, ib):
                for cb in it["combos"]:
                    c, w = cb["c"], cb["w"]
                    head = 4 + hp * 2 + cb["sub"]
                    STT("dve", el[ib][:, c * 256:c * 256 + w], Zt[ib][:, c * 256:c * 256 + w], SCALE,
                        btl[:, head, 0:w], ALU.mult, ALU.add, [r_Z[ib][c // 2], r_btl], [r_el[ib]])

            def sB2(it, ib):
                sc = 8 + (it["idx"] % 4) * 8
                for cb in it["combos"]:
                    c, w = cb["c"], cb["w"]
                    S.op("act", lambda e, c=c, w=w: e.activation(out=wb[ib][:, c * 256:c * 256 + w],
                                                                 in_=el[ib][:, c * 256:c * 256 + w], func=AF.Exp,
                                                                 accum_out=small[:, sc + c:sc + c + 1]),
                         [r_el[ib]], [r_wb[ib], r_small[sc + c]])
                    if w < 256:
                        S.op("pool", lambda e, c=c, w=w: e.memset(wb[ib][:, c * 256 + w:(c + 1) * 256], 0.0),
                             [], [r_wb[ib]])

            def sT(it, ib):
                for jb in range(8):
                    TR(Tt[ib][:, jb * 128:(jb + 1) * 128], wb[ib][:, jb * 128:(jb + 1) * 128],
                       [r_wb[ib], r_const], [r_T[ib]], inc=(jb == 7))

            def sCP(it, ib):
                ACT(wT[ib][:, :], Tt[ib][:, :], AF.Copy, [r_T[ib]], [r_wT[ib]])

            def sPV(it, ib):
                oi = state["o"]
                state["o"] = (oi + 1) % 2
                sc = 8 + (it["idx"] % 4) * 8
                fb = it["idx"] % 2
                for cb in it["combos"]:
                    c, w, qb = cb["c"], cb["w"], cb["qb"]
                    nb = w // 128
                    for jb in range(nb):
                        kb = qb + jb
                        MM(Ot[oi][:, c * 64:(c + 1) * 64], wT[ib][:, c * 256 + jb * 128:c * 256 + (jb + 1) * 128],
                           vv[:, kb, g * 64:(g + 1) * 64], jb == 0, jb == nb - 1,
                           [r_wT[ib], r_vv[kb // 4]], [r_O[oi]], inc=(jb == nb - 1))
                es = esink[:, l * 8 + hp * 2:l * 8 + hp * 2 + 2]
                for q in range(2):
                    TT("dve", small[:, sc + 4 + q * 2:sc + 6 + q * 2], small[:, sc + q * 2:sc + q * 2 + 2], es, ALU.add,
                       [r_small[sc + q * 2], r_small[sc + q * 2 + 1], r_lamt],
                       [r_small[sc + 4 + q * 2], r_small[sc + 5 + q * 2]])
                S.op("dve", lambda e: e.reciprocal(out=small[:, sc + 4:sc + 8], in_=small[:, sc + 4:sc + 8]),
                     [r_small[sc + 4 + j] for j in range(4)], [r_small[sc + 4 + j] for j in range(4)])
                for c in range(4):
                    ACT(osbw[fb][:, c * 64:(c + 1) * 64], Ot[oi][:, c * 64:(c + 1) * 64], AF.Copy,
                        [r_O[oi], r_small[sc + 4 + c]], [r_osbw[fb]], scale=small[:, sc + 4 + c:sc + 5 + c])
                qi0 = it["pair"] * 2
                for q in range(2):
                    tdst = Ot[oi][:, 256 + q * 64:320 + q * 64].bitcast(BF16)
                    TR(tdst, osbw[fb][:, q * 128:(q + 1) * 128], [r_osbw[fb], r_const], [r_O[oi]], inc=True)
                    COPY("dve", oT[:, hp, (qi0 + q) * 128:(qi0 + q + 1) * 128], tdst, [r_O[oi]], [r_oT[hp][qi0 + q]])

            fns = dict(A=sA, B1=sB1, B2=sB2, T=sT, CP=sCP, PV=sPV)
            return [(it, fns) for it in items]

        def epilogue(bname, l, first, goff):
            wp_ap = w_proj_d[bname][l]
            for ch in range(2):
                buf = ch
                wp = wview(buf * 3 + 0, 4, 512)
                wgt = wview(buf * 3 + 1, 8, 512)
                rp = [r_W[buf * 3 + 0]]
                rgt = [r_W[buf * 3 + 1], r_W[buf * 3 + 2]]
                for k in range(4):
                    load_w(wp[:, k, :], wp_ap[k * 128:(k + 1) * 128, ch * 512:(ch + 1) * 512], 512, rp)
                for dc in range(8):
                    load_w(wgt[:, dc, :], w_in_d[l][dc * 128:(dc + 1) * 128, goff + ch * 512: goff + (ch + 1) * 512],
                           512, rgt)
                for cc in range(4):
                    c = ch * 4 + cc
                    for ti in range(2):
                        t0 = state["tok0"] + ti * 512
                        tt = t0 // 512
                        P, r_P = next_d()
                        G, r_G = next_d()
                        for k in range(4):
                            MM(P, wp[:, k, cc * 128:(cc + 1) * 128], oT[:, k, ti * 512:(ti + 1) * 512], k == 0, k == 3,
                               rp + r_oT[k][ti * 4:(ti + 1) * 4], [r_P], inc=(k == 3))
                        for dc in range(8):
                            MM(G, wgt[:, dc, cc * 128:(cc + 1) * 128], xn[:, dc, t0:t0 + 512], dc == 0, dc == 7,
                               rgt + [r_xn[dc][tt]], [r_G], inc=(dc == 7))
                        i = state["sg"]
                        state["sg"] = (i + 1) % 2
                        ACT(sg[i][:], G, AF.Sigmoid, [r_G], [r_sg[i]])
                        msl = mrg[:, (c * 2 + ti) * 512:(c * 2 + ti + 1) * 512]
                        rm = r_mrg[c * 2 + ti]
                        if first:
                            TT("dve", msl, sg[i][:], P, ALU.mult, [r_sg[i], r_P], [rm])
                        else:
                            TT("dve", sg[i][:], sg[i][:], P, ALU.mult, [r_sg[i], r_P], [r_sg[i]])
                            TT("dve", msl, msl, sg[i][:], ALU.add, [r_sg[i], rm], [rm])

        def mixer_out(l):
            for ch in range(2):
                wo = wview(ch * 3, 8, 512)
                ro = [r_W[ch * 3], r_W[ch * 3 + 1]]
                for c in range(8):
                    load_w(wo[:, c, :], w_out_d[l][c * 128:(c + 1) * 128, ch * 512:(ch + 1) * 512], 512, ro)
                for cc in range(4):
                    c2 = ch * 4 + cc
                    for ti in range(2):
                        t0 = state["tok0"] + ti * 512
                        tt = t0 // 512
                        ob, r_ob = next_o()
                        for c in range(8):
                            MM(ob[:], wo[:, c, cc * 128:(cc + 1) * 128], mrg[:, (c * 2 + ti) * 512:(c * 2 + ti + 1) * 512],
                               c == 0, c == 7, ro + [r_mrg[c * 2 + ti]], [r_ob], inc=(c == 7))
                        TT("dve", xres[:, c2, t0:t0 + 512], ob[:], xres[:, c2, t0:t0 + 512], ALU.add,
                           [r_ob, r_xres[c2][tt]], [r_xres[c2][tt]])

        def mixer(l):
            rmsnorm_to_xn(PC_NORM + (l * 3 + 1) * 8)
            for half in range(2):
                tok0 = half * 1024
                state["tok0"] = tok0
                nkt = (SEQ - tok0) // 512
                first = True
                for bname in branches:
                    if bname == "sb":
                        qoff, koff, voff, goff = OFF_QA, OFF_KA, OFF_VA, OFF_GA
                    elif bname == "diff":
                        qoff, koff, voff, goff = OFF_QD, OFF_KD, OFF_VD, OFF_GD
                    else:
                        qoff, koff, voff, goff = OFF_QS, OFF_KS, OFF_VS, OFF_GS
                    wq = wview(0, 8, 512)
                    wk = wview(2, 8, 512)
                    wv_ = wview(4, 8, 512)
                    rq, rk, rv = [r_W[0], r_W[1]], [r_W[2], r_W[3]], [r_W[4], r_W[5]]
                    for dc in range(8):
                        rows = slice(dc * 128, (dc + 1) * 128)
                        load_w(wq[:, dc, :], w_in_d[l][rows, qoff:qoff + 512], 512, rq)
                        if bname != "swa":
                            load_w(wk[:, dc, :], w_in_d[l][rows, koff:koff + 512], 512, rk)
                            load_w(wv_[:, dc, :], w_in_d[l][rows, voff:voff + 512], 512, rv)
                        else:
                            i = state["stg"]
                            state["stg"] = (i + 1) % NSTG
                            S.dma("sp", stg[i][:, :256], w_in_d[l][rows, koff:koff + 256], writes=[r_stg[i]],
                                  dst=r_stg[i])
                            for g in range(2):
                                for dup in range(2):
                                    TS("pool", wk[:, dc, g * 128 + dup * 64: g * 128 + dup * 64 + 64],
                                       stg[i][:, g * 64:(g + 1) * 64], 1.0, 0.0, ALU.mult, ALU.add, [r_stg[i]], rk)
                            TS("pool", wv_[:, dc, 0:128], stg[i][:, 128:256], 1.0, 0.0, ALU.mult, ALU.add,
                               [r_stg[i]], rv)
                    def proj_units(hp):
                        b = hp % 2
                        qd_, rqd = qTb[b], r_qTb[b]
                        u = []
                        if bname == "sb" or bname == "diff":
                            kd_, rkd, vd_, rvd = kTb[b], r_kTb[b], vvb[b], r_vvb[b]
                            mode = "copy" if bname == "sb" else "qknorm"
                            gq = None if bname == "sb" else PC_QK + l * 4 + 0
                            gk = None if bname == "sb" else PC_QK + l * 4 + 1
                            u += proj_fm_units(wq, rq, hp * 128, qd_, lambda ti: rqd[ti], tok0, 2, mode,
                                               gain_col=gq, alt=0)
                            u += proj_fm_units(wk, rk, hp * 128, kd_[:, tok0:], lambda ti: rkd[(tok0 // 512) + ti],
                                               tok0, nkt, mode, gain_col=gk, alt=1)
                            u += proj_v_units(wv_, rv, hp * 128, 128, tok0 // 128, 16, vd_, rvd)
                        else:
                            u += proj_fm_units(wq, rq, hp * 128, qd_, lambda ti: rqd[ti], tok0, 2, "qknorm",
                                               gain_col=PC_QK + l * 4 + 2)
                            if hp % 2 == 0:
                                g = hp // 2
                                kd_, rkd = kTb[g % 2], r_kTb[g % 2]
                                u += proj_fm_units(wk, rk, g * 128, kd_[:, tok0:],
                                                   lambda ti: rkd[(tok0 // 512) + ti], tok0, nkt, "qknorm",
                                                   gain_col=PC_QK + l * 4 + 3)
                            if hp == 0:
                                u += proj_v_units(wv_, rv, 0, 128, tok0 // 128, 16, vvb[0], r_vvb[0])
                        return u

                    for hp2 in (0, 2):
                        for hp in (hp2, hp2 + 1):
                            for u in proj_units(hp):
                                u()
                        for hp in (hp2, hp2 + 1):
                            b = hp % 2
                            if bname == "swa":
                                kb_ = (hp // 2) % 2
                                swa_fn = attention_swa if SWA_BATCHED else (
                                    lambda l_, hp_, *a: attention("swa", l_, hp_, *a))
                                its = swa_fn(l, hp, tok0, qTb[b], r_qTb[b], kTb[kb_], r_kTb[kb_],
                                             vvb[0], r_vvb[0])
                            else:
                                its = attention(bname, l, hp, tok0, qTb[b], r_qTb[b], kTb[b], r_kTb[b],
                                                vvb[b], r_vvb[b])
                            for (it, fns) in its:
                                pipe_push(it, fns)
                        pipe_flush()
                    if debug == "oT" and half == 1:
                        for c4 in range(4):
                            COPY("dve", dbgf[:], oT[:, c4, :], r_oT[c4], [r_dbg])
                            dbg_ids.append(S.dma("sp", dbg_d[:, c4 * 1024:(c4 + 1) * 1024], dbgf[:], reads=[r_dbg],
                                                 dst=r_dbg))
                    epilogue(bname, l, first, goff)
                    first = False
                mixer_out(l)

        out_ids = []
        r_out = [Res("yout%d" % i) for i in range(8)]
        for s in range(n_seq):
            for dc in range(8):
                S.dma("sp", xres[:, dc, :], x_d[s, dc], writes=r_xres[dc], dst=r_xres[dc][0])
                for t in range(1, 4):
                    r_xres[dc][t].w = r_xres[dc][0].w
            for l in layers:
                if "ffn1" in phases:
                    ffn(l, 0)
                if "mix" in phases:
                    mixer(l)
                if "ffn2" in phases:
                    ffn(l, 1)
            for dc in range(8):
                out_ids.append(S.dma("sp", y_d[s, dc], xres[:, dc, :], reads=r_xres[dc], dst=r_out[dc]))
        S.wait_all("sp", out_ids[-8:] + dbg_ids)
        S.emit_all()
        stats = dict(n_instr=S.n_instr, n_wait=S.n_wait, nsem=S.nsem)
    return nc, stats


def _t5_bucket_np(n):
    n = np.maximum(n, 0)
    max_exact = 16
    nf = np.maximum(n, 1).astype(np.float32)
    large = max_exact + (np.log(nf / np.float32(max_exact)) / np.float32(math.log(128 / max_exact))
                         * np.float32(32 - max_exact)).astype(np.int32)
    large = np.minimum(large, 31)
    return np.where(n < max_exact, n, large)


def _host_layout(inputs):
    f32 = np.float32
    prm = np.zeros((128, NP), f32)
    p = np.arange(128)
    norms = [inputs["ffn1_norm"], inputs["mix_norm"], inputs["ffn2_norm"]]
    for l in range(DEPTH):
        for which in range(3):
            g = np.asarray(norms[which][l], f32).reshape(8, 128)
            prm[:, PC_NORM + (l * 3 + which) * 8: PC_NORM + (l * 3 + which) * 8 + 8] = g.T
        for k, name in enumerate(["q_norm_diff", "k_norm_diff", "q_norm_swa", "k_norm_swa"]):
            prm[:, PC_QK + l * 4 + k] = np.asarray(inputs[name][l], f32)[p % 64]
        prm[:, PC_SINK + l * 8: PC_SINK + l * 8 + 8] = np.asarray(inputs["swa_sinks"][l], f32)[None, :]
        prm[:, PC_LAM + l * 256: PC_LAM + (l + 1) * 256] = np.asarray(inputs["diff_lambda"][l], f32).reshape(1, 256)
        prm[:, PC_SUBLN + l * 128: PC_SUBLN + (l + 1) * 128] = np.asarray(inputs["diff_subln"][l], f32)[None, :]
    rb = np.asarray(inputs["rel_bias"], f32)
    prm[:, PC_CFAR: PC_CFAR + 4] = rb[31, 0:4][None, :]
    i = np.arange(128)[:, None]
    j = np.arange(128)[None, :]
    d0 = j - i
    d1 = 128 + j - i
    b0 = _t5_bucket_np(d0)
    b1 = _t5_bucket_np(d1)
    bt = np.zeros((128, 12, 256), f32)
    for h in range(12):
        t0 = rb[b0, h]
        t0 = np.where(d0 >= 0, t0, f32(MASKV))
        t1 = rb[b1, h]
        if h >= 4:
            t1 = np.where(d1 < 128, t1, f32(MASKV))
        bt[:, h, 0:128] = t0
        bt[:, h, 128:256] = t1
    return prm, bt


_CACHE = {}


def kernel(**inputs):
    x = np.asarray(inputs["x"], np.float32)
    B = x.shape[0]
    prm, bt = _host_layout(inputs)
    if "nc" not in _CACHE:
        _CACHE["nc"] = build_program()[0]
    nc = _CACHE["nc"]
    shared = {k: np.ascontiguousarray(np.asarray(inputs[k], np.float32)) for k in
              ["ffn1_w_in", "ffn1_w_out", "ffn2_w_in", "ffn2_w_out", "w_in", "w_proj_sb", "w_proj_diff",
               "w_proj_swa", "w_out"]}
    shared["params"] = prm
    shared["btiles"] = bt
    in_maps = []
    for c in range(N_CORES):
        xs = x[c * SEQ_PER_CORE:(c + 1) * SEQ_PER_CORE]
        xs = xs[:, ::-1, :]
        xt = np.ascontiguousarray(xs.transpose(0, 2, 1)).reshape(SEQ_PER_CORE, 8, 128, SEQ)
        m = dict(shared)
        m["x"] = xt
        in_maps.append(m)
    res = run_bass_kernel_spmd(nc, in_maps, core_ids=list(range(N_CORES)))
    out = np.empty((B, SEQ, D_MODEL), np.float32)
    for c in range(N_CORES):
        y = np.asarray(res.results[c]["y"]).reshape(SEQ_PER_CORE, D_MODEL, SEQ)
        out[c * SEQ_PER_CORE:(c + 1) * SEQ_PER_CORE] = y.transpose(0, 2, 1)[:, ::-1, :]
    return out
```
